# Optimizing a Trainium2 kernel written in Bass

```python
import math
import jax, jax.numpy as jnp
from jax import lax
import numpy as np

D_MODEL = 1024
BATCH = 8
SEQ = 2048
DEPTH = 4
DEC_BATCH = 128
DEC_SEQ = 4
PAST_LEN = 16384
PAGE_SIZE = 128

N_MIXERS = 4
D_PLE = 256
D_FF = 2816
EPS = 1e-6

GM_CHUNK = 128
GM_HALF = D_MODEL
GM_GROUPS = 8
GM_GROUP_DIM = GM_HALF // GM_GROUPS

POOL_WINDOWS = (2, 4, 8, 16)
POOL_GROUPS = len(POOL_WINDOWS)
POOL_GROUP_DIM = D_MODEL // POOL_GROUPS
POOL_HIST = max(POOL_WINDOWS) - 1

GLA_HEADS = 4
GLA_QK = D_MODEL // 2
GLA_V = D_MODEL
GLA_DK = GLA_QK // GLA_HEADS
GLA_DV = GLA_V // GLA_HEADS
GLA_RANK = 16
GLA_NORMALIZER = 16.0
GLA_CHUNK = 16

SSM_D_INNER = 2 * D_MODEL
SSM_HEAD_DIM = 64
SSM_HEADS = SSM_D_INNER // SSM_HEAD_DIM
SSM_GROUPS = 4
SSM_HPG = SSM_HEADS // SSM_GROUPS
SSM_STATE = 128
SSM_CONV = 4
SSM_CHUNK = 64
SSM_CONV_DIM = SSM_D_INNER + 2 * SSM_GROUPS * SSM_STATE

kernel_name = 'hybrid_macaron_chunkmlp_pool_gla_ssd_step'


def rms_norm(x, g):
    xf = x.astype(jnp.float32)
    y = xf * lax.rsqrt(jnp.mean(xf * xf, axis=-1, keepdims=True) + EPS)
    return (y * g.astype(jnp.float32)).astype(x.dtype)


def layer_norm(x, g):
    xf = x.astype(jnp.float32)
    xc = xf - jnp.mean(xf, axis=-1, keepdims=True)
    y = xc * lax.rsqrt(jnp.mean(xc * xc, axis=-1, keepdims=True) + EPS)
    return (y * g.astype(jnp.float32)).astype(x.dtype)


def swiglu(h, w_gate, w_up, w_down):
    return (jax.nn.silu(h @ w_gate) * (h @ w_up)) @ w_down


def pad_seq(t, multiple):
    pad = (-t.shape[1]) % multiple
    return jnp.pad(t, [(0, 0), (0, pad)] + [(0, 0)] * (t.ndim - 2))


def chunk_mlp_mixer(h, w_in, ln_g, w_s, b_s, w_out):
    B, L, _ = h.shape
    u, v = jnp.split(jax.nn.gelu(h @ w_in), 2, axis=-1)
    v = layer_norm(v, ln_g)
    vp = pad_seq(v, GM_CHUNK)
    nc = vp.shape[1] // GM_CHUNK
    vp = vp.reshape(B, nc, GM_CHUNK, GM_GROUPS, GM_GROUP_DIM)
    causal = jnp.tril(jnp.ones((GM_CHUNK, GM_CHUNK), dtype=bool))
    w_masked = jnp.where(causal[None], w_s, jnp.zeros_like(w_s))
    sv = jnp.einsum('gts,bnsgc->bntgc', w_masked, vp) + jnp.transpose(b_s)[None, None, :, :, None]
    sv = sv.reshape(B, nc * GM_CHUNK, GM_HALF)[:, :L]
    return (u * sv) @ w_out, v


def pool_mixer(h, hist, pos0, w_grp, scale):
    B, L, D = h.shape
    ext = jnp.concatenate([hist.astype(h.dtype), h], axis=1)
    cs = jnp.cumsum(ext.astype(jnp.float32), axis=1)
    cs = jnp.pad(cs, ((0, 0), (1, 0), (0, 0)))
    pos = pos0 + jnp.arange(L, dtype=jnp.int32)
    pooled = []
    for gi, w in enumerate(POOL_WINDOWS):
        lo, hi = gi * POOL_GROUP_DIM, (gi + 1) * POOL_GROUP_DIM
        win_sum = (cs[:, POOL_HIST + 1:POOL_HIST + 1 + L, lo:hi]
                   - cs[:, POOL_HIST + 1 - w:POOL_HIST + 1 - w + L, lo:hi])
        count = jnp.minimum(pos + 1, w).astype(jnp.float32)[None, :, None]
        pooled.append(win_sum / count)
    pooled = jnp.concatenate(pooled, axis=-1)
    diff = (pooled - h.astype(jnp.float32)).astype(h.dtype).reshape(B, L, POOL_GROUPS, POOL_GROUP_DIM)
    y = jnp.einsum('blgc,gcd->blgd', diff, w_grp).reshape(B, L, D) * scale
    return y, ext[:, -POOL_HIST:]


def gla_recurrence(q, k, v, log_a, s0):
    B, L, H, _ = q.shape
    DV = v.shape[-1]

    def blocks(t):
        t = pad_seq(t, GLA_CHUNK)
        nc = t.shape[1] // GLA_CHUNK
        return t.reshape(B, nc, GLA_CHUNK, H, t.shape[-1]).transpose(1, 0, 3, 2, 4)

    qc, kc, vc, gc = blocks(q), blocks(k), blocks(v), blocks(log_a)
    b = jnp.cumsum(gc, axis=3)
    b_last = b[:, :, :, -1:, :]
    q_dec = qc * jnp.exp(b)
    k_inv = kc * jnp.exp(-b)
    k_end = kc * jnp.exp(b_last - b)
    causal = jnp.tril(jnp.ones((GLA_CHUNK, GLA_CHUNK), dtype=bool))
    scores = jnp.where(causal, jnp.einsum('nbhtd,nbhsd->nbhts', q_dec, k_inv), 0.0)
    o_intra = jnp.einsum('nbhts,nbhsv->nbhtv', scores, vc)

    def step(s, xs):
        q_dec_c, k_end_c, v_c, decay_c = xs
        o_inter = jnp.einsum('bhtd,bhdv->bhtv', q_dec_c, s)
        s = s * decay_c[:, :, 0, :, None] + jnp.einsum('bhsd,bhsv->bhdv', k_end_c, v_c)
        return s, o_inter

    s_final, o_inter = lax.scan(step, s0, (q_dec, k_end, vc, jnp.exp(b_last)))
    o = (o_intra + o_inter).transpose(1, 0, 3, 2, 4).reshape(B, -1, H, DV)[:, :L]
    return o, s_final


def gla_mixer(h, s0, w_in, w_a1, w_a2, b_a, g_norm, w_out):
    B, L, _ = h.shape
    q, k, v, r = jnp.split(h @ w_in, [GLA_QK, 2 * GLA_QK, 2 * GLA_QK + GLA_V], axis=-1)
    log_a = jax.nn.log_sigmoid(((h @ w_a1) @ w_a2 + b_a).astype(jnp.float32)) / GLA_NORMALIZER
    qh = q.astype(jnp.float32).reshape(B, L, GLA_HEADS, GLA_DK) * (GLA_DK ** -0.5)
    kh = k.astype(jnp.float32).reshape(B, L, GLA_HEADS, GLA_DK)
    vh = v.astype(jnp.float32).reshape(B, L, GLA_HEADS, GLA_DV)
    o, s_new = gla_recurrence(qh, kh, vh, log_a.reshape(B, L, GLA_HEADS, GLA_DK), s0.astype(jnp.float32))
    o = rms_norm(o, g_norm.reshape(GLA_HEADS, GLA_DV)).astype(h.dtype).reshape(B, L, GLA_V)
    return (o * jax.nn.silu(r)) @ w_out, s_new.astype(s0.dtype)


def ssd_recurrence(x, dt, a, bm, cm, s0):
    B, L = x.shape[:2]

    def blocks(t):
        t = pad_seq(t, SSM_CHUNK)
        return t.reshape((B, t.shape[1] // SSM_CHUNK, SSM_CHUNK) + t.shape[2:])

    xc = blocks(x).reshape(B, -1, SSM_CHUNK, SSM_GROUPS, SSM_HPG, SSM_HEAD_DIM)
    dtc = blocks(dt).reshape(B, -1, SSM_CHUNK, SSM_GROUPS, SSM_HPG)
    bc, cc = blocks(bm), blocks(cm)
    cum = jnp.cumsum(dtc * a.reshape(SSM_GROUPS, SSM_HPG), axis=2)
    causal = jnp.tril(jnp.ones((SSM_CHUNK, SSM_CHUNK), dtype=bool))[:, :, None, None]
    seg = cum[:, :, :, None] - cum[:, :, None, :]
    decay = jnp.exp(jnp.where(causal, seg, -jnp.inf))
    cb = jnp.einsum('bctgn,bcsgn->bctsg', cc, bc)
    mix = cb[..., None] * decay * dtc[:, :, None]
    y_intra = jnp.einsum('bctsgr,bcsgrp->bctgrp', mix, xc)

    def step(s, xs):
        c_c, b_c, x_c, dt_c, cum_c = xs
        y_inter = jnp.einsum('btgn,bgrpn->btgrp', c_c, s) * jnp.exp(cum_c)[..., None]
        w_end = jnp.exp(cum_c[:, -1:] - cum_c) * dt_c
        s = s * jnp.exp(cum_c[:, -1])[..., None, None] + jnp.einsum('bsgn,bsgr,bsgrp->bgrpn', b_c, w_end, x_c)
        return s, y_inter

    to_scan = lambda t: jnp.moveaxis(t, 1, 0)
    s_init = s0.reshape(B, SSM_GROUPS, SSM_HPG, SSM_HEAD_DIM, SSM_STATE)
    s_final, y_inter = lax.scan(step, s_init, (to_scan(cc), to_scan(bc), to_scan(xc), to_scan(dtc), to_scan(cum)))
    y = (y_intra + jnp.moveaxis(y_inter, 0, 1)).reshape(B, -1, SSM_HEADS, SSM_HEAD_DIM)[:, :L]
    return y, s_final.reshape(B, SSM_HEADS, SSM_HEAD_DIM, SSM_STATE)


def ssd_mixer(h, conv_s0, ssm_s0, w_in, conv_w, conv_b, dt_bias, a_log, d_skip, g_norm, w_out):
    B, L, _ = h.shape
    z, xbc, dt = jnp.split(h @ w_in, [SSM_D_INNER, SSM_D_INNER + SSM_CONV_DIM], axis=-1)
    ext = jnp.concatenate([conv_s0.astype(h.dtype), xbc], axis=1)
    conv = conv_b + sum(ext[:, j:j + L] * conv_w[j] for j in range(SSM_CONV))
    xbc = jax.nn.silu(conv)
    xs, bm, cm = jnp.split(xbc, [SSM_D_INNER, SSM_D_INNER + SSM_GROUPS * SSM_STATE], axis=-1)
    dt = jax.nn.softplus((dt + dt_bias).astype(jnp.float32))
    a = -jnp.exp(a_log.astype(jnp.float32))
    xh = xs.astype(jnp.float32).reshape(B, L, SSM_HEADS, SSM_HEAD_DIM)
    y, s_new = ssd_recurrence(xh, dt, a,
                              bm.astype(jnp.float32).reshape(B, L, SSM_GROUPS, SSM_STATE),
                              cm.astype(jnp.float32).reshape(B, L, SSM_GROUPS, SSM_STATE),
                              ssm_s0.astype(jnp.float32))
    y = y + xh * d_skip.astype(jnp.float32)[:, None]
    y = y.reshape(B, L, SSM_D_INNER).astype(h.dtype) * jax.nn.silu(z)
    y = rms_norm(y.reshape(B, L, SSM_GROUPS, SSM_D_INNER // SSM_GROUPS),
                 g_norm.reshape(SSM_GROUPS, SSM_D_INNER // SSM_GROUPS)).reshape(B, L, SSM_D_INNER)
    return y @ w_out, ext[:, -(SSM_CONV - 1):], s_new.astype(ssm_s0.dtype)


def setup_inputs(seed: int = 0) -> dict:
    key = jax.random.key(seed)
    keys = iter(jax.random.split(key, 64))
    nk = lambda: next(keys)
    normal = lambda shape, s=1.0: jax.random.normal(nk(), shape, jnp.float32) * s
    dense = lambda shape, fan_in: normal(shape, fan_in ** -0.5)
    gain = lambda shape: 1.0 + normal(shape, 0.02)

    dt0 = jnp.exp(jax.random.uniform(nk(), (SSM_HEADS,), jnp.float32, math.log(1e-3), math.log(1e-1)))
    return {
        'x_prompt': normal((BATCH, SEQ, D_MODEL)),
        'x_sample': normal((DEC_BATCH, DEC_SEQ, D_MODEL)),
        'state_pool_l1': normal((DEC_BATCH, POOL_HIST, D_MODEL)),
        'state_gla_l2': normal((DEC_BATCH, GLA_HEADS, GLA_DK, GLA_DV), 0.5),
        'state_ssm_l3': normal((DEC_BATCH, SSM_HEADS, SSM_HEAD_DIM, SSM_STATE), 0.1),
        'state_conv_l3': normal((DEC_BATCH, SSM_CONV - 1, SSM_CONV_DIM)),
        'p_prompt': normal((DEPTH, BATCH, SEQ, D_PLE)),
        'p_sample': normal((DEPTH, DEC_BATCH, DEC_SEQ, D_PLE)),
        'norm_ffn1': gain((DEPTH, D_MODEL)),
        'ffn1_gate': dense((DEPTH, D_MODEL, D_FF), D_MODEL),
        'ffn1_up': dense((DEPTH, D_MODEL, D_FF), D_MODEL),
        'ffn1_down': dense((DEPTH, D_FF, D_MODEL), D_FF),
        'norm_mix': gain((DEPTH, D_MODEL)),
        'norm_ffn2': gain((DEPTH, D_MODEL)),
        'ffn2_gate': dense((DEPTH, D_MODEL, D_FF), D_MODEL),
        'ffn2_up': dense((DEPTH, D_MODEL, D_FF), D_MODEL),
        'ffn2_down': dense((DEPTH, D_FF, D_MODEL), D_FF),
        'norm_ple': gain((DEPTH, D_MODEL)),
        'ple_gate': dense((DEPTH, D_MODEL, D_MODEL), D_MODEL),
        'ple_proj': dense((DEPTH, D_PLE, D_MODEL), D_PLE),
        'norm_final': gain((D_MODEL,)),
        'gm_w_in': dense((D_MODEL, 2 * GM_HALF), D_MODEL),
        'gm_ln': gain((GM_HALF,)),
        'gm_w_s': dense((GM_GROUPS, GM_CHUNK, GM_CHUNK), GM_CHUNK),
        'gm_b_s': 1.0 + normal((GM_GROUPS, GM_CHUNK), 0.1),
        'gm_w_out': dense((GM_HALF, D_MODEL), GM_HALF),
        'pool_w': dense((POOL_GROUPS, POOL_GROUP_DIM, POOL_GROUP_DIM), POOL_GROUP_DIM),
        'pool_scale': gain((D_MODEL,)),
        'gla_w_in': dense((D_MODEL, 2 * GLA_QK + 2 * GLA_V), D_MODEL),
        'gla_w_a1': dense((D_MODEL, GLA_RANK), D_MODEL),
        'gla_w_a2': dense((GLA_RANK, GLA_QK), GLA_RANK),
        'gla_b_a': normal((GLA_QK,), 0.5),
        'gla_norm': gain((GLA_V,)),
        'gla_w_out': dense((GLA_V, D_MODEL), GLA_V),
        'ssm_w_in': dense((D_MODEL, 2 * SSM_D_INNER + 2 * SSM_GROUPS * SSM_STATE + SSM_HEADS), D_MODEL),
        'ssm_conv_w': dense((SSM_CONV, SSM_CONV_DIM), SSM_CONV),
        'ssm_conv_b': normal((SSM_CONV_DIM,), 0.02),
        'ssm_dt_bias': dt0 + jnp.log(-jnp.expm1(-dt0)),
        'ssm_a_log': jnp.log(jax.random.uniform(nk(), (SSM_HEADS,), jnp.float32, 1.0, 16.0)),
        'ssm_d': gain((SSM_HEADS,)),
        'ssm_norm': gain((SSM_D_INNER,)),
        'ssm_w_out': dense((SSM_D_INNER, D_MODEL), SSM_D_INNER),
    }


def reference(x_prompt, x_sample, state_pool_l1, state_gla_l2, state_ssm_l3, state_conv_l3,
              p_prompt, p_sample,
              norm_ffn1, ffn1_gate, ffn1_up, ffn1_down, norm_mix,
              norm_ffn2, ffn2_gate, ffn2_up, ffn2_down,
              norm_ple, ple_gate, ple_proj, norm_final,
              gm_w_in, gm_ln, gm_w_s, gm_b_s, gm_w_out,
              pool_w, pool_scale,
              gla_w_in, gla_w_a1, gla_w_a2, gla_b_a, gla_norm, gla_w_out,
              ssm_w_in, ssm_conv_w, ssm_conv_b, ssm_dt_bias, ssm_a_log, ssm_d, ssm_norm, ssm_w_out):

    def trunk(x, p, pos0, pool_hist, gla_s0, ssm_s0, conv_s0):
        for i in range(DEPTH):
            x = x + 0.5 * swiglu(rms_norm(x, norm_ffn1[i]), ffn1_gate[i], ffn1_up[i], ffn1_down[i])
            h = rms_norm(x, norm_mix[i])
            kind = i % N_MIXERS
            if kind == 0:
                mixed, chunk_v = chunk_mlp_mixer(h, gm_w_in, gm_ln, gm_w_s, gm_b_s, gm_w_out)
            elif kind == 1:
                mixed, pool_new = pool_mixer(h, pool_hist, pos0, pool_w, pool_scale)
            elif kind == 2:
                mixed, gla_new = gla_mixer(h, gla_s0, gla_w_in, gla_w_a1, gla_w_a2, gla_b_a, gla_norm, gla_w_out)
            else:
                mixed, conv_new, ssm_new = ssd_mixer(h, conv_s0, ssm_s0, ssm_w_in, ssm_conv_w, ssm_conv_b,
                                                     ssm_dt_bias, ssm_a_log, ssm_d, ssm_norm, ssm_w_out)
            x = x + mixed
            x = x + 0.5 * swiglu(rms_norm(x, norm_ffn2[i]), ffn2_gate[i], ffn2_up[i], ffn2_down[i])
            x = x + jax.nn.sigmoid(rms_norm(x, norm_ple[i]) @ ple_gate[i]) * (p[i] @ ple_proj[i])
        return rms_norm(x, norm_final), chunk_v, pool_new, gla_new, ssm_new, conv_new

    nb = x_prompt.shape[0]
    y_prompt, _, pool_prompt, gla_prompt, ssm_prompt, conv_prompt = trunk(
        x_prompt, p_prompt, 0,
        jnp.zeros((nb,) + state_pool_l1.shape[1:], state_pool_l1.dtype),
        jnp.zeros((nb,) + state_gla_l2.shape[1:], state_gla_l2.dtype),
        jnp.zeros((nb,) + state_ssm_l3.shape[1:], state_ssm_l3.dtype),
        jnp.zeros((nb,) + state_conv_l3.shape[1:], state_conv_l3.dtype))
    y_sample, chunk_v_sample, pool_sample, gla_sample, ssm_sample, conv_sample = trunk(
        x_sample, p_sample, PAST_LEN, state_pool_l1, state_gla_l2, state_ssm_l3, state_conv_l3)
    return (y_prompt, y_sample, chunk_v_sample, pool_prompt, pool_sample, gla_prompt, gla_sample,
            ssm_prompt, ssm_sample, conv_prompt, conv_sample)
```

```python
import numpy as np
import concourse.bass as bass
import concourse.mybir as mybir
from concourse.bass_utils import run_bass_kernel_spmd

F32 = mybir.dt.float32
BF16 = mybir.dt.bfloat16
AF = mybir.ActivationFunctionType
ALU = mybir.AluOpType

D = 1024
DFF = 2816
NCH = 8
EPS = 1e-6
NEG = -30000.0


def _esize(dt):
    return 4 if dt == F32 else 2


class Rec:
    __slots__ = ("plo", "phi", "ivals", "lo", "hi", "who", "key")


class Sched:
    ENG = ["pe", "act", "dve", "pool", "sp"]

    def __init__(self, nc, same_engine_sync=True):
        self.nc = nc
        self.sem = {e: nc.alloc_semaphore("sem_" + e) for e in self.ENG}
        self.count = {e: 0 for e in self.ENG}
        self.ops = {e: [] for e in self.ENG}
        self.seen = {e: {} for e in self.ENG}
        self.tens = {}
        self.chan = {}
        self.chan_by_sem = {}
        self.rec = None
        import os as _os
        self.vclock = _os.environ.get("VCLOCK", "1") == "1"
        self.sem_eng = {self.sem[e].num: e for e in self.ENG}
        self.snaps = {e: {} for e in self.ENG}
        self.same = same_engine_sync
        self.nwaits = 0

    def region(self, ap):
        tn = type(ap.tensor).__name__
        if not (tn.startswith("SB") or tn.startswith("PSum")):
            return None
        pat = ap.ap
        es = _esize(ap.dtype)
        pstep, pcnt = pat[0]
        off = int(ap.offset)
        if pstep > 0:
            p0 = off // pstep
            f0 = off % pstep
        else:
            p0 = 0
            f0 = off
        free = [(s, c) for (s, c) in pat[1:] if c > 1 and s != 0]
        free.sort(key=lambda x: x[0])
        run = 1
        rest = []
        for (s, c) in free:
            if s == run:
                run = run * c
            else:
                rest.append((s, c))
        starts = [0]
        nrest = 1
        for (s, c) in rest:
            nrest *= c
        if nrest <= 64:
            for (s, c) in rest:
                starts = [a + s * i for a in starts for i in range(c)]
            ivals = [((f0 + a) * es, (f0 + a + run) * es) for a in starts]
        else:
            ext = run + sum(s * (c - 1) for (s, c) in rest)
            ivals = [(f0 * es, (f0 + ext) * es)]
        r = Rec()
        r.plo = p0
        r.phi = p0 + pcnt
        if tn.startswith("PSum"):
            r.plo, r.phi = 0, 128
            ivals = [(0, 2048)]
        r.ivals = ivals
        r.lo = min(a for a, b in ivals)
        r.hi = max(b for a, b in ivals)
        r.key = (ap.tensor.name, r.plo, r.phi, tuple(ivals))
        return ap.tensor.name, r

    @staticmethod
    def _overlap(a, b):
        if a.hi <= b.lo or b.hi <= a.lo or a.phi <= b.plo or b.phi <= a.plo:
            return False
        for (x0, x1) in a.ivals:
            for (y0, y1) in b.ivals:
                if x0 < y1 and y0 < x1:
                    return True
        return False

    @staticmethod
    def _covers(w, r):
        if not (w.plo <= r.plo and r.phi <= w.phi):
            return False
        for (y0, y1) in r.ivals:
            ok = False
            for (x0, x1) in w.ivals:
                if x0 <= y0 and y1 <= x1:
                    ok = True
                    break
            if not ok:
                return False
        return True

    def _collect(self, reads, writes):
        whos = []
        rr = []
        ww = []
        for ap in reads:
            x = self.region(ap)
            if x is None:
                continue
            name, r = x
            rr.append((name, r))
            t = self.tens.setdefault(name, {"w": [], "r": []})
            for rec in t["w"]:
                if self._overlap(rec, r):
                    whos.append(rec.who)
            if type(ap.tensor).__name__.startswith("PSum"):
                for rec in t["r"]:
                    whos.append(rec.who)
        for ap in writes:
            x = self.region(ap)
            if x is None:
                continue
            name, r = x
            ww.append((name, r))
            t = self.tens.setdefault(name, {"w": [], "r": []})
            for rec in t["w"]:
                if self._overlap(rec, r):
                    whos.append(rec.who)
            for rec in t["r"]:
                if self._overlap(rec, r):
                    whos.append(rec.who)
        return whos, rr, ww

    def _record(self, rr, ww, who):
        for name, r in ww:
            t = self.tens[name]
            t["w"] = [x for x in t["w"] if not self._covers(r, x)]
            t["r"] = [x for x in t["r"] if not self._covers(r, x)]
            r.who = who
            t["w"].append(r)
        for name, r in rr:
            t = self.tens[name]
            r.who = who
            done = False
            if who[0] == "e":
                for x in t["r"]:
                    if x.key == r.key and x.who[0] == "e" and x.who[1] == who[1]:
                        x.who = who
                        done = True
                        break
            if not done:
                t["r"].append(r)

    def _waits(self, eng, whos):
        seen = self.seen[eng]
        best = {}
        for w in whos:
            if w[0] == "e":
                _, e2, seq = w
                if e2 == eng:
                    if eng == "pe" or not self.same:
                        continue
                    assert seq <= self.count[eng], "same-engine dep on pending op"
                sem = self.sem[e2]
                val = seq
            else:
                _, sem, val = w
                val = max(val, self.chan_by_sem[sem.num][1])
            k = sem.num
            if k not in best or best[k][1] < val:
                best[k] = (sem, val)
        waits = []
        items = sorted(best.items(), key=lambda kv: -kv[1][1])
        for k, (sem, val) in items:
            if seen.get(k, 0) >= val:
                continue
            seen[k] = val
            waits.append((sem, val))
            if self.vclock:
                e2 = self.sem_eng.get(k)
                if e2 is not None:
                    snap = self.snaps[e2].get(val)
                    if snap is not None:
                        for kk, vv in snap.items():
                            if seen.get(kk, 0) < vv:
                                seen[kk] = vv
        self.nwaits += len(waits)
        return waits

    def play(self, items):
        for it in items:
            if it[0] == "group":
                self.play(it[1])
            elif it[0] == "op":
                self.op(*it[1:])
            else:
                self.dma(*it[1:-1], **it[-1])

    def op(self, eng, fn, reads=(), writes=(), inc=True):
        if self.rec is not None:
            self.rec.append(("op", eng, fn, list(reads), list(writes), inc))
            return
        whos, rr, ww = self._collect(reads, writes)
        waits = self._waits(eng, whos)
        seq = self.count[eng] + 1
        if inc:
            self.count[eng] = seq
            if self.vclock:
                self.snaps[eng][seq] = dict(self.seen[eng])
        self.ops[eng].append((waits, fn, (self.sem[eng], 1) if inc else None))
        self._record(rr, ww, ("e", eng, seq))

    def dma(self, eng, out, in_, chan, **kw):
        if self.rec is not None:
            self.rec.append(("dma", eng, out, in_, chan, kw))
            return
        if not chan.startswith(eng + "_"):
            chan = eng + "_" + chan
        whos, rr, ww = self._collect([in_], [out])
        waits = self._waits(eng, whos)
        if chan not in self.chan:
            self.chan[chan] = [self.nc.alloc_semaphore("ch_" + chan), 0]
            self.chan_by_sem[self.chan[chan][0].num] = self.chan[chan]
        c = self.chan[chan]
        c[1] += 16
        sem, val = c[0], c[1]
        self.ops[eng].append((waits, lambda e: e.dma_start(out=out, in_=in_, **kw), (sem, 16)))
        self._record(rr, ww, ("d", sem, val))

    def finish(self):
        nc = self.nc
        whos = [("d", c[0], c[1]) for c in self.chan.values()]
        whos += [("e", e, self.count[e]) for e in self.ENG if e != "sp" and self.count[e] > 0]
        waits = self._waits("sp", whos)
        self.ops["sp"].append((waits, None, None))

        import os as _os
        fuse = _os.environ.get("FUSE_WAIT", "1") == "1"

        def replay(e, lst):
            for waits, fn, inc in lst:
                if fn is None:
                    for (sem, val) in waits:
                        e.wait_ge(sem, val)
                    continue
                ws = list(waits)
                last = ws.pop() if (fuse and ws) else None
                for (sem, val) in ws:
                    e.wait_ge(sem, val)
                ins = fn(e)
                if last is not None:
                    ins._wait_ge(last[0], last[1])
                if inc is not None:
                    ins.then_inc(inc[0], inc[1])

        with nc.Block() as block:
            @block.tensor
            def _(e):
                replay(e, self.ops["pe"])

            @block.scalar
            def _(e):
                replay(e, self.ops["act"])

            @block.vector
            def _(e):
                replay(e, self.ops["dve"])

            @block.gpsimd
            def _(e):
                replay(e, self.ops["pool"])

            @block.sync
            def _(e):
                replay(e, self.ops["sp"])


class Arena:
    def __init__(self, t, nbytes):
        self.t = t
        self.n = nbytes
        self.off = 0

    def reset(self, off=0):
        self.off = off

    def alloc(self, shape, dt):
        es = _esize(dt)
        n = 1
        for s in shape[1:]:
            n *= s
        nb = n * es
        self.off = (self.off + 31) // 32 * 32
        assert self.off + nb <= self.n, f"arena overflow {self.off}+{nb}>{self.n}"
        v = self.t[:, self.off // 2:(self.off + nb) // 2]
        self.off += nb
        if dt == F32:
            v = v.bitcast(F32)
        if len(shape) == 3:
            v = v.rearrange("p (a b) -> p a b", a=shape[1])
        elif len(shape) == 4:
            v = v.rearrange("p (a b c) -> p a b c", a=shape[1], b=shape[2])
        if shape[0] < 128:
            v = v[0:shape[0]]
        return v


class Stage:
    def __init__(self, loads, compute, la=2):
        self.loads = loads
        self.compute = compute
        self.la = la


INPUT_SHAPES = None


def in_specs(NPT):
    TP = NPT * 128
    return {
        "xp": [TP, D], "xs": [64, D],
        "st_pool": [240, D], "st_gla": [16, 4, 128, 256], "st_ssm": [16, 2048, 128], "st_conv": [48, 3072],
        "pp": [4, TP, 256], "ps": [4, 64, 256],
        "norm_ffn1": [4, D], "ffn1_gate": [4, D, DFF], "ffn1_up": [4, D, DFF], "ffn1_down": [4, DFF, D],
        "norm_mix": [4, D], "norm_ffn2": [4, D],
        "ffn2_gate": [4, D, DFF], "ffn2_up": [4, D, DFF], "ffn2_down": [4, DFF, D],
        "norm_ple": [4, D], "ple_gate": [4, D, D], "ple_proj": [4, 256, D], "norm_final": [1, D],
        "gm_w_in": [D, 2048], "gm_ln": [1, D], "gm_w_s": [8, 128, 128], "gm_b_s": [8, 128], "gm_w_out": [D, D],
        "pool_w": [4, 256, 256], "pool_scale": [1, D],
        "gla_w_in": [D, 3072], "gla_w_a1": [D, 16], "gla_w_a2": [16, 512], "gla_b_a": [1, 512],
        "gla_norm": [1, D], "gla_w_out": [D, D],
        "ssm_w_in": [D, 5152], "ssm_conv_w": [4, 3072], "ssm_conv_b": [1, 3072], "ssm_dt_bias": [1, 32],
        "ssm_a_log": [1, 32], "ssm_d": [1, 32], "ssm_norm": [1, 2048], "ssm_w_out": [2048, D],
        "c_sq": [128, 10, 128],
        "c_pool": [128, 12, 128], "c_pools": [64, 4, 64], "c_poolh": [120, 4, 2, 64],
        "c_selb": [64, 16], "c_selbt": [128, 16, 64], "c_expand": [8, 4, 128],
    }


def out_specs(NPT):
    TP = NPT * 128
    return {
        "yp": [TP, D], "ys": [64, D], "chunk_v": [64, D], "pool_p": [15, D], "pool_s": [16, 15, D],
        "gla_p": [4, 128, 256], "gla_s": [16, 4, 128, 256], "ssm_p": [2048, 128], "ssm_s": [16, 2048, 128],
        "conv_p": [3, 3072], "conv_s": [16, 3, 3072],
    }


def make_consts():
    c = {}
    sq = np.zeros((128, 10, 128), np.float32)
    s = np.arange(128)[:, None]
    t = np.arange(128)[None, :]
    sq[:, 0] = (s == t)
    sq[:, 1] = (s <= t)
    sq[:, 2] = (s > t)
    sq[:, 3] = 1.0
    same = (s // 4 == t // 4) & (s < 64) & (t < 64)
    sq[:, 4] = same & (s <= t)
    sq[:, 5] = same & (s > t)
    sq[:, 6] = np.where(s <= t, 0.0, NEG)
    sq[:, 7] = np.where(same & (s <= t), 0.0, NEG)
    sq[:, 8] = (s >= t)
    sq[:, 9] = (t == 4 * (s // 4) + 3) & (s < 64)
    c["c_sq"] = sq
    pc = np.zeros((128, 12, 128), np.float32)
    pss = np.zeros((64, 4, 64), np.float32)
    ph = np.zeros((120, 4, 2, 64), np.float32)
    for wi, w in enumerate((2, 4, 8, 16)):
        d = t - s
        pc[:, wi * 3 + 0] = ((d >= 0) & (d < w)) / float(w) - (s == t)
        d2 = t + 128 - s
        pc[:, wi * 3 + 1] = ((d2 >= 0) & (d2 < w)) / float(w)
        cnt = np.minimum(t + 1, w).astype(np.float32)
        pc[:, wi * 3 + 2] = ((d >= 0) & (d < w)) / cnt - (s == t)
        s6 = np.arange(64)[:, None]
        t6 = np.arange(64)[None, :]
        d6 = (t6 % 4) - (s6 % 4)
        pss[:, wi] = ((s6 // 4 == t6 // 4) & (d6 >= 0) & (d6 < w)) / float(w) - (s6 == t6)
        for half in range(2):
            r = np.arange(120)[:, None]
            bb = r // 15 + half * 8
            j = r % 15
            dd = 15 + (t6 % 4) - j
            ph[:, wi, half] = ((bb == t6 // 4) & (dd < w)) / float(w)
    c["c_pool"] = pc
    c["c_pools"] = pss
    c["c_poolh"] = ph
    s6 = np.arange(64)[:, None]
    c["c_selb"] = (s6 // 4 == np.arange(16)[None, :]).astype(np.float32)
    sel = (np.arange(16)[:, None] == (np.arange(64)[None, :] // 4)).astype(np.float32)
    c["c_selbt"] = np.broadcast_to(sel[None], (128, 16, 64)).copy()
    ex = np.zeros((8, 4, 128), np.float32)
    for h in range(8):
        ex[h, h // 2, (h % 2) * 64:(h % 2) * 64 + 64] = 1.0
    c["c_expand"] = ex
    return c


class K:
    def __init__(self, NPT=16, mixers=(0, 1, 2, 3), same_engine_sync=True):
        self.NPT = NPT
        self.NT = NPT + 1
        self.T = NPT * 128 + 64
        self.mixers = mixers
        self.tiles = [(i, i * 128, 128) for i in range(NPT)] + [(NPT, NPT * 128, 64)]
        self.blocks = [(c0, min(512, NPT * 128 - c0)) for c0 in range(0, NPT * 128, 512)] + [(NPT * 128, 64)]
        nc = bass.Bass("TRN2", target_bir_lowering=False)
        self.nc = nc
        self.d = {k: nc.dram_tensor(k, v, F32, kind="ExternalInput").ap() for k, v in in_specs(NPT).items()}
        self.o = {k: nc.dram_tensor(k, v, F32, kind="ExternalOutput").ap() for k, v in out_specs(NPT).items()}
        self.S = Sched(nc, same_engine_sync)
        self.stages = []
        self.reserved = set()
        self._bset = None
        self._tbset = None
        self._bctr = {}
        self.phase_start = 0
        self._bank = 0
        self._tbank = 0
        self._gb = 0
        self._uid = 0

    def use_banks(self, idx, tidx):
        self._bset = idx
        self._tbset = tidx

    def bank(self):
        lst = self.banks if self._bset is None else [self.banks[j] for j in self._bset]
        key = "all" if self._bset is None else tuple(self._bset)
        while True:
            n = self._bctr.get(key, 0)
            self._bctr[key] = n + 1
            b = lst[n % len(lst)]
            if b.name not in self.reserved:
                return b

    def tbank(self):
        lst = self.tbanks if self._tbset is None else [self.tbanks[j] for j in self._tbset]
        key = "tall" if self._tbset is None else ("t",) + tuple(self._tbset)
        n = self._bctr.get(key, 0)
        self._bctr[key] = n + 1
        return lst[n % len(lst)]

    def record(self, f):
        assert self.S.rec is None
        self.S.rec = []
        f()
        lst = self.S.rec
        self.S.rec = None
        return lst

    @staticmethod
    def merge(a, b):
        out = []
        na, nb = len(a), len(b)
        ia = ib = 0
        while ia < na or ib < nb:
            if ib >= nb or (ia < na and ia * nb <= ib * na):
                out.append(a[ia])
                ia += 1
            else:
                out.append(b[ib])
                ib += 1
        return out

    def mm(self, ps, pairs, last_inc=True):
        n = len(pairs)
        for i, (l, r) in enumerate(pairs):
            self.S.op("pe", (lambda e, l=l, r=r, st=(i == 0), sp=(i == n - 1): e.matmul(ps, l, r, start=st, stop=sp)),
                      reads=[l, r], writes=[ps], inc=(i == n - 1) and last_inc)

    def tr(self, ps, in_, ident, inc):
        if in_.dtype == F32:
            self.mm(ps, [(in_, ident)], last_inc=inc)
            return
        self.S.op("pe", lambda e: e.transpose(out=ps, in_=in_, identity=ident), reads=[in_, ident], writes=[ps], inc=inc)

    def act(self, out, in_, func, reads=None, **kw):
        rd = [in_] + [v for v in kw.values() if hasattr(v, "ap")]
        wr = [out]
        if "accum_out" in kw:
            wr.append(kw["accum_out"])
            rd = [in_] + [v for k, v in kw.items() if hasattr(v, "ap") and k != "accum_out"]
        self.S.op("act", lambda e: e.activation(out=out, in_=in_, func=func, **kw), reads=rd, writes=wr)

    def tt(self, out, in0, in1, op, eng="dve"):
        self.S.op(eng, lambda e: e.tensor_tensor(out=out, in0=in0, in1=in1, op=op), reads=[in0, in1], writes=[out])

    def ts(self, out, in0, s1, s2, op0, op1=None, eng="dve"):
        rd = [in0] + [x for x in (s1, s2) if hasattr(x, "ap")]
        if op1 is None:
            self.S.op(eng, lambda e: e.tensor_scalar(out=out, in0=in0, scalar1=s1, scalar2=None, op0=op0), reads=rd, writes=[out])
        else:
            self.S.op(eng, lambda e: e.tensor_scalar(out=out, in0=in0, scalar1=s1, scalar2=s2, op0=op0, op1=op1), reads=rd, writes=[out])

    def stt(self, out, in0, scalar, in1, op0, op1, eng="dve"):
        rd = [in0, in1] + ([scalar] if hasattr(scalar, "ap") else [])
        self.S.op(eng, lambda e: e.scalar_tensor_tensor(out=out, in0=in0, scalar=scalar, in1=in1, op0=op0, op1=op1), reads=rd, writes=[out])

    def cp(self, out, in_, eng="dve"):
        self.S.op(eng, lambda e: e.tensor_copy(out=out, in_=in_), reads=[in_], writes=[out])

    def memset(self, out, val, eng="dve"):
        self.S.op(eng, lambda e: e.memset(out, val), reads=[], writes=[out])

    def uid(self, p):
        self._uid += 1
        return f"{p}{self._uid}"

    def load_cast(self, out, in_, chan):
        self.S.dma("pool", out, in_, chan)

    def load(self, out, in_, chan):
        self.S.dma("sp", out, in_, chan)

    def store(self, out, in_, chan):
        self.S.dma("sp", out, in_, chan)

    def build(self):
        nc = self.nc
        NT = self.NT
        T = self.T
        ARENA = 93184
        with (
            nc.sbuf_tensor("X", [128, NT, D], F32) as X,
            nc.sbuf_tensor("hT", [128, NCH, T], BF16) as hT,
            nc.sbuf_tensor("arena", [128, ARENA // 2], BF16) as arena,
            nc.sbuf_tensor("csq", [128, 10, 128], F32) as csq,
            nc.sbuf_tensor("identb", [128, 128], BF16) as identb,
            nc.sbuf_tensor("gB", [128, 1, D], F32) as gB,
            nc.sbuf_tensor("hn", [128, 2, D], BF16) as hn,
            nc.sbuf_tensor("junk", [128, D], BF16) as junk,
            nc.sbuf_tensor("stat", [128, 64], F32) as stat,
            nc.psum_tensor("B0", [128, 512], F32) as B0,
            nc.psum_tensor("B1", [128, 512], F32) as B1,
            nc.psum_tensor("B2", [128, 512], F32) as B2,
            nc.psum_tensor("B3", [128, 512], F32) as B3,
            nc.psum_tensor("B4", [128, 512], F32) as B4,
            nc.psum_tensor("B5", [128, 512], F32) as B5,
            nc.psum_tensor("T0", [128, 1024], BF16) as T0,
            nc.psum_tensor("T1", [128, 1024], BF16) as T1,
        ):
            self.X, self.hT, self.csq, self.identb, self.gB, self.hn, self.junk, self.stat = X, hT, csq, identb, gB, hn, junk, stat
            self.banks = [B0, B1, B2, B3, B4, B5]
            self.tbanks = [T0, T1]
            self.ar = Arena(arena, ARENA)
            self.prologue()
            for l in range(4):
                self.ffn(l, 1)
                self.mixer(l)
                self.ffn(l, 2)
                self.ple(l)
            self.final()
            self.emit()
            self.S.finish()
        return nc

    def emit(self):
        st = self.stages
        n = len(st)
        at = {}
        for i, s in enumerate(st):
            at.setdefault(max(0, i - s.la), []).append(i)
        for j in range(n):
            for i in at.get(j, []):
                if st[i].loads is not None:
                    st[i].loads()
            st[j].compute()

    def stage(self, loads, compute, la=2):
        i = len(self.stages)
        la = max(0, min(la, i - self.phase_start))
        self.stages.append(Stage(loads, compute, la))

    def new_phase(self):
        self.phase_start = len(self.stages)

    def prologue(self):
        def loads():
            self.load(self.csq[:], self.d["c_sq"][:], "csq")
            self.load_cast(self.identb[:], self.d["c_sq"][:, 0, :], "identb")
            for (i, c0, r) in self.tiles:
                src = self.d["xp"][c0:c0 + r, :] if i < self.NPT else self.d["xs"][:, :]
                self.load(self.X[0:r, i, :], src, "X%d" % (i * 4 // self.NT))

        self.stage(loads, lambda: None, la=0)

    def make_hT(self, grow, extra=None):
        slot = 0
        gBs = self.gB[:, slot, :]

        def loads():
            self.load(gBs, grow.partition_broadcast(128), "gB%d" % slot)

        def compute(after_tile=None):
            X, hT, stat = self.X, self.hT, self.stat

            def stA(i, c0, r):
                hs = self.hn[0:r, i % 2, :]
                ss = stat[0:r, (i % 4) * 2:(i % 4) * 2 + 1]
                rs = stat[0:r, (i % 4) * 2 + 1:(i % 4) * 2 + 2]
                self.act(self.junk[0:r, :], X[0:r, i, :], AF.Square, accum_out=ss)
                self.act(rs, ss, AF.Ln, scale=1.0 / D, bias=EPS)
                self.act(rs, rs, AF.Exp, scale=-0.5)
                self.stt(hs, X[0:r, i, :], rs, gBs[0:r, :], ALU.mult, ALU.mult)
                if extra is not None:
                    extra(i, c0, r, rs, gBs)

            def stB(i, c0, r):
                hs = self.hn[0:r, i % 2, :]
                tb = self.tbank()
                tbv = tb[:].rearrange("p (a b) -> p a b", a=8)
                for k in range(8):
                    self.tr(tbv[:, k, 0:r], hs[:, 128 * k:128 * k + 128], self.identb[0:r, 0:r], inc=(k == 7))
                self.cp(hT[:, :, c0:c0 + r], tbv[:, :, 0:r])

            n = len(self.tiles)
            for idx in range(n + 1):
                if idx < n:
                    stA(*self.tiles[idx])
                if idx >= 1:
                    stB(*self.tiles[idx - 1])
                    if after_tile is not None:
                        after_tile(idx - 1)

        return loads, compute

    def ffn(self, l, which):
        S = self.S
        Wg = self.d["ffn%d_gate" % which][l]
        Wu = self.d["ffn%d_up" % which][l]
        Wd = self.d["ffn%d_down" % which][l]
        self.new_phase()
        nl, ncmp = self.make_hT(self.d["norm_ffn%d" % which][l:l + 1, :])
        ar = self.ar
        ar.reset()
        NSLOT = 4
        gu = [ar.alloc([128, 2, 8, 128], BF16) for _ in range(NSLOT)]
        wd = ar.alloc([128, 6, D], BF16)
        aT = ar.alloc([128, 6, self.T], BF16)
        sil = [ar.alloc([128, 512], F32) for _ in range(2)]
        quarters = [list(range(0, 6)), list(range(6, 11)), list(range(11, 17)), list(range(17, 22))]
        first = [True]
        cnt = [0]

        def gu_stage(j, jj, la, first_hook=None):
            slot = cnt[0] % NSLOT
            cnt[0] += 1
            g = gu[slot]

            def loads():
                self.load_cast(g[:, 0], Wg[:, 128 * j:128 * j + 128].rearrange("(k p) n -> p k n", p=128), "gug%d" % slot)
                self.load_cast(g[:, 1], Wu[:, 128 * j:128 * j + 128].rearrange("(k p) n -> p k n", p=128), "guu%d" % slot)

            def block(bi):
                c0, bs = self.blocks[bi]
                pg = self.bank()
                pu = self.bank()
                self.mm(pg[:, 0:bs], [(g[:, 0, k, :], self.hT[:, k, c0:c0 + bs]) for k in range(8)])
                self.mm(pu[:, 0:bs], [(g[:, 1, k, :], self.hT[:, k, c0:c0 + bs]) for k in range(8)])
                st = sil[bi % 2]
                self.act(st[:, 0:bs], pg[:, 0:bs], AF.Silu)
                self.tt(aT[:, jj, c0:c0 + bs], st[:, 0:bs], pu[:, 0:bs], ALU.mult)

            def compute():
                for bi in range(len(self.blocks)):
                    block(bi)

            if first_hook is not None:
                first_hook.append(block)
                self.stage(loads, lambda: None, la)
            else:
                self.stage(loads, compute, la)

        def down_stage(q):
            nq = len(q)

            def loads():
                self.load_cast(wd[:, 0:nq, :], Wd[128 * q[0]:128 * (q[-1] + 1), :].rearrange("(j p) n -> p j n", p=128), "wd")

            def compute():
                for (i, c0, r) in self.tiles:
                    for nb in range(2):
                        ps = self.bank()
                        self.mm(ps[0:r, :], [(aT[:, jj, c0:c0 + r], wd[:, jj, nb * 512:nb * 512 + 512]) for jj in range(nq)])
                        xs = self.X[0:r, i, nb * 512:nb * 512 + 512]
                        self.stt(xs, ps[0:r, :], 0.5, xs, ALU.mult, ALU.add)

            self.stage(loads, compute, 2)

        hook = []
        last_tile = {}
        for bi, (bc0, bs) in enumerate(self.blocks):
            for (ti, tc0, tr_) in self.tiles:
                if bc0 <= tc0 < bc0 + bs:
                    last_tile[bi] = ti

        def after_tile(ti):
            for bi, lt in last_tile.items():
                if lt == ti:
                    hook[0](bi)

        self.stage(nl, lambda: ncmp(after_tile=after_tile), la=0)
        for qi, q in enumerate(quarters):
            for jj, j in enumerate(q):
                gu_stage(j, jj, 2, first_hook=(hook if (qi == 0 and jj == 0) else None))
            down_stage(q)

    def ple(self, l):
        ar = self.ar
        self.new_phase()
        nl, ncmp = self.make_hT(self.d["norm_ple"][l:l + 1, :])
        ar.reset()
        Wpg = ar.alloc([128, 8, D], BF16)
        Wpp = ar.alloc([128, 2, D], BF16)
        ptok = ar.alloc([128, self.NT, 256], BF16)
        pT = ar.alloc([128, 2, self.T], BF16)
        sig = [ar.alloc([128, 512], F32) for _ in range(2)]
        tmp = [ar.alloc([128, 512], F32) for _ in range(2)]
        NPT = self.NPT

        def loads():
            nl()
            self.load_cast(ptok[:, 0:NPT, :], self.d["pp"][l].rearrange("(i p) c -> p i c", p=128), "ptok")
            self.load_cast(ptok[0:64, NPT, :], self.d["ps"][l], "ptoks")
            self.load_cast(Wpp[:], self.d["ple_proj"][l].rearrange("(k p) n -> p k n", p=128), "Wpp")
            self.load_cast(Wpg[:], self.d["ple_gate"][l].rearrange("(k p) n -> p k n", p=128), "Wpg")

        def compute():
            for (i, c0, r) in self.tiles:
                tb = self.tbank()
                for k in range(2):
                    self.tr(tb[:, k * 128:k * 128 + r], ptok[0:r, i, 128 * k:128 * k + 128], self.identb[0:r, 0:r], inc=(k == 1))
                self.act(pT[:, :, c0:c0 + r], tb[:, 0:256].rearrange("p (a b) -> p a b", a=2)[:, :, 0:r], AF.Copy)

            def tile_part(ti):
                (i, c0, r) = self.tiles[ti]
                for nb in range(2):
                    pg = self.bank()
                    pp_ = self.bank()
                    self.mm(pg[0:r, :], [(self.hT[:, k, c0:c0 + r], Wpg[:, k, nb * 512:nb * 512 + 512]) for k in range(8)])
                    self.mm(pp_[0:r, :], [(pT[:, k, c0:c0 + r], Wpp[:, k, nb * 512:nb * 512 + 512]) for k in range(2)])
                    sg = sig[nb]
                    tm = tmp[nb]
                    self.act(sg[0:r, :], pg[0:r, :], AF.Exp, scale=-1.0)
                    self.ts(sg[0:r, :], sg[0:r, :], 1.0, None, ALU.add, eng="pool")
                    self.S.op("dve", lambda e, a_=sg[0:r, :]: e.reciprocal(out=a_, in_=a_), reads=[sg[0:r, :]], writes=[sg[0:r, :]])
                    self.tt(tm[0:r, :], sg[0:r, :], pp_[0:r, :], ALU.mult)
                    xs = self.X[0:r, i, nb * 512:nb * 512 + 512]
                    self.tt(xs, xs, tm[0:r, :], ALU.add)

            ncmp(after_tile=tile_part)

        self.stage(loads, compute, la=0)

    def final(self):
        ar = self.ar
        self.new_phase()
        ar.reset()
        yb = [ar.alloc([128, D], F32) for _ in range(2)]
        slot = 0
        gBs = self.gB[:, slot, :]

        def loads():
            self.load(gBs, self.d["norm_final"][0:1, :].partition_broadcast(128), "gB%d" % slot)

        def compute():
            X, stat = self.X, self.stat
            for (i, c0, r) in self.tiles:
                ss = stat[0:r, (i % 4) * 2:(i % 4) * 2 + 1]
                rs = stat[0:r, (i % 4) * 2 + 1:(i % 4) * 2 + 2]
                self.act(self.junk[0:r, :], X[0:r, i, :], AF.Square, accum_out=ss)
                self.act(rs, ss, AF.Ln, scale=1.0 / D, bias=EPS)
                self.act(rs, rs, AF.Exp, scale=-0.5)
                y = yb[i % 2]
                self.stt(y[0:r, :], X[0:r, i, :], rs, gBs[0:r, :], ALU.mult, ALU.mult)
                dst = self.o["yp"][c0:c0 + r, :] if i < self.NPT else self.o["ys"][:, :]
                self.store(dst, y[0:r, :], "yb%d" % (i % 2))

        self.stage(loads, compute, la=0)

    def mixer(self, l):
        if l not in self.mixers:
            return
        [self.mix_gmlp, self.mix_pool, self.mix_gla, self.mix_ssd][l](l)

    def mix_gmlp(self, l):
        ar = self.ar
        self.new_phase()
        nl, ncmp = self.make_hT(self.d["norm_mix"][l:l + 1, :])
        ar.reset()
        Wu = ar.alloc([128, 8, D], BF16)
        Wv = ar.alloc([128, 8, D], BF16)
        Wo = ar.alloc([128, 8, D], BF16)
        wsbf = ar.alloc([128, 8, 128], BF16)
        WsT = ar.alloc([128, 8, 128], BF16)
        WsS = ar.alloc([128, 8, 64], BF16)
        bsB = ar.alloc([128, 8, 128], F32)
        bsS = ar.alloc([128, 8, 64], F32)
        gln = ar.alloc([128, D], F32)
        mark = ar.off
        wsnat = ar.alloc([128, 8, 128], F32)
        ar.reset(mark)
        uTs = [ar.alloc([128, 8, 128], F32) for _ in range(2)]
        vt = ar.alloc([128, D], F32)
        vtmp = ar.alloc([128, D], F32)
        vbfs = [ar.alloc([128, D], BF16) for _ in range(2)]
        gT = ar.alloc([128, 8, 128], BF16)
        svt = ar.alloc([128, 4, 128], F32)
        st = self.stat
        gw = self.d["gm_w_in"]

        def loads():
            nl()
            self.load_cast(Wu[:], gw[:, 0:D].rearrange("(k p) n -> p k n", p=128), "mwA")
            self.load_cast(Wv[:], gw[:, D:2 * D].rearrange("(k p) n -> p k n", p=128), "mwB")
            self.load_cast(Wo[:], self.d["gm_w_out"].rearrange("(k p) n -> p k n", p=128), "mwC")
            self.load(wsnat[:], self.d["gm_w_s"].rearrange("g t s -> t g s"), "mwD")
            self.load(bsB[:].rearrange("p a b -> p (a b)"), self.d["gm_b_s"].rearrange("g t -> (g t)").partition_broadcast(128), "mwE")
            self.load(gln[:], self.d["gm_ln"][0, :].partition_broadcast(128), "mwF")

        def compute():
            ncmp()
            self.tt(wsbf[:], wsnat[:], self.csq[:, 8, :].unsqueeze(1).to_broadcast([128, 8, 128]), ALU.mult)
            tb = self.tbank()
            tbv = tb[:].rearrange("p (a b) -> p a b", a=8)
            for g in range(8):
                self.tr(tbv[:, g, :], wsbf[:, g, :], self.identb[:], inc=(g == 7))
            self.act(WsT[:], tbv[:], AF.Copy)
            self.memset(WsS[:], 0.0)
            for b in range(16):
                self.S.dma("sp", WsS[4 * b:4 * b + 4, :, 4 * b:4 * b + 4], WsT[0:4, :, 0:4], "wss")
            self.cp(bsS[:].rearrange("p g (b t) -> p g b t", b=16), bsB[:, :, 0:4].unsqueeze(2).to_broadcast([128, 8, 16, 4]))
            def part1(i, c0, r, par):
                samp = (i == self.NPT)
                uT, vbf = uTs[par], vbfs[par]
                for half in range(2):
                    pu = self.bank()
                    puv = pu[:].rearrange("p (a b) -> p a b", a=4)
                    for mm_ in range(4):
                        m = half * 4 + mm_
                        self.mm(puv[:, mm_, 0:r], [(Wu[:, k, 128 * m:128 * m + 128], self.hT[:, k, c0:c0 + r]) for k in range(8)], last_inc=(mm_ == 3))
                    self.act(uT[:, half * 4:half * 4 + 4, 0:r], puv[:, :, 0:r], AF.Gelu_apprx_tanh)
                for nb in range(2):
                    pv = self.bank()
                    self.mm(pv[0:r, :], [(self.hT[:, k, c0:c0 + r], Wv[:, k, nb * 512:nb * 512 + 512]) for k in range(8)])
                    self.act(vt[0:r, nb * 512:nb * 512 + 512], pv[0:r, :], AF.Gelu_apprx_tanh, accum_out=st[0:r, 16 + nb:17 + nb])
                self.act(self.junk[0:r, :], vt[0:r, :], AF.Square, accum_out=st[0:r, 18:19])
                self.tt(st[0:r, 19:20], st[0:r, 16:17], st[0:r, 17:18], ALU.add)
                self.ts(st[0:r, 20:21], st[0:r, 19:20], 1.0 / D, None, ALU.mult)
                self.tt(st[0:r, 21:22], st[0:r, 20:21], st[0:r, 20:21], ALU.mult)
                self.stt(st[0:r, 22:23], st[0:r, 18:19], 1.0 / D, st[0:r, 21:22], ALU.mult, ALU.subtract)
                self.act(st[0:r, 23:24], st[0:r, 22:23], AF.Ln, bias=EPS)
                self.act(st[0:r, 23:24], st[0:r, 23:24], AF.Exp, scale=-0.5)
                self.ts(vtmp[0:r, :], vt[0:r, :], st[0:r, 20:21], st[0:r, 23:24], ALU.subtract, ALU.mult)
                if samp:
                    self.tt(vt[0:r, :], vtmp[0:r, :], gln[0:r, :], ALU.mult)
                    self.store(self.o["chunk_v"][:, :], vt[0:r, :], "cv")
                    self.cp(vbf[0:r, :], vt[0:r, :])
                else:
                    self.tt(vbf[0:r, :], vtmp[0:r, :], gln[0:r, :], ALU.mult)

            def part2(i, c0, r, par):
                samp = (i == self.NPT)
                uT, vbf = uTs[par], vbfs[par]
                Wmix = WsS if samp else WsT
                bias = bsS if samp else bsB
                for half in range(2):
                    psv = self.bank()
                    pv4 = psv[:].rearrange("p (a b) -> p a b", a=4)
                    for gg in range(4):
                        g = half * 4 + gg
                        self.mm(pv4[:, gg, 0:r], [(vbf[0:r, 128 * g:128 * g + 128], Wmix[0:r, g, 0:r])], last_inc=(gg == 3))
                    self.tt(svt[:, :, 0:r], pv4[:, :, 0:r], bias[:, half * 4:half * 4 + 4, 0:r], ALU.add)
                    self.tt(gT[:, half * 4:half * 4 + 4, 0:r], svt[:, :, 0:r], uT[:, half * 4:half * 4 + 4, 0:r], ALU.mult)
                for nb in range(2):
                    po = self.bank()
                    self.mm(po[0:r, :], [(gT[:, m, 0:r], Wo[:, m, nb * 512:nb * 512 + 512]) for m in range(8)])
                    xs = self.X[0:r, i, nb * 512:nb * 512 + 512]
                    self.tt(xs, xs, po[0:r, :], ALU.add)

            prev = None
            for idx, (i, c0, r) in enumerate(self.tiles):
                par = idx % 2
                self.use_banks([0, 1, 2], [0])
                L1 = self.record(lambda: part1(i, c0, r, par))
                if prev is None:
                    self.S.play(L1)
                else:
                    self.use_banks([3, 4, 5], [1])
                    L2 = self.record(lambda: part2(*prev))
                    self.S.play(self.merge(L1, L2))
                prev = (i, c0, r, par)
            self.use_banks([3, 4, 5], [1])
            L2 = self.record(lambda: part2(*prev))
            self.S.play(L2)
            self.use_banks(None, None)

        self.stage(loads, compute, la=0)

    def mix_pool(self, l):
        ar = self.ar
        NPT = self.NPT
        self.new_phase()
        ar.reset()
        HN = ar.alloc([128, self.NT, D], BF16)
        Wp = ar.alloc([128, 4, 2, 256], BF16)
        PC = ar.alloc([128, 12, 128], BF16)
        PS = ar.alloc([128, 4, 64], BF16)
        PH = ar.alloc([128, 4, 2, 64], BF16)
        HB = ar.alloc([128, 2, D], BF16)
        scB = ar.alloc([128, D], F32)
        hf = ar.alloc([128, 2, D], F32)
        diffT = [ar.alloc([128, 8, 128], BF16) for _ in range(2)]
        tmp = [ar.alloc([128, 512], F32) for _ in range(2)]

        def extra(i, c0, r, rs, gBs):
            self.cp(HN[0:r, i, :], self.hn[0:r, i % 2, :], eng="pool")
            if i >= NPT - 1:
                self.stt(hf[0:r, i - (NPT - 1), :], self.X[0:r, i, :], rs, gBs[0:r, :], ALU.mult, ALU.mult)

        nl, ncmp = self.make_hT(self.d["norm_mix"][l:l + 1, :], extra=extra)

        def loads():
            nl()
            self.load_cast(Wp[:], self.d["pool_w"].rearrange("g (cc p) d -> p g cc d", p=128), "mwA")
            self.load_cast(PC[:], self.d["c_pool"], "mwB")
            self.load_cast(PS[0:64], self.d["c_pools"], "mwC")
            self.load_cast(PH[0:120], self.d["c_poolh"], "mwD")
            self.load_cast(HB[0:120], self.d["st_pool"].rearrange("(h r) d -> r h d", h=2), "mwE")
            self.load(scB[:], self.d["pool_scale"][0, :].partition_broadcast(128), "mwF")
            self.S.dma("sp", self.o["pool_s"][:, 0:11, :], self.d["st_pool"].rearrange("(b j) d -> b j d", j=15)[:, 4:15, :], "poolhist")

        def compute():
            ncmp()
            self.store(self.o["pool_p"][:, :], hf[113:128, 0, :], "poolp")
            for b in range(16):
                self.store(self.o["pool_s"][b, 11:15, :], hf[4 * b:4 * b + 4, 1, :], "pools")
            for (i, c0, r) in self.tiles:
                samp = (i == NPT)
                dT = diffT[i % 2]
                for half in range(2):
                    pb = self.bank()
                    pbv = pb[:].rearrange("p (a b) -> p a b", a=4)
                    for mm_ in range(4):
                        m = half * 4 + mm_
                        w = m // 2
                        fs = slice(128 * m, 128 * m + 128)
                        if samp:
                            pairs = [(HN[0:64, i, fs], PS[0:64, w, :]), (HB[0:120, 0, fs], PH[0:120, w, 0, :]), (HB[0:120, 1, fs], PH[0:120, w, 1, :])]
                        else:
                            pairs = [(HN[0:r, i, fs], PC[0:r, w * 3 + (2 if i == 0 else 0), 0:r])]
                            if i > 0:
                                pairs.append((HN[64:128, i - 1, fs], PC[64:128, w * 3 + 1, 0:r]))
                        self.mm(pbv[:, mm_, 0:r], pairs, last_inc=(mm_ == 3))
                    self.act(dT[:, half * 4:half * 4 + 4, 0:r], pbv[:, :, 0:r], AF.Copy)
                for nb in range(2):
                    po = self.bank()
                    for gg in range(2):
                        g = nb * 2 + gg
                        self.mm(po[0:r, gg * 256:gg * 256 + 256], [(dT[:, 2 * g + cc, 0:r], Wp[:, g, cc, :]) for cc in range(2)], last_inc=(gg == 1))
                    tm = tmp[nb]
                    self.tt(tm[0:r, :], po[0:r, :], scB[0:r, nb * 512:nb * 512 + 512], ALU.mult)
                    xs = self.X[0:r, i, nb * 512:nb * 512 + 512]
                    self.tt(xs, xs, tm[0:r, :], ALU.add)

        self.stage(loads, compute, la=0)

    def mix_gla(self, l):
        ar = self.ar
        NPT = self.NPT
        T = self.T
        self.new_phase()
        ar.reset()
        nl, ncmp = self.make_hT(self.d["norm_mix"][l:l + 1, :])
        Wa1 = ar.alloc([128, 8, 16], BF16)
        Wa2 = ar.alloc([128, 512], BF16)
        baB = ar.alloc([128, 512], F32)
        gnT = ar.alloc([128, 8], F32)
        gnB = ar.alloc([128, 8, 128], F32)
        selb = ar.alloc([128, 16], F32)
        t1 = ar.alloc([128, T], BF16)
        Ws = [dict(q=ar.alloc([128, 8, 128], BF16), k=ar.alloc([128, 8, 128], BF16), v=ar.alloc([128, 8, 256], BF16),
                   r=ar.alloc([128, 8, 256], BF16), o=ar.alloc([128, 2, D], BF16)) for _ in range(2)]
        qds = [ar.alloc([128, 128], BF16) for _ in range(2)]
        ki = ar.alloc([128, 128], BF16)
        rss = [ar.alloc([128, 2, 128], F32) for _ in range(2)]
        vbfs = [ar.alloc([128, 256], BF16) for _ in range(2)]
        zb = ar.alloc([128, 128], F32)
        lp = ar.alloc([128, 128], F32)
        ebs = [ar.alloc([128, 128], F32) for _ in range(2)]
        einv = ar.alloc([128, 128], F32)
        ee = ar.alloc([128, 128], F32)
        kends = [ar.alloc([128, 128], BF16) for _ in range(2)]
        scs = [ar.alloc([128, 128], BF16) for _ in range(2)]
        oT = ar.alloc([128, 2, 128], F32)
        sq = ar.alloc([128, 2, 128], F32)
        rstdB = ar.alloc([128, 128], F32)
        gT = ar.alloc([128, 2, 128], BF16)
        Sst = ar.alloc([128, 256], F32)
        Sbf = ar.alloc([128, 256], BF16)
        S0 = ar.alloc([128, 16, 256], F32)
        S0bf = [ar.alloc([128, 256], BF16) for _ in range(4)]
        Snew = [ar.alloc([128, 256], F32) for _ in range(4)]
        Vexp = ar.alloc([128, 16, 256], BF16)
        csq = self.csq
        ones = csq[:, 3, :]
        win = self.d["gla_w_in"]

        def loads0():
            nl()
            self.load_cast(Wa1[:], self.d["gla_w_a1"].rearrange("(k p) n -> p k n", p=128), "mwA")
            self.load_cast(Wa2[0:16, :], self.d["gla_w_a2"], "mwB")
            self.load(baB[:], self.d["gla_b_a"][0, :].partition_broadcast(128), "mwC")
            self.S.dma("sp", gnT[:], self.d["gla_norm"][0, :].rearrange("(m p) -> p m", p=128), "mwD", allow_slow_non_contiguous=True)
            self.load(selb[0:64, :], self.d["c_selb"], "mwE")

        def compute0():
            ncmp()
            for (c0, bs) in self.blocks:
                pt = self.bank()
                self.mm(pt[0:16, 0:bs], [(Wa1[:, k, :], self.hT[:, k, c0:c0 + bs]) for k in range(8)])
                self.act(t1[0:16, c0:c0 + bs], pt[0:16, 0:bs], AF.Copy)
            self.cp(gnB[:], gnT[:].unsqueeze(2).to_broadcast([128, 8, 128]))

        self.stage(loads0, compute0, la=0)

        def head_stage(hd):
            W = Ws[hd % 2]

            def loads():
                self.load_cast(W["q"][:], win[:, hd * 128:hd * 128 + 128].rearrange("(k p) n -> p k n", p=128), "gq%d" % (hd % 2))
                self.load_cast(W["k"][:], win[:, 512 + hd * 128:512 + hd * 128 + 128].rearrange("(k p) n -> p k n", p=128), "gk%d" % (hd % 2))
                self.load_cast(W["v"][:], win[:, 1024 + hd * 256:1024 + hd * 256 + 256].rearrange("(k p) n -> p k n", p=128), "gv%d" % (hd % 2))
                self.load_cast(W["r"][:], win[:, 2048 + hd * 256:2048 + hd * 256 + 256].rearrange("(k p) n -> p k n", p=128), "gr%d" % (hd % 2))
                self.load_cast(W["o"][:], self.d["gla_w_out"][hd * 256:hd * 256 + 256, :].rearrange("(k p) n -> p k n", p=128), "go%d" % (hd % 2))

            def part1(i, c0, r, par):
                samp = (i == NPT)
                TriU = csq[:, 4 if samp else 1, :]
                TriSL = csq[:, 5 if samp else 2, :]
                qd, sc, kend, vbf, rs, eb = qds[par], scs[par], kends[par], vbfs[par], rss[par], ebs[par]
                hTt = lambda k: self.hT[:, k, c0:c0 + r]
                st8 = {}

                def pa():
                    pq = self.bank()
                    self.mm(pq[:, 0:r], [(W["q"][:, k, :], hTt(k)) for k in range(8)])
                    self.mm(pq[:, 128:128 + r], [(W["k"][:, k, :], hTt(k)) for k in range(8)])
                    pv = self.bank()
                    self.mm(pv[0:r, 0:256], [(hTt(k), W["v"][:, k, :]) for k in range(8)])
                    self.mm(pv[0:r, 256:384], [(hTt(k), W["k"][:, k, :]) for k in range(8)])
                    self.act(vbf[0:r, :], pv[0:r, 0:256], AF.Copy)
                    st8["pq"], st8["pv"] = pq, pv

                def pb():
                    pz = self.bank()
                    self.mm(pz[0:r, 0:128], [(t1[0:16, c0:c0 + r], Wa2[0:16, hd * 128:hd * 128 + 128])])
                    self.tt(zb[0:r, :], pz[0:r, 0:128], baB[0:r, hd * 128:hd * 128 + 128], ALU.add)
                    self.act(zb[0:r, :], zb[0:r, :], AF.Exp, scale=-1.0)
                    self.act(lp[0:r, :], zb[0:r, :], AF.Ln, bias=1.0)
                    pc = self.bank()
                    self.mm(pc[:, 0:r], [(lp[0:r, :], TriU[0:r, 0:r])])
                    self.mm(pc[0:r, 128:256], [(TriSL[0:r, 0:r], lp[0:r, :])])
                    self.act(eb[:, 0:r], pc[:, 0:r], AF.Exp, scale=-1.0 / 16)
                    self.act(einv[:, 0:r], pc[:, 0:r], AF.Exp, scale=1.0 / 16)
                    self.act(ee[0:r, :], pc[0:r, 128:256], AF.Exp, scale=-1.0 / 16)

                def pc_():
                    pq, pv = st8["pq"], st8["pv"]
                    self.stt(qd[:, 0:r], pq[:, 0:r], 128.0 ** -0.5, eb[:, 0:r], ALU.mult, ALU.mult)
                    self.tt(ki[:, 0:r], pq[:, 128:128 + r], einv[:, 0:r], ALU.mult)
                    self.tt(kend[0:r, :], pv[0:r, 256:384], ee[0:r, :], ALU.mult)
                    pr = self.bank()
                    prv = pr[:, 0:256].rearrange("p (a b) -> p a b", a=2)
                    for vh in range(2):
                        self.mm(prv[:, vh, 0:r], [(W["r"][:, k, vh * 128:vh * 128 + 128], hTt(k)) for k in range(8)], last_inc=(vh == 1))
                    self.act(rs[:, :, 0:r], prv[:, :, 0:r], AF.Silu)
                    psc = self.bank()
                    self.mm(psc[0:r, 0:r], [(ki[:, 0:r], qd[:, 0:r])])
                    self.tt(sc[0:r, 0:r], psc[0:r, 0:r], TriU[0:r, 0:r], ALU.mult)

                self.S.rec = None
                self.use_banks([0, 1], [0])
                LA = self.record(pa)
                self.use_banks([2], [0])
                LB = self.record(pb)
                self.use_banks([0, 1], [0])
                LC = self.record(pc_)
                self.S.rec = self.merge(LA, LB) + LC

            def part2(i, c0, r, par):
                samp = (i == NPT)
                qd, sc, kend, vbf, rs, eb = qds[par], scs[par], kends[par], vbfs[par], rss[par], ebs[par]
                if not samp:
                    po = self.bank()
                    pov = po[:, 0:256].rearrange("p (a b) -> p a b", a=2)
                    for vh in range(2):
                        pairs = [(vbf[0:r, vh * 128:vh * 128 + 128], sc[0:r, 0:r])]
                        if i > 0:
                            pairs.append((Sbf[:, vh * 128:vh * 128 + 128], qd[:, 0:r]))
                        self.mm(pov[:, vh, 0:r], pairs, last_inc=(vh == 1))
                    self.act(oT[:, :, 0:r], pov[:, :, 0:r], AF.Copy)
                else:
                    pos = [self.bank(), self.bank()]
                    for vh in range(2):
                        l0, r0 = vbf[0:r, vh * 128:vh * 128 + 128], sc[0:r, 0:r]
                        self.S.op("pe", (lambda e, o_=pos[vh][:, 0:r], l0=l0, r0=r0: e.matmul(o_, l0, r0, start=True, stop=False)),
                                  reads=[l0, r0], writes=[pos[vh][:, 0:r]], inc=False)
                    for b in range(16):
                        sb = S0bf[b % 4]
                        self.act(sb[:], S0[:, b, :], AF.Copy)
                        for vh in range(2):
                            l1, r1 = sb[:, vh * 128:vh * 128 + 128], qd[:, 4 * b:4 * b + 4]
                            last = (b == 15)
                            self.S.op("pe", (lambda e, o_=pos[vh][:, 4 * b:4 * b + 4], l1=l1, r1=r1, last=last: e.matmul(o_, l1, r1, start=False, stop=last)),
                                      reads=[l1, r1], writes=[pos[vh][:, 4 * b:4 * b + 4]], inc=(vh == 1))
                    for vh in range(2):
                        self.act(oT[:, vh, 0:r], pos[vh][:, 0:r], AF.Copy)
                self.tt(sq[:, :, 0:r], oT[:, :, 0:r], oT[:, :, 0:r], ALU.mult)
                pss = self.bank()
                self.mm(pss[:, 0:r], [(ones, sq[:, 0, 0:r]), (ones, sq[:, 1, 0:r])])
                self.act(rstdB[:, 0:r], pss[:, 0:r], AF.Ln, scale=1.0 / 256, bias=EPS)
                self.act(rstdB[:, 0:r], rstdB[:, 0:r], AF.Exp, scale=-0.5)
                self.tt(oT[:, :, 0:r], oT[:, :, 0:r], rstdB[:, 0:r].unsqueeze(1).to_broadcast([128, 2, r]), ALU.mult)
                self.tt(oT[:, :, 0:r], oT[:, :, 0:r], rs[:, :, 0:r], ALU.mult)
                self.tt(gT[:, :, 0:r], oT[:, :, 0:r], gnB[:, 2 * hd:2 * hd + 2, 0:r], ALU.mult)
                for nb in range(2):
                    pout = self.bank()
                    self.mm(pout[0:r, :], [(gT[:, vh, 0:r], W["o"][:, vh, nb * 512:nb * 512 + 512]) for vh in range(2)])
                    xs = self.X[0:r, i, nb * 512:nb * 512 + 512]
                    self.tt(xs, xs, pout[0:r, :], ALU.add)
                if not samp:
                    psu = self.bank()
                    self.mm(psu[:, 0:256], [(kend[0:r, :], vbf[0:r, :])])
                    if i == 0:
                        self.cp(Sst[:], psu[:, 0:256])
                    else:
                        self.stt(Sst[:], Sst[:], eb[:, r - 1:r], psu[:, 0:256], ALU.mult, ALU.add)
                    if i == NPT - 1:
                        self.store(self.o["gla_p"][hd], Sst[:], "glap")
                    else:
                        self.act(Sbf[:], Sst[:], AF.Copy)
                else:
                    self.tt(Vexp[0:64], vbf[0:64, :].unsqueeze(1).to_broadcast([64, 16, 256]),
                            selb[0:64, :].unsqueeze(2).to_broadcast([64, 16, 256]), ALU.mult)
                    for pb in range(8):
                        psu = self.bank()
                        self.mm(psu[:, 0:512], [(kend[0:64, :], Vexp[0:64, 2 * pb:2 * pb + 2, :].rearrange("p a b -> p (a b)"))])
                        for b in (2 * pb, 2 * pb + 1):
                            sn = Snew[b % 4]
                            self.stt(sn[:], S0[:, b, :], eb[:, 4 * b + 3:4 * b + 4], psu[:, (b % 2) * 256:(b % 2) * 256 + 256], ALU.mult, ALU.add)
                            self.store(self.o["gla_s"][b, hd], sn[:], "glas%d" % (b % 2))

            def compute():
                for b in range(16):
                    self.load(S0[:, b, :], self.d["st_gla"][b, hd], "gS%d" % (b % 2))
                prev = None
                for idx, (i, c0, r) in enumerate(self.tiles):
                    par = idx % 2
                    self.use_banks([0, 1, 2], [0])
                    L1 = self.record(lambda: part1(i, c0, r, par))
                    if prev is None:
                        self.S.play(L1)
                    else:
                        self.use_banks([3, 4, 5], [1])
                        L2 = self.record(lambda: part2(*prev))
                        self.S.play(self.merge(L1, L2))
                    prev = (i, c0, r, par)
                self.use_banks([3, 4, 5], [1])
                L2 = self.record(lambda: part2(*prev))
                self.S.play(L2)
                self.use_banks(None, None)

            self.stage(loads, compute, la=1)

        for hd in range(4):
            head_stage(hd)

    def mix_ssd(self, l):
        ar = self.ar
        NPT = self.NPT
        S = self.S
        AX = mybir.AxisListType.X
        self.new_phase()
        ar.reset()
        nl, ncmp = self.make_hT(self.d["norm_mix"][l:l + 1, :])
        Wz = ar.alloc([128, 8, 512], BF16)
        Wx = ar.alloc([128, 8, 512], BF16)
        WBC = ar.alloc([128, 8, 256], BF16)
        Wdt = ar.alloc([128, 8, 8], BF16)
        Wo = ar.alloc([128, 4, D], BF16)
        gnB2 = ar.alloc([128, 512], F32)
        cw = ar.alloc([128, 6, 4], F32)
        cb = ar.alloc([128, 6], F32)
        dtbB = ar.alloc([128, 32], F32)
        aB = ar.alloc([128, 32], F32)
        DB = ar.alloc([128, 32], F32)
        selb = ar.alloc([128, 16], F32)
        expand = ar.alloc([128, 4, 128], F32)
        xpre = ar.alloc([128, 6, 131], F32)
        acc = ar.alloc([128, 6, 128], F32)
        seg = ar.alloc([128, 8, 128], F32)
        Mh = ar.alloc([128, 8, 128], BF16)
        extT = ar.alloc([128, 6, 16, 7], F32)
        cst = xpre[:].rearrange("p a b -> p (a b)")[:, 0:768]
        cvs = acc[:].rearrange("p a b -> p (a b)")
        xcbs = [ar.alloc([128, 6, 128], BF16) for _ in range(2)]
        xtmBs = [ar.alloc([128, 640], BF16) for _ in range(2)]
        zss = [ar.alloc([128, 512], F32) for _ in range(2)]
        sms = [ar.alloc([128, 8, 8], F32) for _ in range(2)]
        ybs = [ar.alloc([128, 512], F32) for _ in range(2)]
        tmp = ar.alloc([128, 512], F32)
        yn = ar.alloc([128, 512], BF16)
        ynT = ar.alloc([128, 4, 128], BF16)
        xw = ar.alloc([128, 512], BF16)
        ST = ar.alloc([128, 512], F32)
        STbf = ar.alloc([128, 512], BF16)
        S0nat = [ar.alloc([128, 4, 128], F32) for _ in range(4)]
        ST0bf = [ar.alloc([128, 512], BF16) for _ in range(2)]
        Snew = [ar.alloc([128, 4, 128], F32) for _ in range(2)]
        CTm = ar.alloc([128, 16, 64], BF16)
        Btmb = ar.alloc([128, 16, 128], BF16)
        edT = ar.alloc([128, 16], F32)
        edn = ar.alloc([128, 4, 16], F32)
        csq = self.csq
        ones = csq[:, 3, :]
        identF = csq[:, 0, :]
        win = self.d["ssm_w_in"]
        import os as _os
        CONV_ENG = _os.environ.get("SSD_CONV_ENG", "dve")
        POOL_ENG = _os.environ.get("SSD_POOL_ENG", "pool")

        def loads0():
            nl()
            self.load(dtbB[:], self.d["ssm_dt_bias"][0, :].partition_broadcast(128), "mwA")
            self.load(aB[:], self.d["ssm_a_log"][0, :].partition_broadcast(128), "mwB")
            self.load(DB[:], self.d["ssm_d"][0, :].partition_broadcast(128), "mwC")
            self.load(selb[0:64, :], self.d["c_selb"], "mwD")
            self.load(expand[0:8], self.d["c_expand"], "mwE")

        def compute0():
            ncmp()
            self.act(aB[:], aB[:], AF.Exp)
            self.ts(aB[:], aB[:], -1.0, None, ALU.mult)

        self.stage(loads0, compute0, la=0)

        def group_stage(g):
            xcols = [(512 * g + 128 * c) for c in range(4)] + [2048 + 128 * g, 2560 + 128 * g]
            segs = [(0, 512, 512 * g), (512, 128, 2048 + 128 * g), (640, 128, 2560 + 128 * g)]

            def loads():
                self.load_cast(Wx[:], win[:, 2048 + 512 * g:2048 + 512 * g + 512].rearrange("(k p) n -> p k n", p=128), "sx")
                self.load_cast(WBC[:, :, 0:128], win[:, 4096 + 128 * g:4096 + 128 * g + 128].rearrange("(k p) n -> p k n", p=128), "sb")
                self.load_cast(WBC[:, :, 128:256], win[:, 4608 + 128 * g:4608 + 128 * g + 128].rearrange("(k p) n -> p k n", p=128), "sc")
                self.load_cast(Wdt[:], win[:, 5120 + 8 * g:5120 + 8 * g + 8].rearrange("(k p) n -> p k n", p=128), "sd")
                self.load_cast(Wz[:], win[:, 512 * g:512 * g + 512].rearrange("(k p) n -> p k n", p=128), "sz")
                self.load_cast(Wo[:], self.d["ssm_w_out"][512 * g:512 * g + 512, :].rearrange("(k p) n -> p k n", p=128), "so")
                self.load(gnB2[:], self.d["ssm_norm"][0, 512 * g:512 * g + 512].partition_broadcast(128), "sg")
                for c in range(6):
                    S.dma("sp", cw[:, c, :], self.d["ssm_conv_w"][:, xcols[c]:xcols[c] + 128].rearrange("j p -> p j"), "scw", allow_slow_non_contiguous=True)
                    S.dma("sp", cb[:, c:c + 1], self.d["ssm_conv_b"][0, xcols[c]:xcols[c] + 128].rearrange("(p o) -> p o", o=1), "scb")

            def part1a(i, c0, r, par):
                samp = (i == NPT)
                xcb, xtmB, zs = xcbs[par], xtmBs[par], zss[par]
                hTt = lambda k: self.hT[:, k, c0:c0 + r]
                if samp:
                    for si, (a0, w_, d0) in enumerate(segs):
                        self.load(cst[0:48, a0:a0 + w_], self.d["st_conv"][:, d0:d0 + w_], "scs%d" % si)
                    pcs = self.bank()
                    for c in range(6):
                        self.tr(pcs[:, 48 * c:48 * c + 48], cst[0:48, 128 * c:128 * c + 128], identF[0:48, 0:48], inc=(c == 5))
                    self.act(extT[:, :, :, 0:3], pcs[:, 0:288].rearrange("p (c b j) -> p c b j", c=6, b=16), AF.Copy)
                px1 = self.bank()
                px1v = px1[:].rearrange("p (a b) -> p a b", a=4)
                for c in range(4):
                    self.mm(px1v[:, c, 0:r], [(Wx[:, k, 128 * c:128 * c + 128], hTt(k)) for k in range(8)], last_inc=(c == 3))
                if not samp:
                    self.act(xpre[:, 0:4, 3:3 + r], px1v[:, :, 0:r], AF.Copy)
                else:
                    self.act(extT[:, 0:4, :, 3:7], px1v[:, :, 0:64].rearrange("p c (b t) -> p c b t", t=4), AF.Copy)
                px2 = self.bank()
                px2v = px2[:, 0:256].rearrange("p (a b) -> p a b", a=2)
                for c in range(2):
                    self.mm(px2v[:, c, 0:r], [(WBC[:, k, 128 * c:128 * c + 128], hTt(k)) for k in range(8)], last_inc=(c == 1))
                if not samp:
                    self.act(xpre[:, 4:6, 3:3 + r], px2v[:, :, 0:r], AF.Copy)
                    srcv = lambda c, j: xpre[:, c, j:j + r]
                    accv = lambda c: acc[:, c, 0:r]
                else:
                    self.act(extT[:, 4:6, :, 3:7], px2v[:, :, 0:64].rearrange("p c (b t) -> p c b t", t=4), AF.Copy)
                    srcv = lambda c, j: extT[:, c, :, j:j + 4]
                    accv = lambda c: acc[:, c, 0:64].rearrange("p (b t) -> p b t", t=4)
                pz = self.bank()
                self.mm(pz[0:r, :], [(hTt(k), Wz[:, k, :]) for k in range(8)])
                for c in range(6):
                    self.ts(accv(c), srcv(c, 0), cw[:, c, 0:1], cb[:, c:c + 1], ALU.mult, ALU.add, eng=POOL_ENG)
                for j in range(1, 4):
                    for c in range(6):
                        self.stt(accv(c), srcv(c, j), cw[:, c, j:j + 1], accv(c), ALU.mult, ALU.add)
                if not samp and i < NPT - 1:
                    self.cp(xpre[:, :, 0:3], xpre[:, :, r:r + 3])
                outer = S.rec
                S.rec = []
                self.act(xcb[:, :, 0:r], acc[:, :, 0:r], AF.Silu)
                self.act(zs[0:r, :], pz[0:r, :], AF.Silu)
                grp = S.rec
                S.rec = outer
                S.rec.append(("group", grp))
                tbx = self.tbank()
                for c in range(5):
                    self.tr(tbx[0:r, 128 * c:128 * c + 128], xcb[:, c, 0:r], self.identb[:, :], inc=(c == 4))
                self.act(xtmB[0:r, :], tbx[0:r, 0:640], AF.Copy)

            def part1b(i, c0, r, par):
                samp = (i == NPT)
                TriU = csq[:, 4 if samp else 1, :]
                Neg = csq[:, 7 if samp else 6, :]
                sm = sms[par]
                dtp, dt_, lnd, dA, cl, ecum, wendc, edecB = (sm[:, j, :] for j in range(8))
                hTt = lambda k: self.hT[:, k, c0:c0 + r]
                pd = self.bank()
                self.mm(pd[0:r, 0:8], [(hTt(k), Wdt[:, k, :]) for k in range(8)])
                self.tt(dtp[0:r], pd[0:r, 0:8], dtbB[0:r, 8 * g:8 * g + 8], ALU.add)
                self.act(dtp[0:r], dtp[0:r], AF.Exp)
                self.act(dt_[0:r], dtp[0:r], AF.Ln, bias=1.0)
                self.act(lnd[0:r], dt_[0:r], AF.Ln)
                self.tt(dA[0:r], dt_[0:r], aB[0:r, 8 * g:8 * g + 8], ALU.mult)
                pcm = self.bank()
                self.mm(pcm[0:r, 0:8], [(TriU[0:r, 0:r], dA[0:r])])
                self.tt(cl[0:r], pcm[0:r, 0:8], lnd[0:r], ALU.subtract)
                self.act(ecum[0:r], pcm[0:r, 0:8], AF.Exp)
                self.tt(seg[0:r, :, 0:r], dA[0:r].unsqueeze(2).to_broadcast([r, 8, r]), TriU[0:r, 0:r].unsqueeze(1).to_broadcast([r, 8, r]), ALU.mult, eng=POOL_ENG)
                for half in range(2):
                    pcb = self.bank()
                    pcbv = pcb[:].rearrange("p (a b) -> p a b", a=4)
                    if r == 128:
                        self.mm(pcbv[:, :, 0:r], [(ones[0:r, :], seg[0:r, 4 * half:4 * half + 4, 0:r])])
                    else:
                        for hh in range(4):
                            self.mm(pcbv[:, hh, 0:r], [(ones[0:r, :], seg[0:r, 4 * half + hh, 0:r])], last_inc=(hh == 3))
                    if not samp:
                        self.act(edecB[:, 4 * half:4 * half + 4], pcbv[:, :, r - 1], AF.Exp)
                    self.tt(seg[0:r, 4 * half:4 * half + 4, 0:r], pcbv[0:r, :, 0:r],
                            cl[0:r, 4 * half:4 * half + 4].unsqueeze(2).to_broadcast([r, 4, r]), ALU.subtract)
                self.tt(seg[0:r, :, 0:r], seg[0:r, :, 0:r], Neg[0:r, 0:r].unsqueeze(1).to_broadcast([r, 8, r]), ALU.add)
                self.act(seg[0:r, :, 0:r], seg[0:r, :, 0:r], AF.Exp)
                if not samp:
                    self.cp(wendc[0:r], seg[0:r, :, r - 1])
                else:
                    tmpE = Mh[0:64].rearrange("p a b -> p (a b)").bitcast(F32).rearrange("p (h t) -> p h t", h=8)
                    self.tt(tmpE, seg[0:64, :, 0:64], csq[0:64, 9, 0:64].unsqueeze(1).to_broadcast([64, 8, 64]), ALU.mult)
                    S.op("dve", lambda e: e.tensor_reduce(out=wendc[0:64, :], in_=tmpE, axis=AX, op=ALU.add),
                         reads=[tmpE], writes=[wendc[0:64, :]])

            def part1c(i, c0, r, par):
                samp = (i == NPT)
                xcb, xtmB, yb = xcbs[par], xtmBs[par], ybs[par]
                xtm, BCT = xtmB[:, 0:512], xcb[:, 4:6, :]
                hTt = lambda k: self.hT[:, k, c0:c0 + r]
                pg = self.bank()
                self.mm(pg[0:r, 0:r], [(BCT[:, 0, 0:r], BCT[:, 1, 0:r])])
                self.tt(Mh[0:r, :, 0:r], seg[0:r, :, 0:r], pg[0:r, 0:r].unsqueeze(1).to_broadcast([r, 8, r]), ALU.mult)
                py = self.bank()
                for h in range(8):
                    self.mm(py[0:r, 64 * h:64 * h + 64], [(Mh[0:r, h, 0:r], xtm[0:r, 64 * h:64 * h + 64])], last_inc=(h == 7))
                self.act(yb[0:r, :], py[0:r, :], AF.Copy)
                if i >= NPT - 1:
                    pc1 = self.bank()
                    self.mm(pc1[0:r, :], [(hTt(k), Wx[:, k, :]) for k in range(8)])
                    pc2 = self.bank()
                    self.mm(pc2[0:r, 0:256], [(hTt(k), WBC[:, k, :]) for k in range(8)])
                    self.act(cvs[0:r, 0:512], pc1[0:r, :], AF.Copy)
                    self.act(cvs[0:r, 512:768], pc2[0:r, 0:256], AF.Copy)
                    if not samp:
                        for (a0, w_, d0) in segs:
                            self.store(self.o["conv_p"][:, d0:d0 + w_], cvs[125:128, a0:a0 + w_], "cvp")
                    else:
                        for b in range(16):
                            for (a0, w_, d0) in segs:
                                self.store(self.o["conv_s"][b, :, d0:d0 + w_], cvs[4 * b + 1:4 * b + 4, a0:a0 + w_], "cvs")

            def part2(i, c0, r, par):
                samp = (i == NPT)
                TriU = csq[:, 4 if samp else 1, :]
                xcb, xtmB, zs, sm, yb = xcbs[par], xtmBs[par], zss[par], sms[par], ybs[par]
                xtf, Btm, BCT = xtmB[:, 0:512], xtmB[:, 512:640], xcb[:, 4:6, :]
                dtp, dt_, lnd, dA, cl, ecum, wendc, edecB = (sm[:, j, :] for j in range(8))
                v8 = lambda ap: ap.rearrange("p (h q) -> p h q", h=8)
                self.tt(v8(xw[0:r, :]), v8(xtf[0:r, :]), wendc[0:r].unsqueeze(2).to_broadcast([r, 8, 64]), ALU.mult, eng=POOL_ENG)
                if not samp:
                    if i > 0:
                        pi = self.bank()
                        self.mm(pi[0:r, :], [(BCT[:, 1, 0:r], STbf[:, :])])
                        self.tt(v8(tmp[0:r, :]), v8(pi[0:r, :]), ecum[0:r].unsqueeze(2).to_broadcast([r, 8, 64]), ALU.mult)
                        self.tt(yb[0:r, :], yb[0:r, :], tmp[0:r, :], ALU.add)
                    psu = self.bank()
                    self.mm(psu[:, :], [(Btm[0:r, :], xw[0:r, :])])
                    if i == 0:
                        self.cp(ST[:], psu[:, :])
                    else:
                        self.tt(v8(ST[:]), v8(ST[:]), edecB[:].unsqueeze(2).to_broadcast([128, 8, 64]), ALU.mult)
                        self.tt(ST[:], ST[:], psu[:, :], ALU.add)
                    if i < NPT - 1:
                        self.act(STbf[:], ST[:], AF.Copy)
                    else:
                        pso = self.bank()
                        for c in range(4):
                            self.tr(pso[:, 128 * c:128 * c + 128], ST[:, 128 * c:128 * c + 128], identF, inc=(c == 3))
                        so = Snew[0]
                        self.act(so[:], pso[:].rearrange("p (c n) -> p c n", c=4), AF.Copy)
                        self.store(self.o["ssm_p"][512 * g:512 * g + 512, :].rearrange("(c p) n -> p c n", p=128), so[:], "ssmo0")
                else:
                    self.memset(CTm[:], 0.0)
                    for b in range(16):
                        self.cp(CTm[:, b, 4 * b:4 * b + 4], BCT[:, 1, 4 * b:4 * b + 4])
                    self.tt(Btmb[0:64], Btm[0:64, :].unsqueeze(1).to_broadcast([64, 16, 128]), selb[0:64, :].unsqueeze(2).to_broadcast([64, 16, 128]), ALU.mult)
                    pct = self.bank()
                    self.mm(pct[0:8, 0:64], [(dA[0:64], TriU[0:64, 0:64])])
                    self.act(edT[0:8, :], pct[0:8, 0:64].rearrange("h (b t) -> h b t", t=4)[:, :, 3], AF.Exp)
                    pen = self.bank()
                    for c in range(4):
                        self.mm(pen[:, 16 * c:16 * c + 16], [(expand[0:8, c, :], edT[0:8, :])], last_inc=(c == 3))
                    self.act(edn[:], pen[:, 0:64].rearrange("p (c b) -> p c b", c=4), AF.Copy)
                    pi = self.bank()
                    self.reserved.add(pi.name)
                    def ld_state(b_):
                        self.load(S0nat[b_ % 4][:], self.d["st_ssm"][b_, 512 * g:512 * g + 512, :].rearrange("(c p) n -> p c n", p=128), "ssn%d" % (b_ % 4))
                    for b_ in range(3):
                        ld_state(b_)
                    for b in range(16):
                        sn = S0nat[b % 4]
                        if b + 3 < 16:
                            ld_state(b + 3)
                        pst = self.bank()
                        for c in range(4):
                            self.tr(pst[:, 128 * c:128 * c + 128], sn[:, c, :], identF, inc=(c == 3))
                        sb = ST0bf[b % 2]
                        self.act(sb[:], pst[:, :], AF.Copy)
                        l1, r1 = CTm[:, b, :], sb[:, :]
                        S.op("pe", (lambda e, l1=l1, r1=r1, st_=(b == 0), sp_=(b == 15): e.matmul(pi[0:64, :], l1, r1, start=st_, stop=sp_)),
                             reads=[l1, r1], writes=[pi[0:64, :]], inc=True)
                        psn = self.bank()
                        psnv = psn[:].rearrange("p (c n) -> p c n", c=4)
                        for c in range(4):
                            self.mm(psnv[:, c, :], [(xw[0:64, 128 * c:128 * c + 128], Btmb[0:64, b, :])], last_inc=(c == 3))
                        so = Snew[b % 2]
                        for c in range(4):
                            self.stt(so[:, c, :], sn[:, c, :], edn[:, c, b:b + 1], psnv[:, c, :], ALU.mult, ALU.add)
                        self.S.dma("pool", self.o["ssm_s"][b, 512 * g:512 * g + 512, :].rearrange("(c p) n -> p c n", p=128), so[:], "ssms%d" % (b % 2))
                    self.reserved.discard(pi.name)
                    self.tt(v8(tmp[0:64, :]), v8(pi[0:64, :]), ecum[0:64].unsqueeze(2).to_broadcast([64, 8, 64]), ALU.mult)
                    self.tt(yb[0:64, :], yb[0:64, :], tmp[0:64, :], ALU.add)
                self.tt(v8(tmp[0:r, :]), v8(xtf[0:r, :]), DB[0:r, 8 * g:8 * g + 8].unsqueeze(2).to_broadcast([r, 8, 64]), ALU.mult, eng=POOL_ENG)
                self.tt(yb[0:r, :], yb[0:r, :], tmp[0:r, :], ALU.add, eng=POOL_ENG)
                self.tt(yb[0:r, :], yb[0:r, :], zs[0:r, :], ALU.mult, eng=POOL_ENG)
                ssq = self.stat[0:r, 32:33]
                rsd = self.stat[0:r, 33:34]
                self.act(self.junk[0:r, 0:512], yb[0:r, :], AF.Square, accum_out=ssq)
                self.act(rsd, ssq, AF.Ln, scale=1.0 / 512, bias=EPS)
                self.act(rsd, rsd, AF.Exp, scale=-0.5)
                self.stt(yn[0:r, :], yb[0:r, :], rsd, gnB2[0:r, :], ALU.mult, ALU.mult)
                tb = self.tbank()
                tbv = tb[:, 0:512].rearrange("p (a b) -> p a b", a=4)
                for c in range(4):
                    self.tr(tbv[:, c, 0:r], yn[0:r, 128 * c:128 * c + 128], self.identb[0:r, 0:r], inc=(c == 3))
                self.act(ynT[:, :, 0:r], tbv[:, :, 0:r], AF.Copy)
                for nb in range(2):
                    pout = self.bank()
                    self.mm(pout[0:r, :], [(ynT[:, c, 0:r], Wo[:, c, nb * 512:nb * 512 + 512]) for c in range(4)])
                    xs = self.X[0:r, i, nb * 512:nb * 512 + 512]
                    self.tt(xs, xs, pout[0:r, :], ALU.add)

            def compute():
                self.memset(xpre[:, :, 0:3], 0.0)
                prev = None
                for idx, (i, c0, r) in enumerate(self.tiles):
                    par = idx % 2
                    self.use_banks([0, 1], [0])
                    LA = self.record(lambda: part1a(i, c0, r, par))
                    self.use_banks([2], [0])
                    LB = self.record(lambda: part1b(i, c0, r, par))
                    self.use_banks([0, 1], [0])
                    LC = self.record(lambda: part1c(i, c0, r, par))
                    L1 = self.merge(LA, LB) + LC
                    if prev is None:
                        S.play(L1)
                    else:
                        self.use_banks([3, 4, 5], [1])
                        L2 = self.record(lambda: part2(*prev))
                        S.play(self.merge(L1, L2))
                    prev = (i, c0, r, par)
                self.use_banks([3, 4, 5], [1])
                L2 = self.record(lambda: part2(*prev))
                S.play(L2)
                self.use_banks(None, None)

            self.stage(loads, compute, la=0)

        for g in range(4):
            group_stage(g)


def build_nc(NPT=16, mixers=(0, 1, 2, 3), same_engine_sync=True):
    k = K(NPT, mixers, same_engine_sync)
    return k.build()


WEIGHT_NAMES = ["norm_ffn1", "ffn1_gate", "ffn1_up", "ffn1_down", "norm_mix", "norm_ffn2", "ffn2_gate", "ffn2_up",
                "ffn2_down", "norm_ple", "ple_gate", "ple_proj", "gm_w_in", "gm_w_s", "gm_b_s", "gm_w_out",
                "pool_w", "gla_w_in", "gla_w_a1", "gla_w_a2", "gla_w_out", "ssm_w_in", "ssm_conv_w", "ssm_w_out"]
ROW_NAMES = ["norm_final", "gm_ln", "pool_scale", "gla_b_a", "gla_norm", "ssm_conv_b", "ssm_dt_bias", "ssm_a_log",
             "ssm_d", "ssm_norm"]


def make_in_maps(inputs, NPT, ncores):
    f = lambda a: np.ascontiguousarray(np.asarray(a, dtype=np.float32))
    shared = {k: f(inputs[k]) for k in WEIGHT_NAMES}
    for k in ROW_NAMES:
        shared[k] = f(inputs[k]).reshape(1, -1)
    shared.update(make_consts())
    maps = []
    for c in range(ncores):
        m = dict(shared)
        m["xp"] = f(inputs["x_prompt"][c])
        m["xs"] = f(inputs["x_sample"][16 * c:16 * c + 16]).reshape(64, D)
        m["st_pool"] = f(inputs["state_pool_l1"][16 * c:16 * c + 16]).reshape(240, D)
        m["st_gla"] = f(inputs["state_gla_l2"][16 * c:16 * c + 16])
        m["st_ssm"] = f(inputs["state_ssm_l3"][16 * c:16 * c + 16]).reshape(16, 2048, 128)
        m["st_conv"] = f(inputs["state_conv_l3"][16 * c:16 * c + 16]).reshape(48, 3072)
        m["pp"] = f(inputs["p_prompt"][:, c])
        m["ps"] = f(inputs["p_sample"][:, 16 * c:16 * c + 16]).reshape(4, 64, 256)
        maps.append(m)
    return maps


def gather(results, NPT, ncores):
    TP = NPT * 128
    cat = lambda k, shp: np.concatenate([np.asarray(r[k]).reshape(shp) for r in results], axis=0)
    return (
        cat("yp", (1, TP, D)), cat("ys", (16, 4, D)), cat("chunk_v", (16, 4, D)),
        cat("pool_p", (1, 15, D)), cat("pool_s", (16, 15, D)),
        cat("gla_p", (1, 4, 128, 256)), cat("gla_s", (16, 4, 128, 256)),
        cat("ssm_p", (1, 32, 64, 128)), cat("ssm_s", (16, 32, 64, 128)),
        cat("conv_p", (1, 3, 3072)), cat("conv_s", (16, 3, 3072)),
    )


_NC_CACHE = {}


def kernel(**inputs):
    NPT = 16
    ncores = 8
    if "nc" not in _NC_CACHE:
        _NC_CACHE["nc"] = build_nc(NPT)
    nc = _NC_CACHE["nc"]
    maps = make_in_maps(inputs, NPT, ncores)
    res = run_bass_kernel_spmd(nc, maps, core_ids=list(range(ncores)))
    outs = gather(res.results, NPT, ncores)
    return tuple(np.ascontiguousarray(o, dtype=np.float32) for o in outs)
```

```python
import numpy as np
import concourse.bass as bass
import concourse.mybir as mybir
from concourse.bass_utils import run_bass_kernel_spmd

F32 = mybir.dt.float32
BF16 = mybir.dt.bfloat16
AF = mybir.ActivationFunctionType
ALU = mybir.AluOpType

D = 1024
DFF = 2816
NCH = 8
EPS = 1e-6
NEG = -30000.0


def _esize(dt):
    return 4 if dt == F32 else 2


class Rec:
    __slots__ = ("plo", "phi", "ivals", "lo", "hi", "who", "key")


class Sched:
    ENG = ["pe", "act", "dve", "pool", "sp"]

    def __init__(self, nc, same_engine_sync=True):
        self.nc = nc
        self.sem = {e: nc.alloc_semaphore("sem_" + e) for e in self.ENG}
        self.count = {e: 0 for e in self.ENG}
        self.ops = {e: [] for e in self.ENG}
        self.seen = {e: {} for e in self.ENG}
        self.tens = {}
        self.chan = {}
        self.chan_by_sem = {}
        self.rec = None
        import os as _os
        self.vclock = _os.environ.get("VCLOCK", "1") == "1"
        self.sem_eng = {self.sem[e].num: e for e in self.ENG}
        self.snaps = {e: {} for e in self.ENG}
        self.same = same_engine_sync
        self.nwaits = 0

    def region(self, ap):
        tn = type(ap.tensor).__name__
        if not (tn.startswith("SB") or tn.startswith("PSum")):
            return None
        pat = ap.ap
        es = _esize(ap.dtype)
        pstep, pcnt = pat[0]
        off = int(ap.offset)
        if pstep > 0:
            p0 = off // pstep
            f0 = off % pstep
        else:
            p0 = 0
            f0 = off
        free = [(s, c) for (s, c) in pat[1:] if c > 1 and s != 0]
        free.sort(key=lambda x: x[0])
        run = 1
        rest = []
        for (s, c) in free:
            if s == run:
                run = run * c
            else:
                rest.append((s, c))
        starts = [0]
        nrest = 1
        for (s, c) in rest:
            nrest *= c
        if nrest <= 64:
            for (s, c) in rest:
                starts = [a + s * i for a in starts for i in range(c)]
            ivals = [((f0 + a) * es, (f0 + a + run) * es) for a in starts]
        else:
            ext = run + sum(s * (c - 1) for (s, c) in rest)
            ivals = [(f0 * es, (f0 + ext) * es)]
        r = Rec()
        r.plo = p0
        r.phi = p0 + pcnt
        if tn.startswith("PSum"):
            r.plo, r.phi = 0, 128
            ivals = [(0, 2048)]
        r.ivals = ivals
        r.lo = min(a for a, b in ivals)
        r.hi = max(b for a, b in ivals)
        r.key = (ap.tensor.name, r.plo, r.phi, tuple(ivals))
        return ap.tensor.name, r

    @staticmethod
    def _overlap(a, b):
        if a.hi <= b.lo or b.hi <= a.lo or a.phi <= b.plo or b.phi <= a.plo:
            return False
        for (x0, x1) in a.ivals:
            for (y0, y1) in b.ivals:
                if x0 < y1 and y0 < x1:
                    return True
        return False

    @staticmethod
    def _covers(w, r):
        if not (w.plo <= r.plo and r.phi <= w.phi):
            return False
        for (y0, y1) in r.ivals:
            ok = False
            for (x0, x1) in w.ivals:
                if x0 <= y0 and y1 <= x1:
                    ok = True
                    break
            if not ok:
                return False
        return True

    def _collect(self, reads, writes):
        whos = []
        rr = []
        ww = []
        for ap in reads:
            x = self.region(ap)
            if x is None:
                continue
            name, r = x
            rr.append((name, r))
            t = self.tens.setdefault(name, {"w": [], "r": []})
            for rec in t["w"]:
                if self._overlap(rec, r):
                    whos.append(rec.who)
            if type(ap.tensor).__name__.startswith("PSum"):
                for rec in t["r"]:
                    whos.append(rec.who)
        for ap in writes:
            x = self.region(ap)
            if x is None:
                continue
            name, r = x
            ww.append((name, r))
            t = self.tens.setdefault(name, {"w": [], "r": []})
            for rec in t["w"]:
                if self._overlap(rec, r):
                    whos.append(rec.who)
            for rec in t["r"]:
                if self._overlap(rec, r):
                    whos.append(rec.who)
        return whos, rr, ww

    def _record(self, rr, ww, who):
        for name, r in ww:
            t = self.tens[name]
            t["w"] = [x for x in t["w"] if not self._covers(r, x)]
            t["r"] = [x for x in t["r"] if not self._covers(r, x)]
            r.who = who
            t["w"].append(r)
        for name, r in rr:
            t = self.tens[name]
            r.who = who
            done = False
            if who[0] == "e":
                for x in t["r"]:
                    if x.key == r.key and x.who[0] == "e" and x.who[1] == who[1]:
                        x.who = who
                        done = True
                        break
            if not done:
                t["r"].append(r)

    def _waits(self, eng, whos):
        seen = self.seen[eng]
        best = {}
        for w in whos:
            if w[0] == "e":
                _, e2, seq = w
                if e2 == eng:
                    if eng == "pe" or not self.same:
                        continue
                    assert seq <= self.count[eng], "same-engine dep on pending op"
                sem = self.sem[e2]
                val = seq
            else:
                _, sem, val = w
                val = max(val, self.chan_by_sem[sem.num][1])
            k = sem.num
            if k not in best or best[k][1] < val:
                best[k] = (sem, val)
        waits = []
        items = sorted(best.items(), key=lambda kv: -kv[1][1])
        for k, (sem, val) in items:
            if seen.get(k, 0) >= val:
                continue
            seen[k] = val
            waits.append((sem, val))
            if self.vclock:
                e2 = self.sem_eng.get(k)
                if e2 is not None:
                    snap = self.snaps[e2].get(val)
                    if snap is not None:
                        for kk, vv in snap.items():
                            if seen.get(kk, 0) < vv:
                                seen[kk] = vv
        self.nwaits += len(waits)
        return waits

    def play(self, items):
        for it in items:
            if it[0] == "group":
                self.play(it[1])
            elif it[0] == "op":
                self.op(*it[1:])
            else:
                self.dma(*it[1:-1], **it[-1])

    def op(self, eng, fn, reads=(), writes=(), inc=True):
        if self.rec is not None:
            self.rec.append(("op", eng, fn, list(reads), list(writes), inc))
            return
        whos, rr, ww = self._collect(reads, writes)
        waits = self._waits(eng, whos)
        seq = self.count[eng] + 1
        if inc:
            self.count[eng] = seq
            if self.vclock:
                self.snaps[eng][seq] = dict(self.seen[eng])
        self.ops[eng].append((waits, fn, (self.sem[eng], 1) if inc else None))
        self._record(rr, ww, ("e", eng, seq))

    def dma(self, eng, out, in_, chan, **kw):
        if self.rec is not None:
            self.rec.append(("dma", eng, out, in_, chan, kw))
            return
        if not chan.startswith(eng + "_"):
            chan = eng + "_" + chan
        whos, rr, ww = self._collect([in_], [out])
        waits = self._waits(eng, whos)
        if chan not in self.chan:
            self.chan[chan] = [self.nc.alloc_semaphore("ch_" + chan), 0]
            self.chan_by_sem[self.chan[chan][0].num] = self.chan[chan]
        c = self.chan[chan]
        c[1] += 16
        sem, val = c[0], c[1]
        self.ops[eng].append((waits, lambda e: e.dma_start(out=out, in_=in_, **kw), (sem, 16)))
        self._record(rr, ww, ("d", sem, val))

    def finish(self):
        nc = self.nc
        whos = [("d", c[0], c[1]) for c in self.chan.values()]
        whos += [("e", e, self.count[e]) for e in self.ENG if e != "sp" and self.count[e] > 0]
        waits = self._waits("sp", whos)
        self.ops["sp"].append((waits, None, None))

        import os as _os
        fuse = _os.environ.get("FUSE_WAIT", "1") == "1"

        def replay(e, lst):
            for waits, fn, inc in lst:
                if fn is None:
                    for (sem, val) in waits:
                        e.wait_ge(sem, val)
                    continue
                ws = list(waits)
                last = ws.pop() if (fuse and ws) else None
                for (sem, val) in ws:
                    e.wait_ge(sem, val)
                ins = fn(e)
                if last is not None:
                    ins._wait_ge(last[0], last[1])
                if inc is not None:
                    ins.then_inc(inc[0], inc[1])

        with nc.Block() as block:
            @block.tensor
            def _(e):
                replay(e, self.ops["pe"])

            @block.scalar
            def _(e):
                replay(e, self.ops["act"])

            @block.vector
            def _(e):
                replay(e, self.ops["dve"])

            @block.gpsimd
            def _(e):
                replay(e, self.ops["pool"])

            @block.sync
            def _(e):
                replay(e, self.ops["sp"])


class Arena:
    def __init__(self, t, nbytes):
        self.t = t
        self.n = nbytes
        self.off = 0

    def reset(self, off=0):
        self.off = off

    def alloc(self, shape, dt):
        es = _esize(dt)
        n = 1
        for s in shape[1:]:
            n *= s
        nb = n * es
        self.off = (self.off + 31) // 32 * 32
        assert self.off + nb <= self.n, f"arena overflow {self.off}+{nb}>{self.n}"
        v = self.t[:, self.off // 2:(self.off + nb) // 2]
        self.off += nb
        if dt == F32:
            v = v.bitcast(F32)
        if len(shape) == 3:
            v = v.rearrange("p (a b) -> p a b", a=shape[1])
        elif len(shape) == 4:
            v = v.rearrange("p (a b c) -> p a b c", a=shape[1], b=shape[2])
        if shape[0] < 128:
            v = v[0:shape[0]]
        return v


class Stage:
    def __init__(self, loads, compute, la=2):
        self.loads = loads
        self.compute = compute
        self.la = la


INPUT_SHAPES = None


def in_specs(NPT):
    TP = NPT * 128
    return {
        "xp": [TP, D], "xs": [64, D],
        "st_pool": [240, D], "st_gla": [16, 4, 128, 256], "st_ssm": [16, 2048, 128], "st_conv": [48, 3072],
        "pp": [4, TP, 256], "ps": [4, 64, 256],
        "norm_ffn1": [4, D], "ffn1_gate": [4, D, DFF], "ffn1_up": [4, D, DFF], "ffn1_down": [4, DFF, D],
        "norm_mix": [4, D], "norm_ffn2": [4, D],
        "ffn2_gate": [4, D, DFF], "ffn2_up": [4, D, DFF], "ffn2_down": [4, DFF, D],
        "norm_ple": [4, D], "ple_gate": [4, D, D], "ple_proj": [4, 256, D], "norm_final": [1, D],
        "gm_w_in": [D, 2048], "gm_ln": [1, D], "gm_w_s": [8, 128, 128], "gm_b_s": [8, 128], "gm_w_out": [D, D],
        "pool_w": [4, 256, 256], "pool_scale": [1, D],
        "gla_w_in": [D, 3072], "gla_w_a1": [D, 16], "gla_w_a2": [16, 512], "gla_b_a": [1, 512],
        "gla_norm": [1, D], "gla_w_out": [D, D],
        "ssm_w_in": [D, 5152], "ssm_conv_w": [4, 3072], "ssm_conv_b": [1, 3072], "ssm_dt_bias": [1, 32],
        "ssm_a_log": [1, 32], "ssm_d": [1, 32], "ssm_norm": [1, 2048], "ssm_w_out": [2048, D],
        "c_sq": [128, 10, 128],
        "c_pool": [128, 12, 128], "c_pools": [64, 4, 64], "c_poolh": [120, 4, 2, 64],
        "c_selb": [64, 16], "c_selbt": [128, 16, 64], "c_expand": [8, 4, 128],
    }


def out_specs(NPT):
    TP = NPT * 128
    return {
        "yp": [TP, D], "ys": [64, D], "chunk_v": [64, D], "pool_p": [15, D], "pool_s": [16, 15, D],
        "gla_p": [4, 128, 256], "gla_s": [16, 4, 128, 256], "ssm_p": [2048, 128], "ssm_s": [16, 2048, 128],
        "conv_p": [3, 3072], "conv_s": [16, 3, 3072],
    }


def make_consts():
    c = {}
    sq = np.zeros((128, 10, 128), np.float32)
    s = np.arange(128)[:, None]
    t = np.arange(128)[None, :]
    sq[:, 0] = (s == t)
    sq[:, 1] = (s <= t)
    sq[:, 2] = (s > t)
    sq[:, 3] = 1.0
    same = (s // 4 == t // 4) & (s < 64) & (t < 64)
    sq[:, 4] = same & (s <= t)
    sq[:, 5] = same & (s > t)
    sq[:, 6] = np.where(s <= t, 0.0, NEG)
    sq[:, 7] = np.where(same & (s <= t), 0.0, NEG)
    sq[:, 8] = (s >= t)
    sq[:, 9] = (t == 4 * (s // 4) + 3) & (s < 64)
    c["c_sq"] = sq
    pc = np.zeros((128, 12, 128), np.float32)
    pss = np.zeros((64, 4, 64), np.float32)
    ph = np.zeros((120, 4, 2, 64), np.float32)
    for wi, w in enumerate((2, 4, 8, 16)):
        d = t - s
        pc[:, wi * 3 + 0] = ((d >= 0) & (d < w)) / float(w) - (s == t)
        d2 = t + 128 - s
        pc[:, wi * 3 + 1] = ((d2 >= 0) & (d2 < w)) / float(w)
        cnt = np.minimum(t + 1, w).astype(np.float32)
        pc[:, wi * 3 + 2] = ((d >= 0) & (d < w)) / cnt - (s == t)
        s6 = np.arange(64)[:, None]
        t6 = np.arange(64)[None, :]
        d6 = (t6 % 4) - (s6 % 4)
        pss[:, wi] = ((s6 // 4 == t6 // 4) & (d6 >= 0) & (d6 < w)) / float(w) - (s6 == t6)
        for half in range(2):
            r = np.arange(120)[:, None]
            bb = r // 15 + half * 8
            j = r % 15
            dd = 15 + (t6 % 4) - j
            ph[:, wi, half] = ((bb == t6 // 4) & (dd < w)) / float(w)
    c["c_pool"] = pc
    c["c_pools"] = pss
    c["c_poolh"] = ph
    s6 = np.arange(64)[:, None]
    c["c_selb"] = (s6 // 4 == np.arange(16)[None, :]).astype(np.float32)
    sel = (np.arange(16)[:, None] == (np.arange(64)[None, :] // 4)).astype(np.float32)
    c["c_selbt"] = np.broadcast_to(sel[None], (128, 16, 64)).copy()
    ex = np.zeros((8, 4, 128), np.float32)
    for h in range(8):
        ex[h, h // 2, (h % 2) * 64:(h % 2) * 64 + 64] = 1.0
    c["c_expand"] = ex
    return c


class K:
    def __init__(self, NPT=16, mixers=(0, 1, 2, 3), same_engine_sync=True):
        self.NPT = NPT
        self.NT = NPT + 1
        self.T = NPT * 128 + 64
        self.mixers = mixers
        self.tiles = [(i, i * 128, 128) for i in range(NPT)] + [(NPT, NPT * 128, 64)]
        self.blocks = [(c0, min(512, NPT * 128 - c0)) for c0 in range(0, NPT * 128, 512)] + [(NPT * 128, 64)]
        nc = bass.Bass("TRN2", target_bir_lowering=False)
        self.nc = nc
        self.d = {k: nc.dram_tensor(k, v, F32, kind="ExternalInput").ap() for k, v in in_specs(NPT).items()}
        self.o = {k: nc.dram_tensor(k, v, F32, kind="ExternalOutput").ap() for k, v in out_specs(NPT).items()}
        self.S = Sched(nc, same_engine_sync)
        self.stages = []
        self.reserved = set()
        self._bset = None
        self._tbset = None
        self._bctr = {}
        self.phase_start = 0
        self._bank = 0
        self._tbank = 0
        self._gb = 0
        self._uid = 0

    def use_banks(self, idx, tidx):
        self._bset = idx
        self._tbset = tidx

    def bank(self):
        lst = self.banks if self._bset is None else [self.banks[j] for j in self._bset]
        key = "all" if self._bset is None else tuple(self._bset)
        while True:
            n = self._bctr.get(key, 0)
            self._bctr[key] = n + 1
            b = lst[n % len(lst)]
            if b.name not in self.reserved:
                return b

    def tbank(self):
        lst = self.tbanks if self._tbset is None else [self.tbanks[j] for j in self._tbset]
        key = "tall" if self._tbset is None else ("t",) + tuple(self._tbset)
        n = self._bctr.get(key, 0)
        self._bctr[key] = n + 1
        return lst[n % len(lst)]

    def record(self, f):
        assert self.S.rec is None
        self.S.rec = []
        f()
        lst = self.S.rec
        self.S.rec = None
        return lst

    @staticmethod
    def merge(a, b):
        out = []
        na, nb = len(a), len(b)
        ia = ib = 0
        while ia < na or ib < nb:
            if ib >= nb or (ia < na and ia * nb <= ib * na):
                out.append(a[ia])
                ia += 1
            else:
                out.append(b[ib])
                ib += 1
        return out

    def mm(self, ps, pairs, last_inc=True):
        n = len(pairs)
        for i, (l, r) in enumerate(pairs):
            self.S.op("pe", (lambda e, l=l, r=r, st=(i == 0), sp=(i == n - 1): e.matmul(ps, l, r, start=st, stop=sp)),
                      reads=[l, r], writes=[ps], inc=(i == n - 1) and last_inc)

    def tr(self, ps, in_, ident, inc):
        if in_.dtype == F32:
            self.mm(ps, [(in_, ident)], last_inc=inc)
            return
        self.S.op("pe", lambda e: e.transpose(out=ps, in_=in_, identity=ident), reads=[in_, ident], writes=[ps], inc=inc)

    def act(self, out, in_, func, reads=None, **kw):
        rd = [in_] + [v for v in kw.values() if hasattr(v, "ap")]
        wr = [out]
        if "accum_out" in kw:
            wr.append(kw["accum_out"])
            rd = [in_] + [v for k, v in kw.items() if hasattr(v, "ap") and k != "accum_out"]
        self.S.op("act", lambda e: e.activation(out=out, in_=in_, func=func, **kw), reads=rd, writes=wr)

    def tt(self, out, in0, in1, op, eng="dve"):
        self.S.op(eng, lambda e: e.tensor_tensor(out=out, in0=in0, in1=in1, op=op), reads=[in0, in1], writes=[out])

    def ts(self, out, in0, s1, s2, op0, op1=None, eng="dve"):
        rd = [in0] + [x for x in (s1, s2) if hasattr(x, "ap")]
        if op1 is None:
            self.S.op(eng, lambda e: e.tensor_scalar(out=out, in0=in0, scalar1=s1, scalar2=None, op0=op0), reads=rd, writes=[out])
        else:
            self.S.op(eng, lambda e: e.tensor_scalar(out=out, in0=in0, scalar1=s1, scalar2=s2, op0=op0, op1=op1), reads=rd, writes=[out])

    def stt(self, out, in0, scalar, in1, op0, op1, eng="dve"):
        rd = [in0, in1] + ([scalar] if hasattr(scalar, "ap") else [])
        self.S.op(eng, lambda e: e.scalar_tensor_tensor(out=out, in0=in0, scalar=scalar, in1=in1, op0=op0, op1=op1), reads=rd, writes=[out])

    def cp(self, out, in_, eng="dve"):
        self.S.op(eng, lambda e: e.tensor_copy(out=out, in_=in_), reads=[in_], writes=[out])

    def memset(self, out, val, eng="dve"):
        self.S.op(eng, lambda e: e.memset(out, val), reads=[], writes=[out])

    def uid(self, p):
        self._uid += 1
        return f"{p}{self._uid}"

    def load_cast(self, out, in_, chan):
        self.S.dma("pool", out, in_, chan)

    def load(self, out, in_, chan):
        self.S.dma("sp", out, in_, chan)

    def store(self, out, in_, chan):
        self.S.dma("sp", out, in_, chan)

    def build(self):
        nc = self.nc
        NT = self.NT
        T = self.T
        ARENA = 93184
        with (
            nc.sbuf_tensor("X", [128, NT, D], F32) as X,
            nc.sbuf_tensor("hT", [128, NCH, T], BF16) as hT,
            nc.sbuf_tensor("arena", [128, ARENA // 2], BF16) as arena,
            nc.sbuf_tensor("csq", [128, 10, 128], F32) as csq,
            nc.sbuf_tensor("identb", [128, 128], BF16) as identb,
            nc.sbuf_tensor("gB", [128, 1, D], F32) as gB,
            nc.sbuf_tensor("hn", [128, 2, D], BF16) as hn,
            nc.sbuf_tensor("junk", [128, D], BF16) as junk,
            nc.sbuf_tensor("stat", [128, 64], F32) as stat,
            nc.psum_tensor("B0", [128, 512], F32) as B0,
            nc.psum_tensor("B1", [128, 512], F32) as B1,
            nc.psum_tensor("B2", [128, 512], F32) as B2,
            nc.psum_tensor("B3", [128, 512], F32) as B3,
            nc.psum_tensor("B4", [128, 512], F32) as B4,
            nc.psum_tensor("B5", [128, 512], F32) as B5,
            nc.psum_tensor("T0", [128, 1024], BF16) as T0,
            nc.psum_tensor("T1", [128, 1024], BF16) as T1,
        ):
            self.X, self.hT, self.csq, self.identb, self.gB, self.hn, self.junk, self.stat = X, hT, csq, identb, gB, hn, junk, stat
            self.banks = [B0, B1, B2, B3, B4, B5]
            self.tbanks = [T0, T1]
            self.ar = Arena(arena, ARENA)
            self.prologue()
            for l in range(4):
                self.ffn(l, 1)
                self.mixer(l)
                self.ffn(l, 2)
                self.ple(l)
            self.final()
            self.emit()
            self.S.finish()
        return nc

    def emit(self):
        st = self.stages
        n = len(st)
        at = {}
        for i, s in enumerate(st):
            at.setdefault(max(0, i - s.la), []).append(i)
        for j in range(n):
            for i in at.get(j, []):
                if st[i].loads is not None:
                    st[i].loads()
            st[j].compute()

    def stage(self, loads, compute, la=2):
        i = len(self.stages)
        la = max(0, min(la, i - self.phase_start))
        self.stages.append(Stage(loads, compute, la))

    def new_phase(self):
        self.phase_start = len(self.stages)

    def prologue(self):
        def loads():
            self.load(self.csq[:], self.d["c_sq"][:], "csq")
            self.load_cast(self.identb[:], self.d["c_sq"][:, 0, :], "identb")
            for (i, c0, r) in self.tiles:
                src = self.d["xp"][c0:c0 + r, :] if i < self.NPT else self.d["xs"][:, :]
                self.load(self.X[0:r, i, :], src, "X%d" % (i * 4 // self.NT))

        self.stage(loads, lambda: None, la=0)

    def make_hT(self, grow, extra=None):
        slot = 0
        gBs = self.gB[:, slot, :]

        def loads():
            self.load(gBs, grow.partition_broadcast(128), "gB%d" % slot)

        def compute(after_tile=None):
            X, hT, stat = self.X, self.hT, self.stat

            def stA(i, c0, r):
                hs = self.hn[0:r, i % 2, :]
                ss = stat[0:r, (i % 4) * 2:(i % 4) * 2 + 1]
                rs = stat[0:r, (i % 4) * 2 + 1:(i % 4) * 2 + 2]
                self.act(self.junk[0:r, :], X[0:r, i, :], AF.Square, accum_out=ss)
                self.act(rs, ss, AF.Ln, scale=1.0 / D, bias=EPS)
                self.act(rs, rs, AF.Exp, scale=-0.5)
                self.stt(hs, X[0:r, i, :], rs, gBs[0:r, :], ALU.mult, ALU.mult)
                if extra is not None:
                    extra(i, c0, r, rs, gBs)

            def stB(i, c0, r):
                hs = self.hn[0:r, i % 2, :]
                tb = self.tbank()
                tbv = tb[:].rearrange("p (a b) -> p a b", a=8)
                for k in range(8):
                    self.tr(tbv[:, k, 0:r], hs[:, 128 * k:128 * k + 128], self.identb[0:r, 0:r], inc=(k == 7))
                self.cp(hT[:, :, c0:c0 + r], tbv[:, :, 0:r])

            n = len(self.tiles)
            for idx in range(n + 1):
                if idx < n:
                    stA(*self.tiles[idx])
                if idx >= 1:
                    stB(*self.tiles[idx - 1])
                    if after_tile is not None:
                        after_tile(idx - 1)

        return loads, compute

    def ffn(self, l, which):
        S = self.S
        Wg = self.d["ffn%d_gate" % which][l]
        Wu = self.d["ffn%d_up" % which][l]
        Wd = self.d["ffn%d_down" % which][l]
        self.new_phase()
        nl, ncmp = self.make_hT(self.d["norm_ffn%d" % which][l:l + 1, :])
        ar = self.ar
        ar.reset()
        NSLOT = 4
        gu = [ar.alloc([128, 2, 8, 128], BF16) for _ in range(NSLOT)]
        wd = ar.alloc([128, 6, D], BF16)
        aT = ar.alloc([128, 6, self.T], BF16)
        sil = [ar.alloc([128, 512], F32) for _ in range(2)]
        quarters = [list(range(0, 6)), list(range(6, 11)), list(range(11, 17)), list(range(17, 22))]
        first = [True]
        cnt = [0]

        def gu_stage(j, jj, la, first_hook=None):
            slot = cnt[0] % NSLOT
            cnt[0] += 1
            g = gu[slot]

            def loads():
                self.load_cast(g[:, 0], Wg[:, 128 * j:128 * j + 128].rearrange("(k p) n -> p k n", p=128), "gug%d" % slot)
                self.load_cast(g[:, 1], Wu[:, 128 * j:128 * j + 128].rearrange("(k p) n -> p k n", p=128), "guu%d" % slot)

            def block(bi):
                c0, bs = self.blocks[bi]
                pg = self.bank()
                pu = self.bank()
                self.mm(pg[:, 0:bs], [(g[:, 0, k, :], self.hT[:, k, c0:c0 + bs]) for k in range(8)])
                self.mm(pu[:, 0:bs], [(g[:, 1, k, :], self.hT[:, k, c0:c0 + bs]) for k in range(8)])
                st = sil[bi % 2]
                self.act(st[:, 0:bs], pg[:, 0:bs], AF.Silu)
                self.tt(aT[:, jj, c0:c0 + bs], st[:, 0:bs], pu[:, 0:bs], ALU.mult)

            def compute():
                for bi in range(len(self.blocks)):
                    block(bi)

            if first_hook is not None:
                first_hook.append(block)
                self.stage(loads, lambda: None, la)
            else:
                self.stage(loads, compute, la)

        def down_stage(q):
            nq = len(q)

            def loads():
                self.load_cast(wd[:, 0:nq, :], Wd[128 * q[0]:128 * (q[-1] + 1), :].rearrange("(j p) n -> p j n", p=128), "wd")

            def compute():
                for (i, c0, r) in self.tiles:
                    for nb in range(2):
                        ps = self.bank()
                        self.mm(ps[0:r, :], [(aT[:, jj, c0:c0 + r], wd[:, jj, nb * 512:nb * 512 + 512]) for jj in range(nq)])
                        xs = self.X[0:r, i, nb * 512:nb * 512 + 512]
                        self.stt(xs, ps[0:r, :], 0.5, xs, ALU.mult, ALU.add)

            self.stage(loads, compute, 2)

        hook = []
        last_tile = {}
        for bi, (bc0, bs) in enumerate(self.blocks):
            for (ti, tc0, tr_) in self.tiles:
                if bc0 <= tc0 < bc0 + bs:
                    last_tile[bi] = ti

        def after_tile(ti):
            for bi, lt in last_tile.items():
                if lt == ti:
                    hook[0](bi)

        self.stage(nl, lambda: ncmp(after_tile=after_tile), la=0)
        for qi, q in enumerate(quarters):
            for jj, j in enumerate(q):
                gu_stage(j, jj, 2, first_hook=(hook if (qi == 0 and jj == 0) else None))
            down_stage(q)

    def ple(self, l):
        ar = self.ar
        self.new_phase()
        nl, ncmp = self.make_hT(self.d["norm_ple"][l:l + 1, :])
        ar.reset()
        Wpg = ar.alloc([128, 8, D], BF16)
        Wpp = ar.alloc([128, 2, D], BF16)
        ptok = ar.alloc([128, self.NT, 256], BF16)
        pT = ar.alloc([128, 2, self.T], BF16)
        sig = [ar.alloc([128, 512], F32) for _ in range(2)]
        tmp = [ar.alloc([128, 512], F32) for _ in range(2)]
        NPT = self.NPT

        def loads():
            nl()
            self.load_cast(ptok[:, 0:NPT, :], self.d["pp"][l].rearrange("(i p) c -> p i c", p=128), "ptok")
            self.load_cast(ptok[0:64, NPT, :], self.d["ps"][l], "ptoks")
            self.load_cast(Wpp[:], self.d["ple_proj"][l].rearrange("(k p) n -> p k n", p=128), "Wpp")
            self.load_cast(Wpg[:], self.d["ple_gate"][l].rearrange("(k p) n -> p k n", p=128), "Wpg")

        def compute():
            for (i, c0, r) in self.tiles:
                tb = self.tbank()
                for k in range(2):
                    self.tr(tb[:, k * 128:k * 128 + r], ptok[0:r, i, 128 * k:128 * k + 128], self.identb[0:r, 0:r], inc=(k == 1))
                self.act(pT[:, :, c0:c0 + r], tb[:, 0:256].rearrange("p (a b) -> p a b", a=2)[:, :, 0:r], AF.Copy)

            def tile_part(ti):
                (i, c0, r) = self.tiles[ti]
                for nb in range(2):
                    pg = self.bank()
                    pp_ = self.bank()
                    self.mm(pg[0:r, :], [(self.hT[:, k, c0:c0 + r], Wpg[:, k, nb * 512:nb * 512 + 512]) for k in range(8)])
                    self.mm(pp_[0:r, :], [(pT[:, k, c0:c0 + r], Wpp[:, k, nb * 512:nb * 512 + 512]) for k in range(2)])
                    sg = sig[nb]
                    tm = tmp[nb]
                    self.act(sg[0:r, :], pg[0:r, :], AF.Sigmoid)
                    self.tt(tm[0:r, :], sg[0:r, :], pp_[0:r, :], ALU.mult)
                    xs = self.X[0:r, i, nb * 512:nb * 512 + 512]
                    self.tt(xs, xs, tm[0:r, :], ALU.add)

            ncmp()
            for ti in range(len(self.tiles)):
                tile_part(ti)

        self.stage(loads, compute, la=0)

    def final(self):
        ar = self.ar
        self.new_phase()
        ar.reset()
        yb = [ar.alloc([128, D], F32) for _ in range(2)]
        slot = 0
        gBs = self.gB[:, slot, :]

        def loads():
            self.load(gBs, self.d["norm_final"][0:1, :].partition_broadcast(128), "gB%d" % slot)

        def compute():
            X, stat = self.X, self.stat
            for (i, c0, r) in self.tiles:
                ss = stat[0:r, (i % 4) * 2:(i % 4) * 2 + 1]
                rs = stat[0:r, (i % 4) * 2 + 1:(i % 4) * 2 + 2]
                self.act(self.junk[0:r, :], X[0:r, i, :], AF.Square, accum_out=ss)
                self.act(rs, ss, AF.Ln, scale=1.0 / D, bias=EPS)
                self.act(rs, rs, AF.Exp, scale=-0.5)
                y = yb[i % 2]
                self.stt(y[0:r, :], X[0:r, i, :], rs, gBs[0:r, :], ALU.mult, ALU.mult)
                dst = self.o["yp"][c0:c0 + r, :] if i < self.NPT else self.o["ys"][:, :]
                self.store(dst, y[0:r, :], "yb%d" % (i % 2))

        self.stage(loads, compute, la=0)

    def mixer(self, l):
        if l not in self.mixers:
            return
        [self.mix_gmlp, self.mix_pool, self.mix_gla, self.mix_ssd][l](l)

    def mix_gmlp(self, l):
        ar = self.ar
        self.new_phase()
        nl, ncmp = self.make_hT(self.d["norm_mix"][l:l + 1, :])
        ar.reset()
        Wu = ar.alloc([128, 8, D], BF16)
        Wv = ar.alloc([128, 8, D], BF16)
        Wo = ar.alloc([128, 8, D], BF16)
        wsbf = ar.alloc([128, 8, 128], BF16)
        WsT = ar.alloc([128, 8, 128], BF16)
        WsS = ar.alloc([128, 8, 64], BF16)
        bsB = ar.alloc([128, 8, 128], F32)
        bsS = ar.alloc([128, 8, 64], F32)
        gln = ar.alloc([128, D], F32)
        mark = ar.off
        wsnat = ar.alloc([128, 8, 128], F32)
        ar.reset(mark)
        uTs = [ar.alloc([128, 8, 128], F32) for _ in range(2)]
        vt = ar.alloc([128, D], F32)
        vtmp = ar.alloc([128, D], F32)
        vbfs = [ar.alloc([128, D], BF16) for _ in range(2)]
        gT = ar.alloc([128, 8, 128], BF16)
        svt = ar.alloc([128, 4, 128], F32)
        st = self.stat
        gw = self.d["gm_w_in"]

        def loads():
            nl()
            self.load_cast(Wu[:], gw[:, 0:D].rearrange("(k p) n -> p k n", p=128), "mwA")
            self.load_cast(Wv[:], gw[:, D:2 * D].rearrange("(k p) n -> p k n", p=128), "mwB")
            self.load_cast(Wo[:], self.d["gm_w_out"].rearrange("(k p) n -> p k n", p=128), "mwC")
            self.load(wsnat[:], self.d["gm_w_s"].rearrange("g t s -> t g s"), "mwD")
            self.load(bsB[:].rearrange("p a b -> p (a b)"), self.d["gm_b_s"].rearrange("g t -> (g t)").partition_broadcast(128), "mwE")
            self.load(gln[:], self.d["gm_ln"][0, :].partition_broadcast(128), "mwF")

        def compute():
            ncmp()
            self.tt(wsbf[:], wsnat[:], self.csq[:, 8, :].unsqueeze(1).to_broadcast([128, 8, 128]), ALU.mult)
            tb = self.tbank()
            tbv = tb[:].rearrange("p (a b) -> p a b", a=8)
            for g in range(8):
                self.tr(tbv[:, g, :], wsbf[:, g, :], self.identb[:], inc=(g == 7))
            self.act(WsT[:], tbv[:], AF.Copy)
            self.memset(WsS[:], 0.0)
            for b in range(16):
                self.S.dma("sp", WsS[4 * b:4 * b + 4, :, 4 * b:4 * b + 4], WsT[0:4, :, 0:4], "wss")
            self.cp(bsS[:].rearrange("p g (b t) -> p g b t", b=16), bsB[:, :, 0:4].unsqueeze(2).to_broadcast([128, 8, 16, 4]))
            def part1(i, c0, r, par):
                samp = (i == self.NPT)
                uT, vbf = uTs[par], vbfs[par]
                for half in range(2):
                    pu = self.bank()
                    puv = pu[:].rearrange("p (a b) -> p a b", a=4)
                    for mm_ in range(4):
                        m = half * 4 + mm_
                        self.mm(puv[:, mm_, 0:r], [(Wu[:, k, 128 * m:128 * m + 128], self.hT[:, k, c0:c0 + r]) for k in range(8)], last_inc=(mm_ == 3))
                    self.act(uT[:, half * 4:half * 4 + 4, 0:r], puv[:, :, 0:r], AF.Gelu_apprx_tanh)
                for nb in range(2):
                    pv = self.bank()
                    self.mm(pv[0:r, :], [(self.hT[:, k, c0:c0 + r], Wv[:, k, nb * 512:nb * 512 + 512]) for k in range(8)])
                    self.act(vt[0:r, nb * 512:nb * 512 + 512], pv[0:r, :], AF.Gelu_apprx_tanh, accum_out=st[0:r, 16 + nb:17 + nb])
                self.act(self.junk[0:r, :], vt[0:r, :], AF.Square, accum_out=st[0:r, 18:19])
                self.tt(st[0:r, 19:20], st[0:r, 16:17], st[0:r, 17:18], ALU.add)
                self.ts(st[0:r, 20:21], st[0:r, 19:20], 1.0 / D, None, ALU.mult)
                self.tt(st[0:r, 21:22], st[0:r, 20:21], st[0:r, 20:21], ALU.mult)
                self.stt(st[0:r, 22:23], st[0:r, 18:19], 1.0 / D, st[0:r, 21:22], ALU.mult, ALU.subtract)
                self.act(st[0:r, 23:24], st[0:r, 22:23], AF.Ln, bias=EPS)
                self.act(st[0:r, 23:24], st[0:r, 23:24], AF.Exp, scale=-0.5)
                self.ts(vtmp[0:r, :], vt[0:r, :], st[0:r, 20:21], st[0:r, 23:24], ALU.subtract, ALU.mult)
                if samp:
                    self.tt(vt[0:r, :], vtmp[0:r, :], gln[0:r, :], ALU.mult)
                    self.store(self.o["chunk_v"][:, :], vt[0:r, :], "cv")
                    self.cp(vbf[0:r, :], vt[0:r, :])
                else:
                    self.tt(vbf[0:r, :], vtmp[0:r, :], gln[0:r, :], ALU.mult)

            def part2(i, c0, r, par):
                samp = (i == self.NPT)
                uT, vbf = uTs[par], vbfs[par]
                Wmix = WsS if samp else WsT
                bias = bsS if samp else bsB
                for half in range(2):
                    psv = self.bank()
                    pv4 = psv[:].rearrange("p (a b) -> p a b", a=4)
                    for gg in range(4):
                        g = half * 4 + gg
                        self.mm(pv4[:, gg, 0:r], [(vbf[0:r, 128 * g:128 * g + 128], Wmix[0:r, g, 0:r])], last_inc=(gg == 3))
                    self.tt(svt[:, :, 0:r], pv4[:, :, 0:r], bias[:, half * 4:half * 4 + 4, 0:r], ALU.add)
                    self.tt(gT[:, half * 4:half * 4 + 4, 0:r], svt[:, :, 0:r], uT[:, half * 4:half * 4 + 4, 0:r], ALU.mult)
                for nb in range(2):
                    po = self.bank()
                    self.mm(po[0:r, :], [(gT[:, m, 0:r], Wo[:, m, nb * 512:nb * 512 + 512]) for m in range(8)])
                    xs = self.X[0:r, i, nb * 512:nb * 512 + 512]
                    self.tt(xs, xs, po[0:r, :], ALU.add)

            prev = None
            for idx, (i, c0, r) in enumerate(self.tiles):
                par = idx % 2
                self.use_banks([0, 1, 2], [0])
                L1 = self.record(lambda: part1(i, c0, r, par))
                if prev is None:
                    self.S.play(L1)
                else:
                    self.use_banks([3, 4, 5], [1])
                    L2 = self.record(lambda: part2(*prev))
                    self.S.play(self.merge(L1, L2))
                prev = (i, c0, r, par)
            self.use_banks([3, 4, 5], [1])
            L2 = self.record(lambda: part2(*prev))
            self.S.play(L2)
            self.use_banks(None, None)

        self.stage(loads, compute, la=0)

    def mix_pool(self, l):
        ar = self.ar
        NPT = self.NPT
        self.new_phase()
        ar.reset()
        HN = ar.alloc([128, self.NT, D], BF16)
        Wp = ar.alloc([128, 4, 2, 256], BF16)
        PC = ar.alloc([128, 12, 128], BF16)
        PS = ar.alloc([128, 4, 64], BF16)
        PH = ar.alloc([128, 4, 2, 64], BF16)
        HB = ar.alloc([128, 2, D], BF16)
        scB = ar.alloc([128, D], F32)
        hf = ar.alloc([128, 2, D], F32)
        diffT = [ar.alloc([128, 8, 128], BF16) for _ in range(2)]
        tmp = [ar.alloc([128, 512], F32) for _ in range(2)]

        def extra(i, c0, r, rs, gBs):
            self.cp(HN[0:r, i, :], self.hn[0:r, i % 2, :], eng="pool")
            if i >= NPT - 1:
                self.stt(hf[0:r, i - (NPT - 1), :], self.X[0:r, i, :], rs, gBs[0:r, :], ALU.mult, ALU.mult)

        nl, ncmp = self.make_hT(self.d["norm_mix"][l:l + 1, :], extra=extra)

        def loads():
            nl()
            self.load_cast(Wp[:], self.d["pool_w"].rearrange("g (cc p) d -> p g cc d", p=128), "mwA")
            self.load_cast(PC[:], self.d["c_pool"], "mwB")
            self.load_cast(PS[0:64], self.d["c_pools"], "mwC")
            self.load_cast(PH[0:120], self.d["c_poolh"], "mwD")
            self.load_cast(HB[0:120], self.d["st_pool"].rearrange("(h r) d -> r h d", h=2), "mwE")
            self.load(scB[:], self.d["pool_scale"][0, :].partition_broadcast(128), "mwF")
            self.S.dma("sp", self.o["pool_s"][:, 0:11, :], self.d["st_pool"].rearrange("(b j) d -> b j d", j=15)[:, 4:15, :], "poolhist")

        def compute():
            ncmp()
            self.store(self.o["pool_p"][:, :], hf[113:128, 0, :], "poolp")
            for b in range(16):
                self.store(self.o["pool_s"][b, 11:15, :], hf[4 * b:4 * b + 4, 1, :], "pools")
            for (i, c0, r) in self.tiles:
                samp = (i == NPT)
                dT = diffT[i % 2]
                for half in range(2):
                    pb = self.bank()
                    pbv = pb[:].rearrange("p (a b) -> p a b", a=4)
                    for mm_ in range(4):
                        m = half * 4 + mm_
                        w = m // 2
                        fs = slice(128 * m, 128 * m + 128)
                        if samp:
                            pairs = [(HN[0:64, i, fs], PS[0:64, w, :]), (HB[0:120, 0, fs], PH[0:120, w, 0, :]), (HB[0:120, 1, fs], PH[0:120, w, 1, :])]
                        else:
                            pairs = [(HN[0:r, i, fs], PC[0:r, w * 3 + (2 if i == 0 else 0), 0:r])]
                            if i > 0:
                                pairs.append((HN[64:128, i - 1, fs], PC[64:128, w * 3 + 1, 0:r]))
                        self.mm(pbv[:, mm_, 0:r], pairs, last_inc=(mm_ == 3))
                    self.act(dT[:, half * 4:half * 4 + 4, 0:r], pbv[:, :, 0:r], AF.Copy)
                for nb in range(2):
                    po = self.bank()
                    for gg in range(2):
                        g = nb * 2 + gg
                        self.mm(po[0:r, gg * 256:gg * 256 + 256], [(dT[:, 2 * g + cc, 0:r], Wp[:, g, cc, :]) for cc in range(2)], last_inc=(gg == 1))
                    tm = tmp[nb]
                    self.tt(tm[0:r, :], po[0:r, :], scB[0:r, nb * 512:nb * 512 + 512], ALU.mult)
                    xs = self.X[0:r, i, nb * 512:nb * 512 + 512]
                    self.tt(xs, xs, tm[0:r, :], ALU.add)

        self.stage(loads, compute, la=0)

    def mix_gla(self, l):
        ar = self.ar
        NPT = self.NPT
        T = self.T
        self.new_phase()
        ar.reset()
        nl, ncmp = self.make_hT(self.d["norm_mix"][l:l + 1, :])
        Wa1 = ar.alloc([128, 8, 16], BF16)
        Wa2 = ar.alloc([128, 512], BF16)
        baB = ar.alloc([128, 512], F32)
        gnT = ar.alloc([128, 8], F32)
        gnB = ar.alloc([128, 8, 128], F32)
        selb = ar.alloc([128, 16], F32)
        t1 = ar.alloc([128, T], BF16)
        Ws = [dict(q=ar.alloc([128, 8, 128], BF16), k=ar.alloc([128, 8, 128], BF16), v=ar.alloc([128, 8, 256], BF16),
                   r=ar.alloc([128, 8, 256], BF16), o=ar.alloc([128, 2, D], BF16)) for _ in range(2)]
        qds = [ar.alloc([128, 128], BF16) for _ in range(2)]
        ki = ar.alloc([128, 128], BF16)
        rss = [ar.alloc([128, 2, 128], F32) for _ in range(2)]
        vbfs = [ar.alloc([128, 256], BF16) for _ in range(2)]
        zb = ar.alloc([128, 128], F32)
        lp = ar.alloc([128, 128], F32)
        ebs = [ar.alloc([128, 128], F32) for _ in range(2)]
        einv = ar.alloc([128, 128], F32)
        ee = ar.alloc([128, 128], F32)
        kends = [ar.alloc([128, 128], BF16) for _ in range(2)]
        scs = [ar.alloc([128, 128], BF16) for _ in range(2)]
        oT = ar.alloc([128, 2, 128], F32)
        sq = ar.alloc([128, 2, 128], F32)
        rstdB = ar.alloc([128, 128], F32)
        gT = ar.alloc([128, 2, 128], BF16)
        Sst = ar.alloc([128, 256], F32)
        Sbf = ar.alloc([128, 256], BF16)
        S0 = ar.alloc([128, 16, 256], F32)
        S0bf = [ar.alloc([128, 256], BF16) for _ in range(4)]
        Snew = [ar.alloc([128, 256], F32) for _ in range(4)]
        Vexp = ar.alloc([128, 16, 256], BF16)
        csq = self.csq
        ones = csq[:, 3, :]
        win = self.d["gla_w_in"]

        def loads0():
            nl()
            self.load_cast(Wa1[:], self.d["gla_w_a1"].rearrange("(k p) n -> p k n", p=128), "mwA")
            self.load_cast(Wa2[0:16, :], self.d["gla_w_a2"], "mwB")
            self.load(baB[:], self.d["gla_b_a"][0, :].partition_broadcast(128), "mwC")
            self.S.dma("sp", gnT[:], self.d["gla_norm"][0, :].rearrange("(m p) -> p m", p=128), "mwD", allow_slow_non_contiguous=True)
            self.load(selb[0:64, :], self.d["c_selb"], "mwE")

        def compute0():
            ncmp()
            for (c0, bs) in self.blocks:
                pt = self.bank()
                self.mm(pt[0:16, 0:bs], [(Wa1[:, k, :], self.hT[:, k, c0:c0 + bs]) for k in range(8)])
                self.act(t1[0:16, c0:c0 + bs], pt[0:16, 0:bs], AF.Copy)
            self.cp(gnB[:], gnT[:].unsqueeze(2).to_broadcast([128, 8, 128]))

        self.stage(loads0, compute0, la=0)

        def head_stage(hd):
            W = Ws[hd % 2]

            def loads():
                self.load_cast(W["q"][:], win[:, hd * 128:hd * 128 + 128].rearrange("(k p) n -> p k n", p=128), "gq%d" % (hd % 2))
                self.load_cast(W["k"][:], win[:, 512 + hd * 128:512 + hd * 128 + 128].rearrange("(k p) n -> p k n", p=128), "gk%d" % (hd % 2))
                self.load_cast(W["v"][:], win[:, 1024 + hd * 256:1024 + hd * 256 + 256].rearrange("(k p) n -> p k n", p=128), "gv%d" % (hd % 2))
                self.load_cast(W["r"][:], win[:, 2048 + hd * 256:2048 + hd * 256 + 256].rearrange("(k p) n -> p k n", p=128), "gr%d" % (hd % 2))
                self.load_cast(W["o"][:], self.d["gla_w_out"][hd * 256:hd * 256 + 256, :].rearrange("(k p) n -> p k n", p=128), "go%d" % (hd % 2))

            def part1(i, c0, r, par):
                samp = (i == NPT)
                TriU = csq[:, 4 if samp else 1, :]
                TriSL = csq[:, 5 if samp else 2, :]
                qd, sc, kend, vbf, rs, eb = qds[par], scs[par], kends[par], vbfs[par], rss[par], ebs[par]
                hTt = lambda k: self.hT[:, k, c0:c0 + r]
                st8 = {}

                def pa():
                    pq = self.bank()
                    self.mm(pq[:, 0:r], [(W["q"][:, k, :], hTt(k)) for k in range(8)])
                    self.mm(pq[:, 128:128 + r], [(W["k"][:, k, :], hTt(k)) for k in range(8)])
                    pv = self.bank()
                    self.mm(pv[0:r, 0:256], [(hTt(k), W["v"][:, k, :]) for k in range(8)])
                    self.mm(pv[0:r, 256:384], [(hTt(k), W["k"][:, k, :]) for k in range(8)])
                    self.act(vbf[0:r, :], pv[0:r, 0:256], AF.Copy)
                    st8["pq"], st8["pv"] = pq, pv

                def pb():
                    pz = self.bank()
                    self.mm(pz[0:r, 0:128], [(t1[0:16, c0:c0 + r], Wa2[0:16, hd * 128:hd * 128 + 128])])
                    self.tt(zb[0:r, :], pz[0:r, 0:128], baB[0:r, hd * 128:hd * 128 + 128], ALU.add)
                    self.act(zb[0:r, :], zb[0:r, :], AF.Exp, scale=-1.0)
                    self.act(lp[0:r, :], zb[0:r, :], AF.Ln, bias=1.0)
                    pc = self.bank()
                    self.mm(pc[:, 0:r], [(lp[0:r, :], TriU[0:r, 0:r])])
                    self.mm(pc[0:r, 128:256], [(TriSL[0:r, 0:r], lp[0:r, :])])
                    self.act(eb[:, 0:r], pc[:, 0:r], AF.Exp, scale=-1.0 / 16)
                    self.act(einv[:, 0:r], pc[:, 0:r], AF.Exp, scale=1.0 / 16)
                    self.act(ee[0:r, :], pc[0:r, 128:256], AF.Exp, scale=-1.0 / 16)

                def pc_():
                    pq, pv = st8["pq"], st8["pv"]
                    self.stt(qd[:, 0:r], pq[:, 0:r], 128.0 ** -0.5, eb[:, 0:r], ALU.mult, ALU.mult)
                    self.tt(ki[:, 0:r], pq[:, 128:128 + r], einv[:, 0:r], ALU.mult)
                    self.tt(kend[0:r, :], pv[0:r, 256:384], ee[0:r, :], ALU.mult)
                    pr = self.bank()
                    prv = pr[:, 0:256].rearrange("p (a b) -> p a b", a=2)
                    for vh in range(2):
                        self.mm(prv[:, vh, 0:r], [(W["r"][:, k, vh * 128:vh * 128 + 128], hTt(k)) for k in range(8)], last_inc=(vh == 1))
                    self.act(rs[:, :, 0:r], prv[:, :, 0:r], AF.Silu)
                    psc = self.bank()
                    self.mm(psc[0:r, 0:r], [(ki[:, 0:r], qd[:, 0:r])])
                    self.tt(sc[0:r, 0:r], psc[0:r, 0:r], TriU[0:r, 0:r], ALU.mult)

                self.S.rec = None
                self.use_banks([0, 1], [0])
                LA = self.record(pa)
                self.use_banks([2], [0])
                LB = self.record(pb)
                self.use_banks([0, 1], [0])
                LC = self.record(pc_)
                self.S.rec = self.merge(LA, LB) + LC

            def part2(i, c0, r, par):
                samp = (i == NPT)
                qd, sc, kend, vbf, rs, eb = qds[par], scs[par], kends[par], vbfs[par], rss[par], ebs[par]
                if not samp:
                    po = self.bank()
                    pov = po[:, 0:256].rearrange("p (a b) -> p a b", a=2)
                    for vh in range(2):
                        pairs = [(vbf[0:r, vh * 128:vh * 128 + 128], sc[0:r, 0:r])]
                        if i > 0:
                            pairs.append((Sbf[:, vh * 128:vh * 128 + 128], qd[:, 0:r]))
                        self.mm(pov[:, vh, 0:r], pairs, last_inc=(vh == 1))
                    self.act(oT[:, :, 0:r], pov[:, :, 0:r], AF.Copy)
                else:
                    pos = [self.bank(), self.bank()]
                    for vh in range(2):
                        l0, r0 = vbf[0:r, vh * 128:vh * 128 + 128], sc[0:r, 0:r]
                        self.S.op("pe", (lambda e, o_=pos[vh][:, 0:r], l0=l0, r0=r0: e.matmul(o_, l0, r0, start=True, stop=False)),
                                  reads=[l0, r0], writes=[pos[vh][:, 0:r]], inc=False)
                    for b in range(16):
                        sb = S0bf[b % 4]
                        self.act(sb[:], S0[:, b, :], AF.Copy)
                        for vh in range(2):
                            l1, r1 = sb[:, vh * 128:vh * 128 + 128], qd[:, 4 * b:4 * b + 4]
                            last = (b == 15)
                            self.S.op("pe", (lambda e, o_=pos[vh][:, 4 * b:4 * b + 4], l1=l1, r1=r1, last=last: e.matmul(o_, l1, r1, start=False, stop=last)),
                                      reads=[l1, r1], writes=[pos[vh][:, 4 * b:4 * b + 4]], inc=(vh == 1))
                    for vh in range(2):
                        self.act(oT[:, vh, 0:r], pos[vh][:, 0:r], AF.Copy)
                self.tt(sq[:, :, 0:r], oT[:, :, 0:r], oT[:, :, 0:r], ALU.mult)
                pss = self.bank()
                self.mm(pss[:, 0:r], [(ones, sq[:, 0, 0:r]), (ones, sq[:, 1, 0:r])])
                self.act(rstdB[:, 0:r], pss[:, 0:r], AF.Ln, scale=1.0 / 256, bias=EPS)
                self.act(rstdB[:, 0:r], rstdB[:, 0:r], AF.Exp, scale=-0.5)
                self.tt(oT[:, :, 0:r], oT[:, :, 0:r], rstdB[:, 0:r].unsqueeze(1).to_broadcast([128, 2, r]), ALU.mult)
                self.tt(oT[:, :, 0:r], oT[:, :, 0:r], rs[:, :, 0:r], ALU.mult)
                self.tt(gT[:, :, 0:r], oT[:, :, 0:r], gnB[:, 2 * hd:2 * hd + 2, 0:r], ALU.mult)
                for nb in range(2):
                    pout = self.bank()
                    self.mm(pout[0:r, :], [(gT[:, vh, 0:r], W["o"][:, vh, nb * 512:nb * 512 + 512]) for vh in range(2)])
                    xs = self.X[0:r, i, nb * 512:nb * 512 + 512]
                    self.tt(xs, xs, pout[0:r, :], ALU.add)
                if not samp:
                    psu = self.bank()
                    self.mm(psu[:, 0:256], [(kend[0:r, :], vbf[0:r, :])])
                    if i == 0:
                        self.cp(Sst[:], psu[:, 0:256])
                    else:
                        self.stt(Sst[:], Sst[:], eb[:, r - 1:r], psu[:, 0:256], ALU.mult, ALU.add)
                    if i == NPT - 1:
                        self.store(self.o["gla_p"][hd], Sst[:], "glap")
                    else:
                        self.act(Sbf[:], Sst[:], AF.Copy)
                else:
                    self.tt(Vexp[0:64], vbf[0:64, :].unsqueeze(1).to_broadcast([64, 16, 256]),
                            selb[0:64, :].unsqueeze(2).to_broadcast([64, 16, 256]), ALU.mult)
                    for pb in range(8):
                        psu = self.bank()
                        self.mm(psu[:, 0:512], [(kend[0:64, :], Vexp[0:64, 2 * pb:2 * pb + 2, :].rearrange("p a b -> p (a b)"))])
                        for b in (2 * pb, 2 * pb + 1):
                            sn = Snew[b % 4]
                            self.stt(sn[:], S0[:, b, :], eb[:, 4 * b + 3:4 * b + 4], psu[:, (b % 2) * 256:(b % 2) * 256 + 256], ALU.mult, ALU.add)
                            self.store(self.o["gla_s"][b, hd], sn[:], "glas%d" % (b % 2))

            def compute():
                for b in range(16):
                    self.load(S0[:, b, :], self.d["st_gla"][b, hd], "gS%d" % (b % 2))
                prev = None
                for idx, (i, c0, r) in enumerate(self.tiles):
                    par = idx % 2
                    self.use_banks([0, 1, 2], [0])
                    L1 = self.record(lambda: part1(i, c0, r, par))
                    if prev is None:
                        self.S.play(L1)
                    else:
                        self.use_banks([3, 4, 5], [1])
                        L2 = self.record(lambda: part2(*prev))
                        self.S.play(self.merge(L1, L2))
                    prev = (i, c0, r, par)
                self.use_banks([3, 4, 5], [1])
                L2 = self.record(lambda: part2(*prev))
                self.S.play(L2)
                self.use_banks(None, None)

            self.stage(loads, compute, la=1)

        for hd in range(4):
            head_stage(hd)

    def mix_ssd(self, l):
        ar = self.ar
        NPT = self.NPT
        S = self.S
        AX = mybir.AxisListType.X
        self.new_phase()
        ar.reset()
        nl, ncmp = self.make_hT(self.d["norm_mix"][l:l + 1, :])
        Wz = ar.alloc([128, 8, 512], BF16)
        Wx = ar.alloc([128, 8, 512], BF16)
        WBC = ar.alloc([128, 8, 256], BF16)
        Wdt = ar.alloc([128, 8, 8], BF16)
        Wo = ar.alloc([128, 4, D], BF16)
        gnB2 = ar.alloc([128, 512], F32)
        cw = ar.alloc([128, 6, 4], F32)
        cb = ar.alloc([128, 6], F32)
        dtbB = ar.alloc([128, 32], F32)
        aB = ar.alloc([128, 32], F32)
        DB = ar.alloc([128, 32], F32)
        selb = ar.alloc([128, 16], F32)
        expand = ar.alloc([128, 4, 128], F32)
        xpre = ar.alloc([128, 6, 131], F32)
        acc = ar.alloc([128, 6, 128], F32)
        seg = ar.alloc([128, 8, 128], F32)
        Mh = ar.alloc([128, 8, 128], BF16)
        extT = ar.alloc([128, 6, 16, 7], F32)
        cst = xpre[:].rearrange("p a b -> p (a b)")[:, 0:768]
        cvs = acc[:].rearrange("p a b -> p (a b)")
        xcbs = [ar.alloc([128, 6, 128], BF16) for _ in range(2)]
        xtmBs = [ar.alloc([128, 640], BF16) for _ in range(2)]
        zss = [ar.alloc([128, 512], F32) for _ in range(2)]
        sms = [ar.alloc([128, 8, 8], F32) for _ in range(2)]
        ybs = [ar.alloc([128, 512], F32) for _ in range(2)]
        tmp = ar.alloc([128, 512], F32)
        yn = ar.alloc([128, 512], BF16)
        ynT = ar.alloc([128, 4, 128], BF16)
        xw = ar.alloc([128, 512], BF16)
        ST = ar.alloc([128, 512], F32)
        STbf = ar.alloc([128, 512], BF16)
        S0nat = [ar.alloc([128, 4, 128], F32) for _ in range(4)]
        ST0bf = [ar.alloc([128, 512], BF16) for _ in range(2)]
        Snew = [ar.alloc([128, 4, 128], F32) for _ in range(2)]
        CTm = ar.alloc([128, 16, 64], BF16)
        Btmb = ar.alloc([128, 16, 128], BF16)
        edT = ar.alloc([128, 16], F32)
        edn = ar.alloc([128, 4, 16], F32)
        csq = self.csq
        ones = csq[:, 3, :]
        identF = csq[:, 0, :]
        win = self.d["ssm_w_in"]
        import os as _os
        CONV_ENG = _os.environ.get("SSD_CONV_ENG", "dve")
        POOL_ENG = _os.environ.get("SSD_POOL_ENG", "pool")

        def loads0():
            nl()
            self.load(dtbB[:], self.d["ssm_dt_bias"][0, :].partition_broadcast(128), "mwA")
            self.load(aB[:], self.d["ssm_a_log"][0, :].partition_broadcast(128), "mwB")
            self.load(DB[:], self.d["ssm_d"][0, :].partition_broadcast(128), "mwC")
            self.load(selb[0:64, :], self.d["c_selb"], "mwD")
            self.load(expand[0:8], self.d["c_expand"], "mwE")

        def compute0():
            ncmp()
            self.act(aB[:], aB[:], AF.Exp)
            self.ts(aB[:], aB[:], -1.0, None, ALU.mult)

        self.stage(loads0, compute0, la=0)

        def group_stage(g):
            xcols = [(512 * g + 128 * c) for c in range(4)] + [2048 + 128 * g, 2560 + 128 * g]
            segs = [(0, 512, 512 * g), (512, 128, 2048 + 128 * g), (640, 128, 2560 + 128 * g)]

            def loads():
                self.load_cast(Wx[:], win[:, 2048 + 512 * g:2048 + 512 * g + 512].rearrange("(k p) n -> p k n", p=128), "sx")
                self.load_cast(WBC[:, :, 0:128], win[:, 4096 + 128 * g:4096 + 128 * g + 128].rearrange("(k p) n -> p k n", p=128), "sb")
                self.load_cast(WBC[:, :, 128:256], win[:, 4608 + 128 * g:4608 + 128 * g + 128].rearrange("(k p) n -> p k n", p=128), "sc")
                self.load_cast(Wdt[:], win[:, 5120 + 8 * g:5120 + 8 * g + 8].rearrange("(k p) n -> p k n", p=128), "sd")
                self.load_cast(Wz[:], win[:, 512 * g:512 * g + 512].rearrange("(k p) n -> p k n", p=128), "sz")
                self.load_cast(Wo[:], self.d["ssm_w_out"][512 * g:512 * g + 512, :].rearrange("(k p) n -> p k n", p=128), "so")
                self.load(gnB2[:], self.d["ssm_norm"][0, 512 * g:512 * g + 512].partition_broadcast(128), "sg")
                for c in range(6):
                    S.dma("sp", cw[:, c, :], self.d["ssm_conv_w"][:, xcols[c]:xcols[c] + 128].rearrange("j p -> p j"), "scw", allow_slow_non_contiguous=True)
                    S.dma("sp", cb[:, c:c + 1], self.d["ssm_conv_b"][0, xcols[c]:xcols[c] + 128].rearrange("(p o) -> p o", o=1), "scb")

            def part1a(i, c0, r, par):
                samp = (i == NPT)
                xcb, xtmB, zs = xcbs[par], xtmBs[par], zss[par]
                hTt = lambda k: self.hT[:, k, c0:c0 + r]
                if samp:
                    for si, (a0, w_, d0) in enumerate(segs):
                        self.load(cst[0:48, a0:a0 + w_], self.d["st_conv"][:, d0:d0 + w_], "scs%d" % si)
                    pcs = self.bank()
                    for c in range(6):
                        self.tr(pcs[:, 48 * c:48 * c + 48], cst[0:48, 128 * c:128 * c + 128], identF[0:48, 0:48], inc=(c == 5))
                    self.act(extT[:, :, :, 0:3], pcs[:, 0:288].rearrange("p (c b j) -> p c b j", c=6, b=16), AF.Copy)
                px1 = self.bank()
                px1v = px1[:].rearrange("p (a b) -> p a b", a=4)
                for c in range(4):
                    self.mm(px1v[:, c, 0:r], [(Wx[:, k, 128 * c:128 * c + 128], hTt(k)) for k in range(8)], last_inc=(c == 3))
                if not samp:
                    self.act(xpre[:, 0:4, 3:3 + r], px1v[:, :, 0:r], AF.Copy)
                else:
                    self.act(extT[:, 0:4, :, 3:7], px1v[:, :, 0:64].rearrange("p c (b t) -> p c b t", t=4), AF.Copy)
                px2 = self.bank()
                px2v = px2[:, 0:256].rearrange("p (a b) -> p a b", a=2)
                for c in range(2):
                    self.mm(px2v[:, c, 0:r], [(WBC[:, k, 128 * c:128 * c + 128], hTt(k)) for k in range(8)], last_inc=(c == 1))
                if not samp:
                    self.act(xpre[:, 4:6, 3:3 + r], px2v[:, :, 0:r], AF.Copy)
                    srcv = lambda c, j: xpre[:, c, j:j + r]
                    accv = lambda c: acc[:, c, 0:r]
                else:
                    self.act(extT[:, 4:6, :, 3:7], px2v[:, :, 0:64].rearrange("p c (b t) -> p c b t", t=4), AF.Copy)
                    srcv = lambda c, j: extT[:, c, :, j:j + 4]
                    accv = lambda c: acc[:, c, 0:64].rearrange("p (b t) -> p b t", t=4)
                pz = self.bank()
                self.mm(pz[0:r, :], [(hTt(k), Wz[:, k, :]) for k in range(8)])
                for c in range(6):
                    self.ts(accv(c), srcv(c, 0), cw[:, c, 0:1], cb[:, c:c + 1], ALU.mult, ALU.add, eng=POOL_ENG)
                for j in range(1, 4):
                    for c in range(6):
                        self.stt(accv(c), srcv(c, j), cw[:, c, j:j + 1], accv(c), ALU.mult, ALU.add)
                if not samp and i < NPT - 1:
                    self.cp(xpre[:, :, 0:3], xpre[:, :, r:r + 3])
                outer = S.rec
                S.rec = []
                self.act(xcb[:, :, 0:r], acc[:, :, 0:r], AF.Silu)
                self.act(zs[0:r, :], pz[0:r, :], AF.Silu)
                grp = S.rec
                S.rec = outer
                S.rec.append(("group", grp))
                tbx = self.tbank()
                for c in range(5):
                    self.tr(tbx[0:r, 128 * c:128 * c + 128], xcb[:, c, 0:r], self.identb[:, :], inc=(c == 4))
                self.act(xtmB[0:r, :], tbx[0:r, 0:640], AF.Copy)

            def part1b(i, c0, r, par):
                samp = (i == NPT)
                TriU = csq[:, 4 if samp else 1, :]
                Neg = csq[:, 7 if samp else 6, :]
                sm = sms[par]
                dtp, dt_, lnd, dA, cl, ecum, wendc, edecB = (sm[:, j, :] for j in range(8))
                hTt = lambda k: self.hT[:, k, c0:c0 + r]
                pd = self.bank()
                self.mm(pd[0:r, 0:8], [(hTt(k), Wdt[:, k, :]) for k in range(8)])
                self.tt(dtp[0:r], pd[0:r, 0:8], dtbB[0:r, 8 * g:8 * g + 8], ALU.add)
                self.act(dtp[0:r], dtp[0:r], AF.Exp)
                self.act(dt_[0:r], dtp[0:r], AF.Ln, bias=1.0)
                self.act(lnd[0:r], dt_[0:r], AF.Ln)
                self.tt(dA[0:r], dt_[0:r], aB[0:r, 8 * g:8 * g + 8], ALU.mult)
                pcm = self.bank()
                self.mm(pcm[0:r, 0:8], [(TriU[0:r, 0:r], dA[0:r])])
                self.tt(cl[0:r], pcm[0:r, 0:8], lnd[0:r], ALU.subtract)
                self.act(ecum[0:r], pcm[0:r, 0:8], AF.Exp)
                self.tt(seg[0:r, :, 0:r], dA[0:r].unsqueeze(2).to_broadcast([r, 8, r]), TriU[0:r, 0:r].unsqueeze(1).to_broadcast([r, 8, r]), ALU.mult, eng=POOL_ENG)
                for half in range(2):
                    pcb = self.bank()
                    pcbv = pcb[:].rearrange("p (a b) -> p a b", a=4)
                    if r == 128:
                        self.mm(pcbv[:, :, 0:r], [(ones[0:r, :], seg[0:r, 4 * half:4 * half + 4, 0:r])])
                    else:
                        for hh in range(4):
                            self.mm(pcbv[:, hh, 0:r], [(ones[0:r, :], seg[0:r, 4 * half + hh, 0:r])], last_inc=(hh == 3))
                    if not samp:
                        self.act(edecB[:, 4 * half:4 * half + 4], pcbv[:, :, r - 1], AF.Exp)
                    self.tt(seg[0:r, 4 * half:4 * half + 4, 0:r], pcbv[0:r, :, 0:r],
                            cl[0:r, 4 * half:4 * half + 4].unsqueeze(2).to_broadcast([r, 4, r]), ALU.subtract)
                self.tt(seg[0:r, :, 0:r], seg[0:r, :, 0:r], Neg[0:r, 0:r].unsqueeze(1).to_broadcast([r, 8, r]), ALU.add)
                self.act(seg[0:r, :, 0:r], seg[0:r, :, 0:r], AF.Exp)
                if not samp:
                    self.cp(wendc[0:r], seg[0:r, :, r - 1])
                else:
                    tmpE = Mh[0:64].rearrange("p a b -> p (a b)").bitcast(F32).rearrange("p (h t) -> p h t", h=8)
                    self.tt(tmpE, seg[0:64, :, 0:64], csq[0:64, 9, 0:64].unsqueeze(1).to_broadcast([64, 8, 64]), ALU.mult)
                    S.op("dve", lambda e: e.tensor_reduce(out=wendc[0:64, :], in_=tmpE, axis=AX, op=ALU.add),
                         reads=[tmpE], writes=[wendc[0:64, :]])

            def part1c(i, c0, r, par):
                samp = (i == NPT)
                xcb, xtmB, yb = xcbs[par], xtmBs[par], ybs[par]
                xtm, BCT = xtmB[:, 0:512], xcb[:, 4:6, :]
                hTt = lambda k: self.hT[:, k, c0:c0 + r]
                pg = self.bank()
                self.mm(pg[0:r, 0:r], [(BCT[:, 0, 0:r], BCT[:, 1, 0:r])])
                self.tt(Mh[0:r, :, 0:r], seg[0:r, :, 0:r], pg[0:r, 0:r].unsqueeze(1).to_broadcast([r, 8, r]), ALU.mult)
                py = self.bank()
                for h in range(8):
                    self.mm(py[0:r, 64 * h:64 * h + 64], [(Mh[0:r, h, 0:r], xtm[0:r, 64 * h:64 * h + 64])], last_inc=(h == 7))
                self.act(yb[0:r, :], py[0:r, :], AF.Copy)
                if i >= NPT - 1:
                    pc1 = self.bank()
                    self.mm(pc1[0:r, :], [(hTt(k), Wx[:, k, :]) for k in range(8)])
                    pc2 = self.bank()
                    self.mm(pc2[0:r, 0:256], [(hTt(k), WBC[:, k, :]) for k in range(8)])
                    self.act(cvs[0:r, 0:512], pc1[0:r, :], AF.Copy)
                    self.act(cvs[0:r, 512:768], pc2[0:r, 0:256], AF.Copy)
                    if not samp:
                        for (a0, w_, d0) in segs:
                            self.store(self.o["conv_p"][:, d0:d0 + w_], cvs[125:128, a0:a0 + w_], "cvp")
                    else:
                        for b in range(16):
                            for (a0, w_, d0) in segs:
                                self.store(self.o["conv_s"][b, :, d0:d0 + w_], cvs[4 * b + 1:4 * b + 4, a0:a0 + w_], "cvs")

            def part2(i, c0, r, par):
                samp = (i == NPT)
                TriU = csq[:, 4 if samp else 1, :]
                xcb, xtmB, zs, sm, yb = xcbs[par], xtmBs[par], zss[par], sms[par], ybs[par]
                xtf, Btm, BCT = xtmB[:, 0:512], xtmB[:, 512:640], xcb[:, 4:6, :]
                dtp, dt_, lnd, dA, cl, ecum, wendc, edecB = (sm[:, j, :] for j in range(8))
                v8 = lambda ap: ap.rearrange("p (h q) -> p h q", h=8)
                self.tt(v8(xw[0:r, :]), v8(xtf[0:r, :]), wendc[0:r].unsqueeze(2).to_broadcast([r, 8, 64]), ALU.mult, eng=POOL_ENG)
                if not samp:
                    if i > 0:
                        pi = self.bank()
                        self.mm(pi[0:r, :], [(BCT[:, 1, 0:r], STbf[:, :])])
                        self.tt(v8(tmp[0:r, :]), v8(pi[0:r, :]), ecum[0:r].unsqueeze(2).to_broadcast([r, 8, 64]), ALU.mult)
                        self.tt(yb[0:r, :], yb[0:r, :], tmp[0:r, :], ALU.add)
                    psu = self.bank()
                    self.mm(psu[:, :], [(Btm[0:r, :], xw[0:r, :])])
                    if i == 0:
                        self.cp(ST[:], psu[:, :])
                    else:
                        self.tt(v8(ST[:]), v8(ST[:]), edecB[:].unsqueeze(2).to_broadcast([128, 8, 64]), ALU.mult)
                        self.tt(ST[:], ST[:], psu[:, :], ALU.add)
                    if i < NPT - 1:
                        self.act(STbf[:], ST[:], AF.Copy)
                    else:
                        pso = self.bank()
                        for c in range(4):
                            self.tr(pso[:, 128 * c:128 * c + 128], ST[:, 128 * c:128 * c + 128], identF, inc=(c == 3))
                        so = Snew[0]
                        self.act(so[:], pso[:].rearrange("p (c n) -> p c n", c=4), AF.Copy)
                        self.store(self.o["ssm_p"][512 * g:512 * g + 512, :].rearrange("(c p) n -> p c n", p=128), so[:], "ssmo0")
                else:
                    self.memset(CTm[:], 0.0)
                    for b in range(16):
                        self.cp(CTm[:, b, 4 * b:4 * b + 4], BCT[:, 1, 4 * b:4 * b + 4])
                    self.tt(Btmb[0:64], Btm[0:64, :].unsqueeze(1).to_broadcast([64, 16, 128]), selb[0:64, :].unsqueeze(2).to_broadcast([64, 16, 128]), ALU.mult)
                    pct = self.bank()
                    self.mm(pct[0:8, 0:64], [(dA[0:64], TriU[0:64, 0:64])])
                    self.act(edT[0:8, :], pct[0:8, 0:64].rearrange("h (b t) -> h b t", t=4)[:, :, 3], AF.Exp)
                    pen = self.bank()
                    for c in range(4):
                        self.mm(pen[:, 16 * c:16 * c + 16], [(expand[0:8, c, :], edT[0:8, :])], last_inc=(c == 3))
                    self.act(edn[:], pen[:, 0:64].rearrange("p (c b) -> p c b", c=4), AF.Copy)
                    pi = self.bank()
                    self.reserved.add(pi.name)
                    def ld_state(b_):
                        self.load(S0nat[b_ % 4][:], self.d["st_ssm"][b_, 512 * g:512 * g + 512, :].rearrange("(c p) n -> p c n", p=128), "ssn%d" % (b_ % 4))
                    for b_ in range(3):
                        ld_state(b_)
                    for b in range(16):
                        sn = S0nat[b % 4]
                        if b + 3 < 16:
                            ld_state(b + 3)
                        pst = self.bank()
                        for c in range(4):
                            self.tr(pst[:, 128 * c:128 * c + 128], sn[:, c, :], identF, inc=(c == 3))
                        sb = ST0bf[b % 2]
                        self.act(sb[:], pst[:, :], AF.Copy)
                        l1, r1 = CTm[:, b, :], sb[:, :]
                        S.op("pe", (lambda e, l1=l1, r1=r1, st_=(b == 0), sp_=(b == 15): e.matmul(pi[0:64, :], l1, r1, start=st_, stop=sp_)),
                             reads=[l1, r1], writes=[pi[0:64, :]], inc=True)
                        psn = self.bank()
                        psnv = psn[:].rearrange("p (c n) -> p c n", c=4)
                        for c in range(4):
                            self.mm(psnv[:, c, :], [(xw[0:64, 128 * c:128 * c + 128], Btmb[0:64, b, :])], last_inc=(c == 3))
                        so = Snew[b % 2]
                        for c in range(4):
                            self.stt(so[:, c, :], sn[:, c, :], edn[:, c, b:b + 1], psnv[:, c, :], ALU.mult, ALU.add)
                        self.S.dma("pool", self.o["ssm_s"][b, 512 * g:512 * g + 512, :].rearrange("(c p) n -> p c n", p=128), so[:], "ssms%d" % (b % 2))
                    self.reserved.discard(pi.name)
                    self.tt(v8(tmp[0:64, :]), v8(pi[0:64, :]), ecum[0:64].unsqueeze(2).to_broadcast([64, 8, 64]), ALU.mult)
                    self.tt(yb[0:64, :], yb[0:64, :], tmp[0:64, :], ALU.add)
                self.tt(v8(tmp[0:r, :]), v8(xtf[0:r, :]), DB[0:r, 8 * g:8 * g + 8].unsqueeze(2).to_broadcast([r, 8, 64]), ALU.mult, eng=POOL_ENG)
                self.tt(yb[0:r, :], yb[0:r, :], tmp[0:r, :], ALU.add, eng=POOL_ENG)
                self.tt(yb[0:r, :], yb[0:r, :], zs[0:r, :], ALU.mult, eng=POOL_ENG)
                ssq = self.stat[0:r, 32:33]
                rsd = self.stat[0:r, 33:34]
                self.act(self.junk[0:r, 0:512], yb[0:r, :], AF.Square, accum_out=ssq)
                self.act(rsd, ssq, AF.Ln, scale=1.0 / 512, bias=EPS)
                self.act(rsd, rsd, AF.Exp, scale=-0.5)
                self.stt(yn[0:r, :], yb[0:r, :], rsd, gnB2[0:r, :], ALU.mult, ALU.mult)
                tb = self.tbank()
                tbv = tb[:, 0:512].rearrange("p (a b) -> p a b", a=4)
                for c in range(4):
                    self.tr(tbv[:, c, 0:r], yn[0:r, 128 * c:128 * c + 128], self.identb[0:r, 0:r], inc=(c == 3))
                self.act(ynT[:, :, 0:r], tbv[:, :, 0:r], AF.Copy)
                for nb in range(2):
                    pout = self.bank()
                    self.mm(pout[0:r, :], [(ynT[:, c, 0:r], Wo[:, c, nb * 512:nb * 512 + 512]) for c in range(4)])
                    xs = self.X[0:r, i, nb * 512:nb * 512 + 512]
                    self.tt(xs, xs, pout[0:r, :], ALU.add)

            def compute():
                self.memset(xpre[:, :, 0:3], 0.0)
                prev = None
                for idx, (i, c0, r) in enumerate(self.tiles):
                    par = idx % 2
                    self.use_banks([0, 1], [0])
                    LA = self.record(lambda: part1a(i, c0, r, par))
                    self.use_banks([2], [0])
                    LB = self.record(lambda: part1b(i, c0, r, par))
                    self.use_banks([0, 1], [0])
                    LC = self.record(lambda: part1c(i, c0, r, par))
                    L1 = self.merge(LA, LB) + LC
                    if prev is None:
                        S.play(L1)
                    else:
                        self.use_banks([3, 4, 5], [1])
                        L2 = self.record(lambda: part2(*prev))
                        S.play(self.merge(L1, L2))
                    prev = (i, c0, r, par)
                self.use_banks([3, 4, 5], [1])
                L2 = self.record(lambda: part2(*prev))
                S.play(L2)
                self.use_banks(None, None)

            self.stage(loads, compute, la=0)

        for g in range(4):
            group_stage(g)


def build_nc(NPT=16, mixers=(0, 1, 2, 3), same_engine_sync=True):
    k = K(NPT, mixers, same_engine_sync)
    return k.build()


WEIGHT_NAMES = ["norm_ffn1", "ffn1_gate", "ffn1_up", "ffn1_down", "norm_mix", "norm_ffn2", "ffn2_gate", "ffn2_up",
                "ffn2_down", "norm_ple", "ple_gate", "ple_proj", "gm_w_in", "gm_w_s", "gm_b_s", "gm_w_out",
                "pool_w", "gla_w_in", "gla_w_a1", "gla_w_a2", "gla_w_out", "ssm_w_in", "ssm_conv_w", "ssm_w_out"]
ROW_NAMES = ["norm_final", "gm_ln", "pool_scale", "gla_b_a", "gla_norm", "ssm_conv_b", "ssm_dt_bias", "ssm_a_log",
             "ssm_d", "ssm_norm"]


def make_in_maps(inputs, NPT, ncores):
    f = lambda a: np.ascontiguousarray(np.asarray(a, dtype=np.float32))
    shared = {k: f(inputs[k]) for k in WEIGHT_NAMES}
    for k in ROW_NAMES:
        shared[k] = f(inputs[k]).reshape(1, -1)
    shared.update(make_consts())
    maps = []
    for c in range(ncores):
        m = dict(shared)
        m["xp"] = f(inputs["x_prompt"][c])
        m["xs"] = f(inputs["x_sample"][16 * c:16 * c + 16]).reshape(64, D)
        m["st_pool"] = f(inputs["state_pool_l1"][16 * c:16 * c + 16]).reshape(240, D)
        m["st_gla"] = f(inputs["state_gla_l2"][16 * c:16 * c + 16])
        m["st_ssm"] = f(inputs["state_ssm_l3"][16 * c:16 * c + 16]).reshape(16, 2048, 128)
        m["st_conv"] = f(inputs["state_conv_l3"][16 * c:16 * c + 16]).reshape(48, 3072)
        m["pp"] = f(inputs["p_prompt"][:, c])
        m["ps"] = f(inputs["p_sample"][:, 16 * c:16 * c + 16]).reshape(4, 64, 256)
        maps.append(m)
    return maps


def gather(results, NPT, ncores):
    TP = NPT * 128
    cat = lambda k, shp: np.concatenate([np.asarray(r[k]).reshape(shp) for r in results], axis=0)
    return (
        cat("yp", (1, TP, D)), cat("ys", (16, 4, D)), cat("chunk_v", (16, 4, D)),
        cat("pool_p", (1, 15, D)), cat("pool_s", (16, 15, D)),
        cat("gla_p", (1, 4, 128, 256)), cat("gla_s", (16, 4, 128, 256)),
        cat("ssm_p", (1, 32, 64, 128)), cat("ssm_s", (16, 32, 64, 128)),
        cat("conv_p", (1, 3, 3072)), cat("conv_s", (16, 3, 3072)),
    )


_NC_CACHE = {}


def kernel(**inputs):
    NPT = 16
    ncores = 8
    if "nc" not in _NC_CACHE:
        _NC_CACHE["nc"] = build_nc(NPT)
    nc = _NC_CACHE["nc"]
    maps = make_in_maps(inputs, NPT, ncores)
    res = run_bass_kernel_spmd(nc, maps, core_ids=list(range(ncores)))
    outs = gather(res.results, NPT, ncores)
    return tuple(np.ascontiguousarray(o, dtype=np.float32) for o in outs)
```

```python
import numpy as np
import concourse.bass as bass
import concourse.mybir as mybir
from concourse.bass_utils import run_bass_kernel_spmd

F32 = mybir.dt.float32
BF16 = mybir.dt.bfloat16
AF = mybir.ActivationFunctionType
ALU = mybir.AluOpType

D = 1024
DFF = 2816
NCH = 8
EPS = 1e-6
NEG = -30000.0


def _esize(dt):
    return 4 if dt == F32 else 2


class Rec:
    __slots__ = ("plo", "phi", "ivals", "lo", "hi", "who", "key")


class Sched:
    ENG = ["pe", "act", "dve", "pool", "sp"]

    def __init__(self, nc, same_engine_sync=True):
        self.nc = nc
        self.sem = {e: nc.alloc_semaphore("sem_" + e) for e in self.ENG}
        self.count = {e: 0 for e in self.ENG}
        self.ops = {e: [] for e in self.ENG}
        self.seen = {e: {} for e in self.ENG}
        self.tens = {}
        self.chan = {}
        self.chan_by_sem = {}
        self.rec = None
        import os as _os
        self.vclock = _os.environ.get("VCLOCK", "1") == "1"
        self.sem_eng = {self.sem[e].num: e for e in self.ENG}
        self.snaps = {e: {} for e in self.ENG}
        self.same = same_engine_sync
        self.nwaits = 0

    def region(self, ap):
        tn = type(ap.tensor).__name__
        if not (tn.startswith("SB") or tn.startswith("PSum")):
            return None
        pat = ap.ap
        es = _esize(ap.dtype)
        pstep, pcnt = pat[0]
        off = int(ap.offset)
        if pstep > 0:
            p0 = off // pstep
            f0 = off % pstep
        else:
            p0 = 0
            f0 = off
        free = [(s, c) for (s, c) in pat[1:] if c > 1 and s != 0]
        free.sort(key=lambda x: x[0])
        run = 1
        rest = []
        for (s, c) in free:
            if s == run:
                run = run * c
            else:
                rest.append((s, c))
        starts = [0]
        nrest = 1
        for (s, c) in rest:
            nrest *= c
        if nrest <= 64:
            for (s, c) in rest:
                starts = [a + s * i for a in starts for i in range(c)]
            ivals = [((f0 + a) * es, (f0 + a + run) * es) for a in starts]
        else:
            ext = run + sum(s * (c - 1) for (s, c) in rest)
            ivals = [(f0 * es, (f0 + ext) * es)]
        r = Rec()
        r.plo = p0
        r.phi = p0 + pcnt
        if tn.startswith("PSum"):
            r.plo, r.phi = 0, 128
            ivals = [(0, 2048)]
        r.ivals = ivals
        r.lo = min(a for a, b in ivals)
        r.hi = max(b for a, b in ivals)
        r.key = (ap.tensor.name, r.plo, r.phi, tuple(ivals))
        return ap.tensor.name, r

    @staticmethod
    def _overlap(a, b):
        if a.hi <= b.lo or b.hi <= a.lo or a.phi <= b.plo or b.phi <= a.plo:
            return False
        for (x0, x1) in a.ivals:
            for (y0, y1) in b.ivals:
                if x0 < y1 and y0 < x1:
                    return True
        return False

    @staticmethod
    def _covers(w, r):
        if not (w.plo <= r.plo and r.phi <= w.phi):
            return False
        for (y0, y1) in r.ivals:
            ok = False
            for (x0, x1) in w.ivals:
                if x0 <= y0 and y1 <= x1:
                    ok = True
                    break
            if not ok:
                return False
        return True

    def _collect(self, reads, writes):
        whos = []
        rr = []
        ww = []
        for ap in reads:
            x = self.region(ap)
            if x is None:
                continue
            name, r = x
            rr.append((name, r))
            t = self.tens.setdefault(name, {"w": [], "r": []})
            for rec in t["w"]:
                if self._overlap(rec, r):
                    whos.append(rec.who)
            if type(ap.tensor).__name__.startswith("PSum"):
                for rec in t["r"]:
                    whos.append(rec.who)
        for ap in writes:
            x = self.region(ap)
            if x is None:
                continue
            name, r = x
            ww.append((name, r))
            t = self.tens.setdefault(name, {"w": [], "r": []})
            for rec in t["w"]:
                if self._overlap(rec, r):
                    whos.append(rec.who)
            for rec in t["r"]:
                if self._overlap(rec, r):
                    whos.append(rec.who)
        return whos, rr, ww

    def _record(self, rr, ww, who):
        for name, r in ww:
            t = self.tens[name]
            t["w"] = [x for x in t["w"] if not self._covers(r, x)]
            t["r"] = [x for x in t["r"] if not self._covers(r, x)]
            r.who = who
            t["w"].append(r)
        for name, r in rr:
            t = self.tens[name]
            r.who = who
            done = False
            if who[0] == "e":
                for x in t["r"]:
                    if x.key == r.key and x.who[0] == "e" and x.who[1] == who[1]:
                        x.who = who
                        done = True
                        break
            if not done:
                t["r"].append(r)

    def _waits(self, eng, whos):
        seen = self.seen[eng]
        best = {}
        for w in whos:
            if w[0] == "e":
                _, e2, seq = w
                if e2 == eng:
                    if eng == "pe" or not self.same:
                        continue
                    assert seq <= self.count[eng], "same-engine dep on pending op"
                sem = self.sem[e2]
                val = seq
            else:
                _, sem, val = w
                val = max(val, self.chan_by_sem[sem.num][1])
            k = sem.num
            if k not in best or best[k][1] < val:
                best[k] = (sem, val)
        waits = []
        items = sorted(best.items(), key=lambda kv: -kv[1][1])
        for k, (sem, val) in items:
            if seen.get(k, 0) >= val:
                continue
            seen[k] = val
            waits.append((sem, val))
            if self.vclock:
                e2 = self.sem_eng.get(k)
                if e2 is not None:
                    snap = self.snaps[e2].get(val)
                    if snap is not None:
                        for kk, vv in snap.items():
                            if seen.get(kk, 0) < vv:
                                seen[kk] = vv
        self.nwaits += len(waits)
        return waits

    def play(self, items):
        for it in items:
            if it[0] == "group":
                self.play(it[1])
            elif it[0] == "op":
                self.op(*it[1:])
            else:
                self.dma(*it[1:-1], **it[-1])

    def op(self, eng, fn, reads=(), writes=(), inc=True):
        if self.rec is not None:
            self.rec.append(("op", eng, fn, list(reads), list(writes), inc))
            return
        whos, rr, ww = self._collect(reads, writes)
        waits = self._waits(eng, whos)
        seq = self.count[eng] + 1
        if inc:
            self.count[eng] = seq
            if self.vclock:
                self.snaps[eng][seq] = dict(self.seen[eng])
        self.ops[eng].append((waits, fn, (self.sem[eng], 1) if inc else None))
        self._record(rr, ww, ("e", eng, seq))

    def dma(self, eng, out, in_, chan, **kw):
        if self.rec is not None:
            self.rec.append(("dma", eng, out, in_, chan, kw))
            return
        if not chan.startswith(eng + "_"):
            chan = eng + "_" + chan
        whos, rr, ww = self._collect([in_], [out])
        waits = self._waits(eng, whos)
        if chan not in self.chan:
            self.chan[chan] = [self.nc.alloc_semaphore("ch_" + chan), 0]
            self.chan_by_sem[self.chan[chan][0].num] = self.chan[chan]
        c = self.chan[chan]
        c[1] += 16
        sem, val = c[0], c[1]
        self.ops[eng].append((waits, lambda e: e.dma_start(out=out, in_=in_, **kw), (sem, 16)))
        self._record(rr, ww, ("d", sem, val))

    def finish(self):
        nc = self.nc
        whos = [("d", c[0], c[1]) for c in self.chan.values()]
        whos += [("e", e, self.count[e]) for e in self.ENG if e != "sp" and self.count[e] > 0]
        waits = self._waits("sp", whos)
        self.ops["sp"].append((waits, None, None))

        import os as _os
        fuse = _os.environ.get("FUSE_WAIT", "1") == "1"

        def replay(e, lst):
            for waits, fn, inc in lst:
                if fn is None:
                    for (sem, val) in waits:
                        e.wait_ge(sem, val)
                    continue
                ws = list(waits)
                last = ws.pop() if (fuse and ws) else None
                for (sem, val) in ws:
                    e.wait_ge(sem, val)
                ins = fn(e)
                if last is not None:
                    ins._wait_ge(last[0], last[1])
                if inc is not None:
                    ins.then_inc(inc[0], inc[1])

        with nc.Block() as block:
            @block.tensor
            def _(e):
                replay(e, self.ops["pe"])

            @block.scalar
            def _(e):
                replay(e, self.ops["act"])

            @block.vector
            def _(e):
                replay(e, self.ops["dve"])

            @block.gpsimd
            def _(e):
                replay(e, self.ops["pool"])

            @block.sync
            def _(e):
                replay(e, self.ops["sp"])


class Arena:
    def __init__(self, t, nbytes):
        self.t = t
        self.n = nbytes
        self.off = 0

    def reset(self, off=0):
        self.off = off

    def alloc(self, shape, dt):
        es = _esize(dt)
        n = 1
        for s in shape[1:]:
            n *= s
        nb = n * es
        self.off = (self.off + 31) // 32 * 32
        assert self.off + nb <= self.n, f"arena overflow {self.off}+{nb}>{self.n}"
        v = self.t[:, self.off // 2:(self.off + nb) // 2]
        self.off += nb
        if dt == F32:
            v = v.bitcast(F32)
        if len(shape) == 3:
            v = v.rearrange("p (a b) -> p a b", a=shape[1])
        elif len(shape) == 4:
            v = v.rearrange("p (a b c) -> p a b c", a=shape[1], b=shape[2])
        if shape[0] < 128:
            v = v[0:shape[0]]
        return v


class Stage:
    def __init__(self, loads, compute, la=2):
        self.loads = loads
        self.compute = compute
        self.la = la


INPUT_SHAPES = None


def in_specs(NPT):
    TP = NPT * 128
    return {
        "xp": [TP, D], "xs": [64, D],
        "st_pool": [240, D], "st_gla": [16, 4, 128, 256], "st_ssm": [16, 2048, 128], "st_conv": [48, 3072],
        "pp": [4, TP, 256], "ps": [4, 64, 256],
        "norm_ffn1": [4, D], "ffn1_gate": [4, D, DFF], "ffn1_up": [4, D, DFF], "ffn1_down": [4, DFF, D],
        "norm_mix": [4, D], "norm_ffn2": [4, D],
        "ffn2_gate": [4, D, DFF], "ffn2_up": [4, D, DFF], "ffn2_down": [4, DFF, D],
        "norm_ple": [4, D], "ple_gate": [4, D, D], "ple_proj": [4, 256, D], "norm_final": [1, D],
        "gm_w_in": [D, 2048], "gm_ln": [1, D], "gm_w_s": [8, 128, 128], "gm_b_s": [8, 128], "gm_w_out": [D, D],
        "pool_w": [4, 256, 256], "pool_scale": [1, D],
        "gla_w_in": [D, 3072], "gla_w_a1": [D, 16], "gla_w_a2": [16, 512], "gla_b_a": [1, 512],
        "gla_norm": [1, D], "gla_w_out": [D, D],
        "ssm_w_in": [D, 5152], "ssm_conv_w": [4, 3072], "ssm_conv_b": [1, 3072], "ssm_dt_bias": [1, 32],
        "ssm_a_log": [1, 32], "ssm_d": [1, 32], "ssm_norm": [1, 2048], "ssm_w_out": [2048, D],
        "c_sq": [128, 10, 128],
        "c_pool": [128, 12, 128], "c_pools": [64, 4, 64], "c_poolh": [120, 4, 2, 64],
        "c_selb": [64, 16], "c_selbt": [128, 16, 64], "c_expand": [8, 4, 128],
    }


def out_specs(NPT):
    TP = NPT * 128
    return {
        "yp": [TP, D], "ys": [64, D], "chunk_v": [64, D], "pool_p": [15, D], "pool_s": [16, 15, D],
        "gla_p": [4, 128, 256], "gla_s": [16, 4, 128, 256], "ssm_p": [2048, 128], "ssm_s": [16, 2048, 128],
        "conv_p": [3, 3072], "conv_s": [16, 3, 3072],
    }


def make_consts():
    c = {}
    sq = np.zeros((128, 10, 128), np.float32)
    s = np.arange(128)[:, None]
    t = np.arange(128)[None, :]
    sq[:, 0] = (s == t)
    sq[:, 1] = (s <= t)
    sq[:, 2] = (s > t)
    sq[:, 3] = 1.0
    same = (s // 4 == t // 4) & (s < 64) & (t < 64)
    sq[:, 4] = same & (s <= t)
    sq[:, 5] = same & (s > t)
    sq[:, 6] = np.where(s <= t, 0.0, NEG)
    sq[:, 7] = np.where(same & (s <= t), 0.0, NEG)
    sq[:, 8] = (s >= t)
    sq[:, 9] = (t == 4 * (s // 4) + 3) & (s < 64)
    c["c_sq"] = sq
    pc = np.zeros((128, 12, 128), np.float32)
    pss = np.zeros((64, 4, 64), np.float32)
    ph = np.zeros((120, 4, 2, 64), np.float32)
    for wi, w in enumerate((2, 4, 8, 16)):
        d = t - s
        pc[:, wi * 3 + 0] = ((d >= 0) & (d < w)) / float(w) - (s == t)
        d2 = t + 128 - s
        pc[:, wi * 3 + 1] = ((d2 >= 0) & (d2 < w)) / float(w)
        cnt = np.minimum(t + 1, w).astype(np.float32)
        pc[:, wi * 3 + 2] = ((d >= 0) & (d < w)) / cnt - (s == t)
        s6 = np.arange(64)[:, None]
        t6 = np.arange(64)[None, :]
        d6 = (t6 % 4) - (s6 % 4)
        pss[:, wi] = ((s6 // 4 == t6 // 4) & (d6 >= 0) & (d6 < w)) / float(w) - (s6 == t6)
        for half in range(2):
            r = np.arange(120)[:, None]
            bb = r // 15 + half * 8
            j = r % 15
            dd = 15 + (t6 % 4) - j
            ph[:, wi, half] = ((bb == t6 // 4) & (dd < w)) / float(w)
    c["c_pool"] = pc
    c["c_pools"] = pss
    c["c_poolh"] = ph
    s6 = np.arange(64)[:, None]
    c["c_selb"] = (s6 // 4 == np.arange(16)[None, :]).astype(np.float32)
    sel = (np.arange(16)[:, None] == (np.arange(64)[None, :] // 4)).astype(np.float32)
    c["c_selbt"] = np.broadcast_to(sel[None], (128, 16, 64)).copy()
    ex = np.zeros((8, 4, 128), np.float32)
    for h in range(8):
        ex[h, h // 2, (h % 2) * 64:(h % 2) * 64 + 64] = 1.0
    c["c_expand"] = ex
    return c


class K:
    def __init__(self, NPT=16, mixers=(0, 1, 2, 3), same_engine_sync=True):
        self.NPT = NPT
        self.NT = NPT + 1
        self.T = NPT * 128 + 64
        self.mixers = mixers
        self.tiles = [(i, i * 128, 128) for i in range(NPT)] + [(NPT, NPT * 128, 64)]
        self.blocks = [(c0, min(512, NPT * 128 - c0)) for c0 in range(0, NPT * 128, 512)] + [(NPT * 128, 64)]
        nc = bass.Bass("TRN2", target_bir_lowering=False)
        self.nc = nc
        self.d = {k: nc.dram_tensor(k, v, F32, kind="ExternalInput").ap() for k, v in in_specs(NPT).items()}
        self.o = {k: nc.dram_tensor(k, v, F32, kind="ExternalOutput").ap() for k, v in out_specs(NPT).items()}
        self.S = Sched(nc, same_engine_sync)
        self.stages = []
        self.reserved = set()
        self._bset = None
        self._tbset = None
        self._bctr = {}
        self.phase_start = 0
        self._bank = 0
        self._tbank = 0
        self._gb = 0
        self._uid = 0

    def use_banks(self, idx, tidx):
        self._bset = idx
        self._tbset = tidx

    def bank(self):
        lst = self.banks if self._bset is None else [self.banks[j] for j in self._bset]
        key = "all" if self._bset is None else tuple(self._bset)
        while True:
            n = self._bctr.get(key, 0)
            self._bctr[key] = n + 1
            b = lst[n % len(lst)]
            if b.name not in self.reserved:
                return b

    def tbank(self):
        lst = self.tbanks if self._tbset is None else [self.tbanks[j] for j in self._tbset]
        key = "tall" if self._tbset is None else ("t",) + tuple(self._tbset)
        n = self._bctr.get(key, 0)
        self._bctr[key] = n + 1
        return lst[n % len(lst)]

    def record(self, f):
        assert self.S.rec is None
        self.S.rec = []
        f()
        lst = self.S.rec
        self.S.rec = None
        return lst

    @staticmethod
    def merge(a, b):
        out = []
        na, nb = len(a), len(b)
        ia = ib = 0
        while ia < na or ib < nb:
            if ib >= nb or (ia < na and ia * nb <= ib * na):
                out.append(a[ia])
                ia += 1
            else:
                out.append(b[ib])
                ib += 1
        return out

    def mm(self, ps, pairs, last_inc=True):
        n = len(pairs)
        for i, (l, r) in enumerate(pairs):
            self.S.op("pe", (lambda e, l=l, r=r, st=(i == 0), sp=(i == n - 1): e.matmul(ps, l, r, start=st, stop=sp)),
                      reads=[l, r], writes=[ps], inc=(i == n - 1) and last_inc)

    def tr(self, ps, in_, ident, inc):
        if in_.dtype == F32:
            self.mm(ps, [(in_, ident)], last_inc=inc)
            return
        self.S.op("pe", lambda e: e.transpose(out=ps, in_=in_, identity=ident), reads=[in_, ident], writes=[ps], inc=inc)

    def act(self, out, in_, func, reads=None, **kw):
        rd = [in_] + [v for v in kw.values() if hasattr(v, "ap")]
        wr = [out]
        if "accum_out" in kw:
            wr.append(kw["accum_out"])
            rd = [in_] + [v for k, v in kw.items() if hasattr(v, "ap") and k != "accum_out"]
        self.S.op("act", lambda e: e.activation(out=out, in_=in_, func=func, **kw), reads=rd, writes=wr)

    def tt(self, out, in0, in1, op, eng="dve"):
        self.S.op(eng, lambda e: e.tensor_tensor(out=out, in0=in0, in1=in1, op=op), reads=[in0, in1], writes=[out])

    def ts(self, out, in0, s1, s2, op0, op1=None, eng="dve"):
        rd = [in0] + [x for x in (s1, s2) if hasattr(x, "ap")]
        if op1 is None:
            self.S.op(eng, lambda e: e.tensor_scalar(out=out, in0=in0, scalar1=s1, scalar2=None, op0=op0), reads=rd, writes=[out])
        else:
            self.S.op(eng, lambda e: e.tensor_scalar(out=out, in0=in0, scalar1=s1, scalar2=s2, op0=op0, op1=op1), reads=rd, writes=[out])

    def stt(self, out, in0, scalar, in1, op0, op1, eng="dve"):
        rd = [in0, in1] + ([scalar] if hasattr(scalar, "ap") else [])
        self.S.op(eng, lambda e: e.scalar_tensor_tensor(out=out, in0=in0, scalar=scalar, in1=in1, op0=op0, op1=op1), reads=rd, writes=[out])

    def cp(self, out, in_, eng="dve"):
        self.S.op(eng, lambda e: e.tensor_copy(out=out, in_=in_), reads=[in_], writes=[out])

    def memset(self, out, val, eng="dve"):
        self.S.op(eng, lambda e: e.memset(out, val), reads=[], writes=[out])

    def uid(self, p):
        self._uid += 1
        return f"{p}{self._uid}"

    def load_cast(self, out, in_, chan):
        self.S.dma("pool", out, in_, chan)

    def load(self, out, in_, chan):
        self.S.dma("sp", out, in_, chan)

    def store(self, out, in_, chan):
        self.S.dma("sp", out, in_, chan)

    def build(self):
        nc = self.nc
        NT = self.NT
        T = self.T
        ARENA = 93184
        with (
            nc.sbuf_tensor("X", [128, NT, D], F32) as X,
            nc.sbuf_tensor("hT", [128, NCH, T], BF16) as hT,
            nc.sbuf_tensor("arena", [128, ARENA // 2], BF16) as arena,
            nc.sbuf_tensor("csq", [128, 10, 128], F32) as csq,
            nc.sbuf_tensor("identb", [128, 128], BF16) as identb,
            nc.sbuf_tensor("gB", [128, 1, D], F32) as gB,
            nc.sbuf_tensor("hn", [128, 2, D], BF16) as hn,
            nc.sbuf_tensor("junk", [128, D], BF16) as junk,
            nc.sbuf_tensor("stat", [128, 64], F32) as stat,
            nc.psum_tensor("B0", [128, 512], F32) as B0,
            nc.psum_tensor("B1", [128, 512], F32) as B1,
            nc.psum_tensor("B2", [128, 512], F32) as B2,
            nc.psum_tensor("B3", [128, 512], F32) as B3,
            nc.psum_tensor("B4", [128, 512], F32) as B4,
            nc.psum_tensor("B5", [128, 512], F32) as B5,
            nc.psum_tensor("T0", [128, 1024], BF16) as T0,
            nc.psum_tensor("T1", [128, 1024], BF16) as T1,
        ):
            self.X, self.hT, self.csq, self.identb, self.gB, self.hn, self.junk, self.stat = X, hT, csq, identb, gB, hn, junk, stat
            self.banks = [B0, B1, B2, B3, B4, B5]
            self.tbanks = [T0, T1]
            self.ar = Arena(arena, ARENA)
            self.prologue()
            for l in range(4):
                self.ffn(l, 1)
                self.mixer(l)
                self.ffn(l, 2)
                self.ple(l)
            self.final()
            self.emit()
            self.S.finish()
        return nc

    def emit(self):
        st = self.stages
        n = len(st)
        at = {}
        for i, s in enumerate(st):
            at.setdefault(max(0, i - s.la), []).append(i)
        for j in range(n):
            for i in at.get(j, []):
                if st[i].loads is not None:
                    st[i].loads()
            st[j].compute()

    def stage(self, loads, compute, la=2):
        i = len(self.stages)
        la = max(0, min(la, i - self.phase_start))
        self.stages.append(Stage(loads, compute, la))

    def new_phase(self):
        self.phase_start = len(self.stages)

    def prologue(self):
        def loads():
            self.load(self.csq[:], self.d["c_sq"][:], "csq")
            self.load_cast(self.identb[:], self.d["c_sq"][:, 0, :], "identb")
            for (i, c0, r) in self.tiles:
                src = self.d["xp"][c0:c0 + r, :] if i < self.NPT else self.d["xs"][:, :]
                self.load(self.X[0:r, i, :], src, "X%d" % (i * 4 // self.NT))

        self.stage(loads, lambda: None, la=0)

    def make_hT(self, grow, extra=None):
        slot = 0
        gBs = self.gB[:, slot, :]

        def loads():
            self.load(gBs, grow.partition_broadcast(128), "gB%d" % slot)

        def compute(after_tile=None):
            X, hT, stat = self.X, self.hT, self.stat

            def stA(i, c0, r):
                hs = self.hn[0:r, i % 2, :]
                ss = stat[0:r, (i % 4) * 2:(i % 4) * 2 + 1]
                rs = stat[0:r, (i % 4) * 2 + 1:(i % 4) * 2 + 2]
                self.act(self.junk[0:r, :], X[0:r, i, :], AF.Square, accum_out=ss)
                self.act(rs, ss, AF.Ln, scale=1.0 / D, bias=EPS)
                self.act(rs, rs, AF.Exp, scale=-0.5)
                self.stt(hs, X[0:r, i, :], rs, gBs[0:r, :], ALU.mult, ALU.mult)
                if extra is not None:
                    extra(i, c0, r, rs, gBs)

            def stB(i, c0, r):
                hs = self.hn[0:r, i % 2, :]
                tb = self.tbank()
                tbv = tb[:].rearrange("p (a b) -> p a b", a=8)
                for k in range(8):
                    self.tr(tbv[:, k, 0:r], hs[:, 128 * k:128 * k + 128], self.identb[0:r, 0:r], inc=(k == 7))
                self.cp(hT[:, :, c0:c0 + r], tbv[:, :, 0:r])

            n = len(self.tiles)
            for idx in range(n + 1):
                if idx < n:
                    stA(*self.tiles[idx])
                if idx >= 1:
                    stB(*self.tiles[idx - 1])
                    if after_tile is not None:
                        after_tile(idx - 1)

        return loads, compute

    def ffn(self, l, which):
        S = self.S
        Wg = self.d["ffn%d_gate" % which][l]
        Wu = self.d["ffn%d_up" % which][l]
        Wd = self.d["ffn%d_down" % which][l]
        self.new_phase()
        nl, ncmp = self.make_hT(self.d["norm_ffn%d" % which][l:l + 1, :])
        ar = self.ar
        ar.reset()
        NSLOT = 4
        gu = [ar.alloc([128, 2, 8, 128], BF16) for _ in range(NSLOT)]
        wd = ar.alloc([128, 6, D], BF16)
        aT = ar.alloc([128, 6, self.T], BF16)
        sil = [ar.alloc([128, 512], F32) for _ in range(2)]
        quarters = [list(range(0, 6)), list(range(6, 11)), list(range(11, 17)), list(range(17, 22))]
        first = [True]
        cnt = [0]

        def gu_stage(j, jj, la, first_hook=None):
            slot = cnt[0] % NSLOT
            cnt[0] += 1
            g = gu[slot]

            def loads():
                self.load_cast(g[:, 0], Wg[:, 128 * j:128 * j + 128].rearrange("(k p) n -> p k n", p=128), "gug%d" % slot)
                self.load_cast(g[:, 1], Wu[:, 128 * j:128 * j + 128].rearrange("(k p) n -> p k n", p=128), "guu%d" % slot)

            def block(bi):
                c0, bs = self.blocks[bi]
                pg = self.bank()
                pu = self.bank()
                self.mm(pg[:, 0:bs], [(g[:, 0, k, :], self.hT[:, k, c0:c0 + bs]) for k in range(8)])
                self.mm(pu[:, 0:bs], [(g[:, 1, k, :], self.hT[:, k, c0:c0 + bs]) for k in range(8)])
                st = sil[bi % 2]
                self.act(st[:, 0:bs], pg[:, 0:bs], AF.Silu)
                self.tt(aT[:, jj, c0:c0 + bs], st[:, 0:bs], pu[:, 0:bs], ALU.mult)

            def compute():
                for bi in range(len(self.blocks)):
                    block(bi)

            if first_hook is not None:
                first_hook.append(block)
                self.stage(loads, lambda: None, la)
            else:
                self.stage(loads, compute, la)

        def down_stage(q):
            nq = len(q)

            def loads():
                self.load_cast(wd[:, 0:nq, :], Wd[128 * q[0]:128 * (q[-1] + 1), :].rearrange("(j p) n -> p j n", p=128), "wd")

            def compute():
                for (i, c0, r) in self.tiles:
                    for nb in range(2):
                        ps = self.bank()
                        self.mm(ps[0:r, :], [(aT[:, jj, c0:c0 + r], wd[:, jj, nb * 512:nb * 512 + 512]) for jj in range(nq)])
                        xs = self.X[0:r, i, nb * 512:nb * 512 + 512]
                        self.stt(xs, ps[0:r, :], 0.5, xs, ALU.mult, ALU.add)

            self.stage(loads, compute, 2)

        hook = []
        last_tile = {}
        for bi, (bc0, bs) in enumerate(self.blocks):
            for (ti, tc0, tr_) in self.tiles:
                if bc0 <= tc0 < bc0 + bs:
                    last_tile[bi] = ti

        def after_tile(ti):
            for bi, lt in last_tile.items():
                if lt == ti:
                    hook[0](bi)

        self.stage(nl, lambda: ncmp(after_tile=after_tile), la=0)
        for qi, q in enumerate(quarters):
            for jj, j in enumerate(q):
                gu_stage(j, jj, 2, first_hook=(hook if (qi == 0 and jj == 0) else None))
            down_stage(q)

    def ple(self, l):
        ar = self.ar
        self.new_phase()
        nl, ncmp = self.make_hT(self.d["norm_ple"][l:l + 1, :])
        ar.reset()
        Wpg = ar.alloc([128, 8, D], BF16)
        Wpp = ar.alloc([128, 2, D], BF16)
        ptok = ar.alloc([128, self.NT, 256], BF16)
        pT = ar.alloc([128, 2, self.T], BF16)
        sig = [ar.alloc([128, 512], F32) for _ in range(2)]
        tmp = [ar.alloc([128, 512], F32) for _ in range(2)]
        NPT = self.NPT

        def loads():
            nl()
            self.load_cast(ptok[:, 0:NPT, :], self.d["pp"][l].rearrange("(i p) c -> p i c", p=128), "ptok")
            self.load_cast(ptok[0:64, NPT, :], self.d["ps"][l], "ptoks")
            self.load_cast(Wpp[:], self.d["ple_proj"][l].rearrange("(k p) n -> p k n", p=128), "Wpp")
            self.load_cast(Wpg[:], self.d["ple_gate"][l].rearrange("(k p) n -> p k n", p=128), "Wpg")

        def compute():
            for (i, c0, r) in self.tiles:
                tb = self.tbank()
                for k in range(2):
                    self.tr(tb[:, k * 128:k * 128 + r], ptok[0:r, i, 128 * k:128 * k + 128], self.identb[0:r, 0:r], inc=(k == 1))
                self.act(pT[:, :, c0:c0 + r], tb[:, 0:256].rearrange("p (a b) -> p a b", a=2)[:, :, 0:r], AF.Copy)

            def tile_part(ti):
                (i, c0, r) = self.tiles[ti]
                for nb in range(2):
                    pg = self.bank()
                    pp_ = self.bank()
                    self.mm(pg[0:r, :], [(self.hT[:, k, c0:c0 + r], Wpg[:, k, nb * 512:nb * 512 + 512]) for k in range(8)])
                    self.mm(pp_[0:r, :], [(pT[:, k, c0:c0 + r], Wpp[:, k, nb * 512:nb * 512 + 512]) for k in range(2)])
                    sg = sig[nb]
                    tm = tmp[nb]
                    self.act(sg[0:r, :], pg[0:r, :], AF.Sigmoid)
                    self.tt(tm[0:r, :], sg[0:r, :], pp_[0:r, :], ALU.mult)
                    xs = self.X[0:r, i, nb * 512:nb * 512 + 512]
                    self.tt(xs, xs, tm[0:r, :], ALU.add)

            ncmp()
            for ti in range(len(self.tiles)):
                tile_part(ti)

        self.stage(loads, compute, la=0)

    def final(self):
        ar = self.ar
        self.new_phase()
        ar.reset()
        yb = [ar.alloc([128, D], F32) for _ in range(4)]
        slot = 0
        gBs = self.gB[:, slot, :]

        def loads():
            self.load(gBs, self.d["norm_final"][0:1, :].partition_broadcast(128), "gB%d" % slot)

        def compute():
            X, stat = self.X, self.stat
            for (i, c0, r) in self.tiles:
                ss = stat[0:r, (i % 4) * 2:(i % 4) * 2 + 1]
                rs = stat[0:r, (i % 4) * 2 + 1:(i % 4) * 2 + 2]
                self.act(self.junk[0:r, :], X[0:r, i, :], AF.Square, accum_out=ss)
                self.act(rs, ss, AF.Ln, scale=1.0 / D, bias=EPS)
                self.act(rs, rs, AF.Exp, scale=-0.5)
                y = yb[i % 4]
                self.stt(y[0:r, :], X[0:r, i, :], rs, gBs[0:r, :], ALU.mult, ALU.mult)
                dst = self.o["yp"][c0:c0 + r, :] if i < self.NPT else self.o["ys"][:, :]
                self.store(dst, y[0:r, :], "yb%d" % (i % 4))

        self.stage(loads, compute, la=0)

    def mixer(self, l):
        if l not in self.mixers:
            return
        [self.mix_gmlp, self.mix_pool, self.mix_gla, self.mix_ssd][l](l)

    def mix_gmlp(self, l):
        ar = self.ar
        self.new_phase()
        nl, ncmp = self.make_hT(self.d["norm_mix"][l:l + 1, :])
        ar.reset()
        Wu = ar.alloc([128, 8, D], BF16)
        Wv = ar.alloc([128, 8, D], BF16)
        Wo = ar.alloc([128, 8, D], BF16)
        wsbf = ar.alloc([128, 8, 128], BF16)
        WsT = ar.alloc([128, 8, 128], BF16)
        WsS = ar.alloc([128, 8, 64], BF16)
        bsB = ar.alloc([128, 8, 128], F32)
        bsS = ar.alloc([128, 8, 64], F32)
        gln = ar.alloc([128, D], F32)
        mark = ar.off
        wsnat = ar.alloc([128, 8, 128], F32)
        ar.reset(mark)
        uTs = [ar.alloc([128, 8, 128], F32) for _ in range(2)]
        vt = ar.alloc([128, D], F32)
        vtmp = ar.alloc([128, D], F32)
        vbfs = [ar.alloc([128, D], BF16) for _ in range(2)]
        gT = ar.alloc([128, 8, 128], BF16)
        svt = ar.alloc([128, 4, 128], F32)
        st = self.stat
        gw = self.d["gm_w_in"]

        def loads():
            nl()
            self.load_cast(Wu[:], gw[:, 0:D].rearrange("(k p) n -> p k n", p=128), "mwA")
            self.load_cast(Wv[:], gw[:, D:2 * D].rearrange("(k p) n -> p k n", p=128), "mwB")
            self.load_cast(Wo[:], self.d["gm_w_out"].rearrange("(k p) n -> p k n", p=128), "mwC")
            self.load(wsnat[:], self.d["gm_w_s"].rearrange("g t s -> t g s"), "mwD")
            self.load(bsB[:].rearrange("p a b -> p (a b)"), self.d["gm_b_s"].rearrange("g t -> (g t)").partition_broadcast(128), "mwE")
            self.load(gln[:], self.d["gm_ln"][0, :].partition_broadcast(128), "mwF")

        def compute():
            ncmp()
            self.tt(wsbf[:], wsnat[:], self.csq[:, 8, :].unsqueeze(1).to_broadcast([128, 8, 128]), ALU.mult)
            tb = self.tbank()
            tbv = tb[:].rearrange("p (a b) -> p a b", a=8)
            for g in range(8):
                self.tr(tbv[:, g, :], wsbf[:, g, :], self.identb[:], inc=(g == 7))
            self.act(WsT[:], tbv[:], AF.Copy)
            self.memset(WsS[:], 0.0)
            for b in range(16):
                self.S.dma("sp", WsS[4 * b:4 * b + 4, :, 4 * b:4 * b + 4], WsT[0:4, :, 0:4], "wss")
            self.cp(bsS[:].rearrange("p g (b t) -> p g b t", b=16), bsB[:, :, 0:4].unsqueeze(2).to_broadcast([128, 8, 16, 4]))
            def part1(i, c0, r, par):
                samp = (i == self.NPT)
                uT, vbf = uTs[par], vbfs[par]
                for half in range(2):
                    pu = self.bank()
                    puv = pu[:].rearrange("p (a b) -> p a b", a=4)
                    for mm_ in range(4):
                        m = half * 4 + mm_
                        self.mm(puv[:, mm_, 0:r], [(Wu[:, k, 128 * m:128 * m + 128], self.hT[:, k, c0:c0 + r]) for k in range(8)], last_inc=(mm_ == 3))
                    self.act(uT[:, half * 4:half * 4 + 4, 0:r], puv[:, :, 0:r], AF.Gelu_apprx_tanh)
                for nb in range(2):
                    pv = self.bank()
                    self.mm(pv[0:r, :], [(self.hT[:, k, c0:c0 + r], Wv[:, k, nb * 512:nb * 512 + 512]) for k in range(8)])
                    self.act(vt[0:r, nb * 512:nb * 512 + 512], pv[0:r, :], AF.Gelu_apprx_tanh, accum_out=st[0:r, 16 + nb:17 + nb])
                self.act(self.junk[0:r, :], vt[0:r, :], AF.Square, accum_out=st[0:r, 18:19])
                self.tt(st[0:r, 19:20], st[0:r, 16:17], st[0:r, 17:18], ALU.add)
                self.ts(st[0:r, 20:21], st[0:r, 19:20], 1.0 / D, None, ALU.mult)
                self.tt(st[0:r, 21:22], st[0:r, 20:21], st[0:r, 20:21], ALU.mult)
                self.stt(st[0:r, 22:23], st[0:r, 18:19], 1.0 / D, st[0:r, 21:22], ALU.mult, ALU.subtract)
                self.act(st[0:r, 23:24], st[0:r, 22:23], AF.Ln, bias=EPS)
                self.act(st[0:r, 23:24], st[0:r, 23:24], AF.Exp, scale=-0.5)
                self.ts(vtmp[0:r, :], vt[0:r, :], st[0:r, 20:21], st[0:r, 23:24], ALU.subtract, ALU.mult)
                if samp:
                    self.tt(vt[0:r, :], vtmp[0:r, :], gln[0:r, :], ALU.mult)
                    self.store(self.o["chunk_v"][:, :], vt[0:r, :], "cv")
                    self.cp(vbf[0:r, :], vt[0:r, :])
                else:
                    self.tt(vbf[0:r, :], vtmp[0:r, :], gln[0:r, :], ALU.mult)

            def part2(i, c0, r, par):
                samp = (i == self.NPT)
                uT, vbf = uTs[par], vbfs[par]
                Wmix = WsS if samp else WsT
                bias = bsS if samp else bsB
                for half in range(2):
                    psv = self.bank()
                    pv4 = psv[:].rearrange("p (a b) -> p a b", a=4)
                    for gg in range(4):
                        g = half * 4 + gg
                        self.mm(pv4[:, gg, 0:r], [(vbf[0:r, 128 * g:128 * g + 128], Wmix[0:r, g, 0:r])], last_inc=(gg == 3))
                    self.tt(svt[:, :, 0:r], pv4[:, :, 0:r], bias[:, half * 4:half * 4 + 4, 0:r], ALU.add)
                    self.tt(gT[:, half * 4:half * 4 + 4, 0:r], svt[:, :, 0:r], uT[:, half * 4:half * 4 + 4, 0:r], ALU.mult)
                for nb in range(2):
                    po = self.bank()
                    self.mm(po[0:r, :], [(gT[:, m, 0:r], Wo[:, m, nb * 512:nb * 512 + 512]) for m in range(8)])
                    xs = self.X[0:r, i, nb * 512:nb * 512 + 512]
                    self.tt(xs, xs, po[0:r, :], ALU.add)

            prev = None
            for idx, (i, c0, r) in enumerate(self.tiles):
                par = idx % 2
                self.use_banks([0, 1, 2], [0])
                L1 = self.record(lambda: part1(i, c0, r, par))
                if prev is None:
                    self.S.play(L1)
                else:
                    self.use_banks([3, 4, 5], [1])
                    L2 = self.record(lambda: part2(*prev))
                    self.S.play(self.merge(L1, L2))
                prev = (i, c0, r, par)
            self.use_banks([3, 4, 5], [1])
            L2 = self.record(lambda: part2(*prev))
            self.S.play(L2)
            self.use_banks(None, None)

        self.stage(loads, compute, la=0)

    def mix_pool(self, l):
        ar = self.ar
        NPT = self.NPT
        self.new_phase()
        ar.reset()
        HN = ar.alloc([128, self.NT, D], BF16)
        Wp = ar.alloc([128, 4, 2, 256], BF16)
        PC = ar.alloc([128, 12, 128], BF16)
        PS = ar.alloc([128, 4, 64], BF16)
        PH = ar.alloc([128, 4, 2, 64], BF16)
        HB = ar.alloc([128, 2, D], BF16)
        scB = ar.alloc([128, D], F32)
        hf = ar.alloc([128, 2, D], F32)
        diffT = [ar.alloc([128, 8, 128], BF16) for _ in range(2)]
        tmp = [ar.alloc([128, 512], F32) for _ in range(2)]

        def extra(i, c0, r, rs, gBs):
            self.cp(HN[0:r, i, :], self.hn[0:r, i % 2, :], eng="pool")
            if i >= NPT - 1:
                self.stt(hf[0:r, i - (NPT - 1), :], self.X[0:r, i, :], rs, gBs[0:r, :], ALU.mult, ALU.mult)

        nl, ncmp = self.make_hT(self.d["norm_mix"][l:l + 1, :], extra=extra)

        def loads():
            nl()
            self.load_cast(Wp[:], self.d["pool_w"].rearrange("g (cc p) d -> p g cc d", p=128), "mwA")
            self.load_cast(PC[:], self.d["c_pool"], "mwB")
            self.load_cast(PS[0:64], self.d["c_pools"], "mwC")
            self.load_cast(PH[0:120], self.d["c_poolh"], "mwD")
            self.load_cast(HB[0:120], self.d["st_pool"].rearrange("(h r) d -> r h d", h=2), "mwE")
            self.load(scB[:], self.d["pool_scale"][0, :].partition_broadcast(128), "mwF")
            self.S.dma("sp", self.o["pool_s"][:, 0:11, :], self.d["st_pool"].rearrange("(b j) d -> b j d", j=15)[:, 4:15, :], "poolhist")

        def compute():
            ncmp()
            self.store(self.o["pool_p"][:, :], hf[113:128, 0, :], "poolp")
            for b in range(16):
                self.store(self.o["pool_s"][b, 11:15, :], hf[4 * b:4 * b + 4, 1, :], "pools")
            for (i, c0, r) in self.tiles:
                samp = (i == NPT)
                dT = diffT[i % 2]
                for half in range(2):
                    pb = self.bank()
                    pbv = pb[:].rearrange("p (a b) -> p a b", a=4)
                    for mm_ in range(4):
                        m = half * 4 + mm_
                        w = m // 2
                        fs = slice(128 * m, 128 * m + 128)
                        if samp:
                            pairs = [(HN[0:64, i, fs], PS[0:64, w, :]), (HB[0:120, 0, fs], PH[0:120, w, 0, :]), (HB[0:120, 1, fs], PH[0:120, w, 1, :])]
                        else:
                            pairs = [(HN[0:r, i, fs], PC[0:r, w * 3 + (2 if i == 0 else 0), 0:r])]
                            if i > 0:
                                pairs.append((HN[64:128, i - 1, fs], PC[64:128, w * 3 + 1, 0:r]))
                        self.mm(pbv[:, mm_, 0:r], pairs, last_inc=(mm_ == 3))
                    self.act(dT[:, half * 4:half * 4 + 4, 0:r], pbv[:, :, 0:r], AF.Copy)
                for nb in range(2):
                    po = self.bank()
                    for gg in range(2):
                        g = nb * 2 + gg
                        self.mm(po[0:r, gg * 256:gg * 256 + 256], [(dT[:, 2 * g + cc, 0:r], Wp[:, g, cc, :]) for cc in range(2)], last_inc=(gg == 1))
                    tm = tmp[nb]
                    self.tt(tm[0:r, :], po[0:r, :], scB[0:r, nb * 512:nb * 512 + 512], ALU.mult)
                    xs = self.X[0:r, i, nb * 512:nb * 512 + 512]
                    self.tt(xs, xs, tm[0:r, :], ALU.add)

        self.stage(loads, compute, la=0)

    def mix_gla(self, l):
        ar = self.ar
        NPT = self.NPT
        T = self.T
        self.new_phase()
        ar.reset()
        nl, ncmp = self.make_hT(self.d["norm_mix"][l:l + 1, :])
        Wa1 = ar.alloc([128, 8, 16], BF16)
        Wa2 = ar.alloc([128, 512], BF16)
        baB = ar.alloc([128, 512], F32)
        gnT = ar.alloc([128, 8], F32)
        gnB = ar.alloc([128, 8, 128], F32)
        selb = ar.alloc([128, 16], F32)
        t1 = ar.alloc([128, T], BF16)
        Ws = [dict(q=ar.alloc([128, 8, 128], BF16), k=ar.alloc([128, 8, 128], BF16), v=ar.alloc([128, 8, 256], BF16),
                   r=ar.alloc([128, 8, 256], BF16), o=ar.alloc([128, 2, D], BF16)) for _ in range(2)]
        qds = [ar.alloc([128, 128], BF16) for _ in range(2)]
        ki = ar.alloc([128, 128], BF16)
        rss = [ar.alloc([128, 2, 128], F32) for _ in range(2)]
        vbfs = [ar.alloc([128, 256], BF16) for _ in range(2)]
        zb = ar.alloc([128, 128], F32)
        lp = ar.alloc([128, 128], F32)
        ebs = [ar.alloc([128, 128], F32) for _ in range(2)]
        einv = ar.alloc([128, 128], F32)
        ee = ar.alloc([128, 128], F32)
        kends = [ar.alloc([128, 128], BF16) for _ in range(2)]
        scs = [ar.alloc([128, 128], BF16) for _ in range(2)]
        oT = ar.alloc([128, 2, 128], F32)
        sq = ar.alloc([128, 2, 128], F32)
        rstdB = ar.alloc([128, 128], F32)
        gT = ar.alloc([128, 2, 128], BF16)
        Sst = ar.alloc([128, 256], F32)
        Sbf = ar.alloc([128, 256], BF16)
        S0 = ar.alloc([128, 16, 256], F32)
        S0bf = [ar.alloc([128, 256], BF16) for _ in range(4)]
        Snew = [ar.alloc([128, 256], F32) for _ in range(4)]
        Vexp = ar.alloc([128, 16, 256], BF16)
        csq = self.csq
        ones = csq[:, 3, :]
        win = self.d["gla_w_in"]

        def loads0():
            nl()
            self.load_cast(Wa1[:], self.d["gla_w_a1"].rearrange("(k p) n -> p k n", p=128), "mwA")
            self.load_cast(Wa2[0:16, :], self.d["gla_w_a2"], "mwB")
            self.load(baB[:], self.d["gla_b_a"][0, :].partition_broadcast(128), "mwC")
            self.S.dma("sp", gnT[:], self.d["gla_norm"][0, :].rearrange("(m p) -> p m", p=128), "mwD", allow_slow_non_contiguous=True)
            self.load(selb[0:64, :], self.d["c_selb"], "mwE")

        def compute0():
            ncmp()
            for (c0, bs) in self.blocks:
                pt = self.bank()
                self.mm(pt[0:16, 0:bs], [(Wa1[:, k, :], self.hT[:, k, c0:c0 + bs]) for k in range(8)])
                self.act(t1[0:16, c0:c0 + bs], pt[0:16, 0:bs], AF.Copy)
            self.cp(gnB[:], gnT[:].unsqueeze(2).to_broadcast([128, 8, 128]))

        self.stage(loads0, compute0, la=0)

        def head_stage(hd):
            W = Ws[hd % 2]

            def loads():
                self.load_cast(W["q"][:], win[:, hd * 128:hd * 128 + 128].rearrange("(k p) n -> p k n", p=128), "gq%d" % (hd % 2))
                self.load_cast(W["k"][:], win[:, 512 + hd * 128:512 + hd * 128 + 128].rearrange("(k p) n -> p k n", p=128), "gk%d" % (hd % 2))
                self.load_cast(W["v"][:], win[:, 1024 + hd * 256:1024 + hd * 256 + 256].rearrange("(k p) n -> p k n", p=128), "gv%d" % (hd % 2))
                self.load_cast(W["r"][:], win[:, 2048 + hd * 256:2048 + hd * 256 + 256].rearrange("(k p) n -> p k n", p=128), "gr%d" % (hd % 2))
                self.load_cast(W["o"][:], self.d["gla_w_out"][hd * 256:hd * 256 + 256, :].rearrange("(k p) n -> p k n", p=128), "go%d" % (hd % 2))

            def part1(i, c0, r, par):
                samp = (i == NPT)
                TriU = csq[:, 4 if samp else 1, :]
                TriSL = csq[:, 5 if samp else 2, :]
                qd, sc, kend, vbf, rs, eb = qds[par], scs[par], kends[par], vbfs[par], rss[par], ebs[par]
                hTt = lambda k: self.hT[:, k, c0:c0 + r]
                st8 = {}

                def pa():
                    pq = self.bank()
                    self.mm(pq[:, 0:r], [(W["q"][:, k, :], hTt(k)) for k in range(8)])
                    self.mm(pq[:, 128:128 + r], [(W["k"][:, k, :], hTt(k)) for k in range(8)])
                    pv = self.bank()
                    self.mm(pv[0:r, 0:256], [(hTt(k), W["v"][:, k, :]) for k in range(8)])
                    self.mm(pv[0:r, 256:384], [(hTt(k), W["k"][:, k, :]) for k in range(8)])
                    self.act(vbf[0:r, :], pv[0:r, 0:256], AF.Copy)
                    st8["pq"], st8["pv"] = pq, pv

                def pb():
                    pz = self.bank()
                    self.mm(pz[0:r, 0:128], [(t1[0:16, c0:c0 + r], Wa2[0:16, hd * 128:hd * 128 + 128])])
                    self.tt(zb[0:r, :], pz[0:r, 0:128], baB[0:r, hd * 128:hd * 128 + 128], ALU.add)
                    self.act(zb[0:r, :], zb[0:r, :], AF.Exp, scale=-1.0)
                    self.act(lp[0:r, :], zb[0:r, :], AF.Ln, bias=1.0)
                    pc = self.bank()
                    self.mm(pc[:, 0:r], [(lp[0:r, :], TriU[0:r, 0:r])])
                    self.mm(pc[0:r, 128:256], [(TriSL[0:r, 0:r], lp[0:r, :])])
                    self.act(eb[:, 0:r], pc[:, 0:r], AF.Exp, scale=-1.0 / 16)
                    self.act(einv[:, 0:r], pc[:, 0:r], AF.Exp, scale=1.0 / 16)
                    self.act(ee[0:r, :], pc[0:r, 128:256], AF.Exp, scale=-1.0 / 16)

                def pc_():
                    pq, pv = st8["pq"], st8["pv"]
                    self.stt(qd[:, 0:r], pq[:, 0:r], 128.0 ** -0.5, eb[:, 0:r], ALU.mult, ALU.mult)
                    self.tt(ki[:, 0:r], pq[:, 128:128 + r], einv[:, 0:r], ALU.mult)
                    self.tt(kend[0:r, :], pv[0:r, 256:384], ee[0:r, :], ALU.mult)
                    pr = self.bank()
                    prv = pr[:, 0:256].rearrange("p (a b) -> p a b", a=2)
                    for vh in range(2):
                        self.mm(prv[:, vh, 0:r], [(W["r"][:, k, vh * 128:vh * 128 + 128], hTt(k)) for k in range(8)], last_inc=(vh == 1))
                    self.act(rs[:, :, 0:r], prv[:, :, 0:r], AF.Silu)
                    psc = self.bank()
                    self.mm(psc[0:r, 0:r], [(ki[:, 0:r], qd[:, 0:r])])
                    self.tt(sc[0:r, 0:r], psc[0:r, 0:r], TriU[0:r, 0:r], ALU.mult)

                self.S.rec = None
                self.use_banks([0, 1], [0])
                LA = self.record(pa)
                self.use_banks([2], [0])
                LB = self.record(pb)
                self.use_banks([0, 1], [0])
                LC = self.record(pc_)
                self.S.rec = self.merge(LA, LB) + LC

            def part2(i, c0, r, par):
                samp = (i == NPT)
                qd, sc, kend, vbf, rs, eb = qds[par], scs[par], kends[par], vbfs[par], rss[par], ebs[par]
                if not samp:
                    po = self.bank()
                    pov = po[:, 0:256].rearrange("p (a b) -> p a b", a=2)
                    for vh in range(2):
                        pairs = [(vbf[0:r, vh * 128:vh * 128 + 128], sc[0:r, 0:r])]
                        if i > 0:
                            pairs.append((Sbf[:, vh * 128:vh * 128 + 128], qd[:, 0:r]))
                        self.mm(pov[:, vh, 0:r], pairs, last_inc=(vh == 1))
                    self.act(oT[:, :, 0:r], pov[:, :, 0:r], AF.Copy)
                else:
                    pos = [self.bank(), self.bank()]
                    for vh in range(2):
                        l0, r0 = vbf[0:r, vh * 128:vh * 128 + 128], sc[0:r, 0:r]
                        self.S.op("pe", (lambda e, o_=pos[vh][:, 0:r], l0=l0, r0=r0: e.matmul(o_, l0, r0, start=True, stop=False)),
                                  reads=[l0, r0], writes=[pos[vh][:, 0:r]], inc=False)
                    for b in range(16):
                        sb = S0bf[b % 4]
                        self.act(sb[:], S0[:, b, :], AF.Copy)
                        for vh in range(2):
                            l1, r1 = sb[:, vh * 128:vh * 128 + 128], qd[:, 4 * b:4 * b + 4]
                            last = (b == 15)
                            self.S.op("pe", (lambda e, o_=pos[vh][:, 4 * b:4 * b + 4], l1=l1, r1=r1, last=last: e.matmul(o_, l1, r1, start=False, stop=last)),
                                      reads=[l1, r1], writes=[pos[vh][:, 4 * b:4 * b + 4]], inc=(vh == 1))
                    for vh in range(2):
                        self.act(oT[:, vh, 0:r], pos[vh][:, 0:r], AF.Copy)
                self.tt(sq[:, :, 0:r], oT[:, :, 0:r], oT[:, :, 0:r], ALU.mult)
                pss = self.bank()
                self.mm(pss[:, 0:r], [(ones, sq[:, 0, 0:r]), (ones, sq[:, 1, 0:r])])
                self.act(rstdB[:, 0:r], pss[:, 0:r], AF.Ln, scale=1.0 / 256, bias=EPS)
                self.act(rstdB[:, 0:r], rstdB[:, 0:r], AF.Exp, scale=-0.5)
                self.tt(oT[:, :, 0:r], oT[:, :, 0:r], rstdB[:, 0:r].unsqueeze(1).to_broadcast([128, 2, r]), ALU.mult)
                self.tt(oT[:, :, 0:r], oT[:, :, 0:r], rs[:, :, 0:r], ALU.mult)
                self.tt(gT[:, :, 0:r], oT[:, :, 0:r], gnB[:, 2 * hd:2 * hd + 2, 0:r], ALU.mult)
                for nb in range(2):
                    pout = self.bank()
                    self.mm(pout[0:r, :], [(gT[:, vh, 0:r], W["o"][:, vh, nb * 512:nb * 512 + 512]) for vh in range(2)])
                    xs = self.X[0:r, i, nb * 512:nb * 512 + 512]
                    self.tt(xs, xs, pout[0:r, :], ALU.add)
                if not samp:
                    psu = self.bank()
                    self.mm(psu[:, 0:256], [(kend[0:r, :], vbf[0:r, :])])
                    if i == 0:
                        self.cp(Sst[:], psu[:, 0:256])
                    else:
                        self.stt(Sst[:], Sst[:], eb[:, r - 1:r], psu[:, 0:256], ALU.mult, ALU.add)
                    if i == NPT - 1:
                        self.store(self.o["gla_p"][hd], Sst[:], "glap")
                    else:
                        self.act(Sbf[:], Sst[:], AF.Copy)
                else:
                    self.tt(Vexp[0:64], vbf[0:64, :].unsqueeze(1).to_broadcast([64, 16, 256]),
                            selb[0:64, :].unsqueeze(2).to_broadcast([64, 16, 256]), ALU.mult)
                    for pb in range(8):
                        psu = self.bank()
                        self.mm(psu[:, 0:512], [(kend[0:64, :], Vexp[0:64, 2 * pb:2 * pb + 2, :].rearrange("p a b -> p (a b)"))])
                        for b in (2 * pb, 2 * pb + 1):
                            sn = Snew[b % 4]
                            self.stt(sn[:], S0[:, b, :], eb[:, 4 * b + 3:4 * b + 4], psu[:, (b % 2) * 256:(b % 2) * 256 + 256], ALU.mult, ALU.add)
                            self.store(self.o["gla_s"][b, hd], sn[:], "glas%d" % (b % 2))

            def compute():
                for b in range(16):
                    self.load(S0[:, b, :], self.d["st_gla"][b, hd], "gS%d" % (b % 2))
                prev = None
                for idx, (i, c0, r) in enumerate(self.tiles):
                    par = idx % 2
                    self.use_banks([0, 1, 2], [0])
                    L1 = self.record(lambda: part1(i, c0, r, par))
                    if prev is None:
                        self.S.play(L1)
                    else:
                        self.use_banks([3, 4, 5], [1])
                        L2 = self.record(lambda: part2(*prev))
                        self.S.play(self.merge(L1, L2))
                    prev = (i, c0, r, par)
                self.use_banks([3, 4, 5], [1])
                L2 = self.record(lambda: part2(*prev))
                self.S.play(L2)
                self.use_banks(None, None)

            self.stage(loads, compute, la=1)

        for hd in range(4):
            head_stage(hd)

    def mix_ssd(self, l):
        ar = self.ar
        NPT = self.NPT
        S = self.S
        AX = mybir.AxisListType.X
        self.new_phase()
        ar.reset()
        nl, ncmp = self.make_hT(self.d["norm_mix"][l:l + 1, :])
        Wz = ar.alloc([128, 8, 512], BF16)
        Wx = ar.alloc([128, 8, 512], BF16)
        WBC = ar.alloc([128, 8, 256], BF16)
        Wdt = ar.alloc([128, 8, 8], BF16)
        Wo = ar.alloc([128, 4, D], BF16)
        gnB2 = ar.alloc([128, 512], F32)
        cw = ar.alloc([128, 6, 4], F32)
        cb = ar.alloc([128, 6], F32)
        dtbB = ar.alloc([128, 32], F32)
        aB = ar.alloc([128, 32], F32)
        DB = ar.alloc([128, 32], F32)
        selb = ar.alloc([128, 16], F32)
        expand = ar.alloc([128, 4, 128], F32)
        xpre = ar.alloc([128, 6, 131], F32)
        acc = ar.alloc([128, 6, 128], F32)
        seg = ar.alloc([128, 8, 128], F32)
        Mh = ar.alloc([128, 8, 128], BF16)
        extT = ar.alloc([128, 6, 16, 7], F32)
        cst = xpre[:].rearrange("p a b -> p (a b)")[:, 0:768]
        cvs = acc[:].rearrange("p a b -> p (a b)")
        xcbs = [ar.alloc([128, 6, 128], BF16) for _ in range(2)]
        xtmBs = [ar.alloc([128, 640], BF16) for _ in range(2)]
        zss = [ar.alloc([128, 512], F32) for _ in range(2)]
        sms = [ar.alloc([128, 8, 8], F32) for _ in range(2)]
        ybs = [ar.alloc([128, 512], F32) for _ in range(2)]
        tmp = ar.alloc([128, 512], F32)
        yn = ar.alloc([128, 512], BF16)
        ynT = ar.alloc([128, 4, 128], BF16)
        xw = ar.alloc([128, 512], BF16)
        ST = ar.alloc([128, 512], F32)
        STbf = ar.alloc([128, 512], BF16)
        S0nat = [ar.alloc([128, 4, 128], F32) for _ in range(4)]
        ST0bf = [ar.alloc([128, 512], BF16) for _ in range(2)]
        Snew = [ar.alloc([128, 4, 128], F32) for _ in range(2)]
        CTm = ar.alloc([128, 16, 64], BF16)
        Btmb = ar.alloc([128, 16, 128], BF16)
        edT = ar.alloc([128, 16], F32)
        edn = ar.alloc([128, 4, 16], F32)
        csq = self.csq
        ones = csq[:, 3, :]
        identF = csq[:, 0, :]
        win = self.d["ssm_w_in"]
        import os as _os
        CONV_ENG = _os.environ.get("SSD_CONV_ENG", "dve")
        POOL_ENG = _os.environ.get("SSD_POOL_ENG", "pool")

        def loads0():
            nl()
            self.load(dtbB[:], self.d["ssm_dt_bias"][0, :].partition_broadcast(128), "mwA")
            self.load(aB[:], self.d["ssm_a_log"][0, :].partition_broadcast(128), "mwB")
            self.load(DB[:], self.d["ssm_d"][0, :].partition_broadcast(128), "mwC")
            self.load(selb[0:64, :], self.d["c_selb"], "mwD")
            self.load(expand[0:8], self.d["c_expand"], "mwE")

        def compute0():
            ncmp()
            self.act(aB[:], aB[:], AF.Exp)
            self.ts(aB[:], aB[:], -1.0, None, ALU.mult)

        self.stage(loads0, compute0, la=0)

        def group_stage(g):
            xcols = [(512 * g + 128 * c) for c in range(4)] + [2048 + 128 * g, 2560 + 128 * g]
            segs = [(0, 512, 512 * g), (512, 128, 2048 + 128 * g), (640, 128, 2560 + 128 * g)]

            def loads():
                self.load_cast(Wx[:], win[:, 2048 + 512 * g:2048 + 512 * g + 512].rearrange("(k p) n -> p k n", p=128), "sx")
                self.load_cast(WBC[:, :, 0:128], win[:, 4096 + 128 * g:4096 + 128 * g + 128].rearrange("(k p) n -> p k n", p=128), "sb")
                self.load_cast(WBC[:, :, 128:256], win[:, 4608 + 128 * g:4608 + 128 * g + 128].rearrange("(k p) n -> p k n", p=128), "sc")
                self.load_cast(Wdt[:], win[:, 5120 + 8 * g:5120 + 8 * g + 8].rearrange("(k p) n -> p k n", p=128), "sd")
                self.load_cast(Wz[:], win[:, 512 * g:512 * g + 512].rearrange("(k p) n -> p k n", p=128), "sz")
                self.load_cast(Wo[:], self.d["ssm_w_out"][512 * g:512 * g + 512, :].rearrange("(k p) n -> p k n", p=128), "so")
                self.load(gnB2[:], self.d["ssm_norm"][0, 512 * g:512 * g + 512].partition_broadcast(128), "sg")
                for c in range(6):
                    S.dma("sp", cw[:, c, :], self.d["ssm_conv_w"][:, xcols[c]:xcols[c] + 128].rearrange("j p -> p j"), "scw", allow_slow_non_contiguous=True)
                    S.dma("sp", cb[:, c:c + 1], self.d["ssm_conv_b"][0, xcols[c]:xcols[c] + 128].rearrange("(p o) -> p o", o=1), "scb")

            def part1a(i, c0, r, par):
                samp = (i == NPT)
                xcb, xtmB, zs = xcbs[par], xtmBs[par], zss[par]
                hTt = lambda k: self.hT[:, k, c0:c0 + r]
                if samp:
                    for si, (a0, w_, d0) in enumerate(segs):
                        self.load(cst[0:48, a0:a0 + w_], self.d["st_conv"][:, d0:d0 + w_], "scs%d" % si)
                    pcs = self.bank()
                    for c in range(6):
                        self.tr(pcs[:, 48 * c:48 * c + 48], cst[0:48, 128 * c:128 * c + 128], identF[0:48, 0:48], inc=(c == 5))
                    self.act(extT[:, :, :, 0:3], pcs[:, 0:288].rearrange("p (c b j) -> p c b j", c=6, b=16), AF.Copy)
                px1 = self.bank()
                px1v = px1[:].rearrange("p (a b) -> p a b", a=4)
                for c in range(4):
                    self.mm(px1v[:, c, 0:r], [(Wx[:, k, 128 * c:128 * c + 128], hTt(k)) for k in range(8)], last_inc=(c == 3))
                if not samp:
                    self.act(xpre[:, 0:4, 3:3 + r], px1v[:, :, 0:r], AF.Copy)
                else:
                    self.act(extT[:, 0:4, :, 3:7], px1v[:, :, 0:64].rearrange("p c (b t) -> p c b t", t=4), AF.Copy)
                px2 = self.bank()
                px2v = px2[:, 0:256].rearrange("p (a b) -> p a b", a=2)
                for c in range(2):
                    self.mm(px2v[:, c, 0:r], [(WBC[:, k, 128 * c:128 * c + 128], hTt(k)) for k in range(8)], last_inc=(c == 1))
                if not samp:
                    self.act(xpre[:, 4:6, 3:3 + r], px2v[:, :, 0:r], AF.Copy)
                    srcv = lambda c, j: xpre[:, c, j:j + r]
                    accv = lambda c: acc[:, c, 0:r]
                else:
                    self.act(extT[:, 4:6, :, 3:7], px2v[:, :, 0:64].rearrange("p c (b t) -> p c b t", t=4), AF.Copy)
                    srcv = lambda c, j: extT[:, c, :, j:j + 4]
                    accv = lambda c: acc[:, c, 0:64].rearrange("p (b t) -> p b t", t=4)
                pz = self.bank()
                self.mm(pz[0:r, :], [(hTt(k), Wz[:, k, :]) for k in range(8)])
                for c in range(6):
                    self.ts(accv(c), srcv(c, 0), cw[:, c, 0:1], cb[:, c:c + 1], ALU.mult, ALU.add, eng=POOL_ENG)
                for j in range(1, 4):
                    for c in range(6):
                        self.stt(accv(c), srcv(c, j), cw[:, c, j:j + 1], accv(c), ALU.mult, ALU.add)
                if not samp and i < NPT - 1:
                    self.cp(xpre[:, :, 0:3], xpre[:, :, r:r + 3])
                outer = S.rec
                S.rec = []
                self.act(xcb[:, :, 0:r], acc[:, :, 0:r], AF.Silu)
                self.act(zs[0:r, :], pz[0:r, :], AF.Silu)
                grp = S.rec
                S.rec = outer
                S.rec.append(("group", grp))
                tbx = self.tbank()
                for c in range(5):
                    self.tr(tbx[0:r, 128 * c:128 * c + 128], xcb[:, c, 0:r], self.identb[:, :], inc=(c == 4))
                self.act(xtmB[0:r, :], tbx[0:r, 0:640], AF.Copy)

            def part1b(i, c0, r, par):
                samp = (i == NPT)
                TriU = csq[:, 4 if samp else 1, :]
                Neg = csq[:, 7 if samp else 6, :]
                sm = sms[par]
                dtp, dt_, lnd, dA, cl, ecum, wendc, edecB = (sm[:, j, :] for j in range(8))
                hTt = lambda k: self.hT[:, k, c0:c0 + r]
                pd = self.bank()
                self.mm(pd[0:r, 0:8], [(hTt(k), Wdt[:, k, :]) for k in range(8)])
                self.tt(dtp[0:r], pd[0:r, 0:8], dtbB[0:r, 8 * g:8 * g + 8], ALU.add)
                self.act(dtp[0:r], dtp[0:r], AF.Exp)
                self.act(dt_[0:r], dtp[0:r], AF.Ln, bias=1.0)
                self.act(lnd[0:r], dt_[0:r], AF.Ln)
                self.tt(dA[0:r], dt_[0:r], aB[0:r, 8 * g:8 * g + 8], ALU.mult)
                pcm = self.bank()
                self.mm(pcm[0:r, 0:8], [(TriU[0:r, 0:r], dA[0:r])])
                self.tt(cl[0:r], pcm[0:r, 0:8], lnd[0:r], ALU.subtract)
                self.act(ecum[0:r], pcm[0:r, 0:8], AF.Exp)
                self.tt(seg[0:r, :, 0:r], dA[0:r].unsqueeze(2).to_broadcast([r, 8, r]), TriU[0:r, 0:r].unsqueeze(1).to_broadcast([r, 8, r]), ALU.mult, eng=POOL_ENG)
                for half in range(2):
                    pcb = self.bank()
                    pcbv = pcb[:].rearrange("p (a b) -> p a b", a=4)
                    if r == 128:
                        self.mm(pcbv[:, :, 0:r], [(ones[0:r, :], seg[0:r, 4 * half:4 * half + 4, 0:r])])
                    else:
                        for hh in range(4):
                            self.mm(pcbv[:, hh, 0:r], [(ones[0:r, :], seg[0:r, 4 * half + hh, 0:r])], last_inc=(hh == 3))
                    if not samp:
                        self.act(edecB[:, 4 * half:4 * half + 4], pcbv[:, :, r - 1], AF.Exp)
                    self.tt(seg[0:r, 4 * half:4 * half + 4, 0:r], pcbv[0:r, :, 0:r],
                            cl[0:r, 4 * half:4 * half + 4].unsqueeze(2).to_broadcast([r, 4, r]), ALU.subtract)
                self.tt(seg[0:r, :, 0:r], seg[0:r, :, 0:r], Neg[0:r, 0:r].unsqueeze(1).to_broadcast([r, 8, r]), ALU.add)
                self.act(seg[0:r, :, 0:r], seg[0:r, :, 0:r], AF.Exp)
                if not samp:
                    self.cp(wendc[0:r], seg[0:r, :, r - 1])
                else:
                    tmpE = Mh[0:64].rearrange("p a b -> p (a b)").bitcast(F32).rearrange("p (h t) -> p h t", h=8)
                    self.tt(tmpE, seg[0:64, :, 0:64], csq[0:64, 9, 0:64].unsqueeze(1).to_broadcast([64, 8, 64]), ALU.mult)
                    S.op("dve", lambda e: e.tensor_reduce(out=wendc[0:64, :], in_=tmpE, axis=AX, op=ALU.add),
                         reads=[tmpE], writes=[wendc[0:64, :]])

            def part1c(i, c0, r, par):
                samp = (i == NPT)
                xcb, xtmB, yb = xcbs[par], xtmBs[par], ybs[par]
                xtm, BCT = xtmB[:, 0:512], xcb[:, 4:6, :]
                hTt = lambda k: self.hT[:, k, c0:c0 + r]
                pg = self.bank()
                self.mm(pg[0:r, 0:r], [(BCT[:, 0, 0:r], BCT[:, 1, 0:r])])
                self.tt(Mh[0:r, :, 0:r], seg[0:r, :, 0:r], pg[0:r, 0:r].unsqueeze(1).to_broadcast([r, 8, r]), ALU.mult)
                py = self.bank()
                for h in range(8):
                    self.mm(py[0:r, 64 * h:64 * h + 64], [(Mh[0:r, h, 0:r], xtm[0:r, 64 * h:64 * h + 64])], last_inc=(h == 7))
                self.act(yb[0:r, :], py[0:r, :], AF.Copy)
                if i >= NPT - 1:
                    pc1 = self.bank()
                    self.mm(pc1[0:r, :], [(hTt(k), Wx[:, k, :]) for k in range(8)])
                    pc2 = self.bank()
                    self.mm(pc2[0:r, 0:256], [(hTt(k), WBC[:, k, :]) for k in range(8)])
                    self.act(cvs[0:r, 0:512], pc1[0:r, :], AF.Copy)
                    self.act(cvs[0:r, 512:768], pc2[0:r, 0:256], AF.Copy)
                    if not samp:
                        for (a0, w_, d0) in segs:
                            self.store(self.o["conv_p"][:, d0:d0 + w_], cvs[125:128, a0:a0 + w_], "cvp")
                    else:
                        for b in range(16):
                            for (a0, w_, d0) in segs:
                                self.store(self.o["conv_s"][b, :, d0:d0 + w_], cvs[4 * b + 1:4 * b + 4, a0:a0 + w_], "cvs")

            def part2(i, c0, r, par):
                samp = (i == NPT)
                TriU = csq[:, 4 if samp else 1, :]
                xcb, xtmB, zs, sm, yb = xcbs[par], xtmBs[par], zss[par], sms[par], ybs[par]
                xtf, Btm, BCT = xtmB[:, 0:512], xtmB[:, 512:640], xcb[:, 4:6, :]
                dtp, dt_, lnd, dA, cl, ecum, wendc, edecB = (sm[:, j, :] for j in range(8))
                v8 = lambda ap: ap.rearrange("p (h q) -> p h q", h=8)
                self.tt(v8(xw[0:r, :]), v8(xtf[0:r, :]), wendc[0:r].unsqueeze(2).to_broadcast([r, 8, 64]), ALU.mult, eng=POOL_ENG)
                if not samp:
                    if i > 0:
                        pi = self.bank()
                        self.mm(pi[0:r, :], [(BCT[:, 1, 0:r], STbf[:, :])])
                        self.tt(v8(tmp[0:r, :]), v8(pi[0:r, :]), ecum[0:r].unsqueeze(2).to_broadcast([r, 8, 64]), ALU.mult)
                        self.tt(yb[0:r, :], yb[0:r, :], tmp[0:r, :], ALU.add)
                    psu = self.bank()
                    self.mm(psu[:, :], [(Btm[0:r, :], xw[0:r, :])])
                    if i == 0:
                        self.cp(ST[:], psu[:, :])
                    else:
                        self.tt(v8(ST[:]), v8(ST[:]), edecB[:].unsqueeze(2).to_broadcast([128, 8, 64]), ALU.mult)
                        self.tt(ST[:], ST[:], psu[:, :], ALU.add)
                    if i < NPT - 1:
                        self.act(STbf[:], ST[:], AF.Copy)
                    else:
                        pso = self.bank()
                        for c in range(4):
                            self.tr(pso[:, 128 * c:128 * c + 128], ST[:, 128 * c:128 * c + 128], identF, inc=(c == 3))
                        so = Snew[0]
                        self.act(so[:], pso[:].rearrange("p (c n) -> p c n", c=4), AF.Copy)
                        self.store(self.o["ssm_p"][512 * g:512 * g + 512, :].rearrange("(c p) n -> p c n", p=128), so[:], "ssmo0")
                else:
                    self.memset(CTm[:], 0.0)
                    for b in range(16):
                        self.cp(CTm[:, b, 4 * b:4 * b + 4], BCT[:, 1, 4 * b:4 * b + 4])
                    self.tt(Btmb[0:64], Btm[0:64, :].unsqueeze(1).to_broadcast([64, 16, 128]), selb[0:64, :].unsqueeze(2).to_broadcast([64, 16, 128]), ALU.mult)
                    pct = self.bank()
                    self.mm(pct[0:8, 0:64], [(dA[0:64], TriU[0:64, 0:64])])
                    self.act(edT[0:8, :], pct[0:8, 0:64].rearrange("h (b t) -> h b t", t=4)[:, :, 3], AF.Exp)
                    pen = self.bank()
                    for c in range(4):
                        self.mm(pen[:, 16 * c:16 * c + 16], [(expand[0:8, c, :], edT[0:8, :])], last_inc=(c == 3))
                    self.act(edn[:], pen[:, 0:64].rearrange("p (c b) -> p c b", c=4), AF.Copy)
                    pi = self.bank()
                    self.reserved.add(pi.name)
                    def ld_state(b_):
                        self.load(S0nat[b_ % 4][:], self.d["st_ssm"][b_, 512 * g:512 * g + 512, :].rearrange("(c p) n -> p c n", p=128), "ssn%d" % (b_ % 4))
                    for b_ in range(3):
                        ld_state(b_)
                    for b in range(16):
                        sn = S0nat[b % 4]
                        if b + 3 < 16:
                            ld_state(b + 3)
                        pst = self.bank()
                        for c in range(4):
                            self.tr(pst[:, 128 * c:128 * c + 128], sn[:, c, :], identF, inc=(c == 3))
                        sb = ST0bf[b % 2]
                        self.act(sb[:], pst[:, :], AF.Copy)
                        l1, r1 = CTm[:, b, :], sb[:, :]
                        S.op("pe", (lambda e, l1=l1, r1=r1, st_=(b == 0), sp_=(b == 15): e.matmul(pi[0:64, :], l1, r1, start=st_, stop=sp_)),
                             reads=[l1, r1], writes=[pi[0:64, :]], inc=True)
                        psn = self.bank()
                        psnv = psn[:].rearrange("p (c n) -> p c n", c=4)
                        for c in range(4):
                            self.mm(psnv[:, c, :], [(xw[0:64, 128 * c:128 * c + 128], Btmb[0:64, b, :])], last_inc=(c == 3))
                        so = Snew[b % 2]
                        for c in range(4):
                            self.stt(so[:, c, :], sn[:, c, :], edn[:, c, b:b + 1], psnv[:, c, :], ALU.mult, ALU.add)
                        self.S.dma("pool", self.o["ssm_s"][b, 512 * g:512 * g + 512, :].rearrange("(c p) n -> p c n", p=128), so[:], "ssms%d" % (b % 2))
                    self.reserved.discard(pi.name)
                    self.tt(v8(tmp[0:64, :]), v8(pi[0:64, :]), ecum[0:64].unsqueeze(2).to_broadcast([64, 8, 64]), ALU.mult)
                    self.tt(yb[0:64, :], yb[0:64, :], tmp[0:64, :], ALU.add)
                self.tt(v8(tmp[0:r, :]), v8(xtf[0:r, :]), DB[0:r, 8 * g:8 * g + 8].unsqueeze(2).to_broadcast([r, 8, 64]), ALU.mult, eng=POOL_ENG)
                self.tt(yb[0:r, :], yb[0:r, :], tmp[0:r, :], ALU.add, eng=POOL_ENG)
                self.tt(yb[0:r, :], yb[0:r, :], zs[0:r, :], ALU.mult, eng=POOL_ENG)
                ssq = self.stat[0:r, 32:33]
                rsd = self.stat[0:r, 33:34]
                self.act(self.junk[0:r, 0:512], yb[0:r, :], AF.Square, accum_out=ssq)
                self.act(rsd, ssq, AF.Ln, scale=1.0 / 512, bias=EPS)
                self.act(rsd, rsd, AF.Exp, scale=-0.5)
                self.stt(yn[0:r, :], yb[0:r, :], rsd, gnB2[0:r, :], ALU.mult, ALU.mult)
                tb = self.tbank()
                tbv = tb[:, 0:512].rearrange("p (a b) -> p a b", a=4)
                for c in range(4):
                    self.tr(tbv[:, c, 0:r], yn[0:r, 128 * c:128 * c + 128], self.identb[0:r, 0:r], inc=(c == 3))
                self.act(ynT[:, :, 0:r], tbv[:, :, 0:r], AF.Copy)
                for nb in range(2):
                    pout = self.bank()
                    self.mm(pout[0:r, :], [(ynT[:, c, 0:r], Wo[:, c, nb * 512:nb * 512 + 512]) for c in range(4)])
                    xs = self.X[0:r, i, nb * 512:nb * 512 + 512]
                    self.tt(xs, xs, pout[0:r, :], ALU.add)

            def compute():
                self.memset(xpre[:, :, 0:3], 0.0)
                prev = None
                for idx, (i, c0, r) in enumerate(self.tiles):
                    par = idx % 2
                    self.use_banks([0, 1], [0])
                    LA = self.record(lambda: part1a(i, c0, r, par))
                    self.use_banks([2], [0])
                    LB = self.record(lambda: part1b(i, c0, r, par))
                    self.use_banks([0, 1], [0])
                    LC = self.record(lambda: part1c(i, c0, r, par))
                    L1 = self.merge(LA, LB) + LC
                    if prev is None:
                        S.play(L1)
                    else:
                        self.use_banks([3, 4, 5], [1])
                        L2 = self.record(lambda: part2(*prev))
                        S.play(self.merge(L1, L2))
                    prev = (i, c0, r, par)
                self.use_banks([3, 4, 5], [1])
                L2 = self.record(lambda: part2(*prev))
                S.play(L2)
                self.use_banks(None, None)

            self.stage(loads, compute, la=0)

        for g in range(4):
            group_stage(g)


def build_nc(NPT=16, mixers=(0, 1, 2, 3), same_engine_sync=True):
    k = K(NPT, mixers, same_engine_sync)
    return k.build()


WEIGHT_NAMES = ["norm_ffn1", "ffn1_gate", "ffn1_up", "ffn1_down", "norm_mix", "norm_ffn2", "ffn2_gate", "ffn2_up",
                "ffn2_down", "norm_ple", "ple_gate", "ple_proj", "gm_w_in", "gm_w_s", "gm_b_s", "gm_w_out",
                "pool_w", "gla_w_in", "gla_w_a1", "gla_w_a2", "gla_w_out", "ssm_w_in", "ssm_conv_w", "ssm_w_out"]
ROW_NAMES = ["norm_final", "gm_ln", "pool_scale", "gla_b_a", "gla_norm", "ssm_conv_b", "ssm_dt_bias", "ssm_a_log",
             "ssm_d", "ssm_norm"]


def make_in_maps(inputs, NPT, ncores):
    f = lambda a: np.ascontiguousarray(np.asarray(a, dtype=np.float32))
    shared = {k: f(inputs[k]) for k in WEIGHT_NAMES}
    for k in ROW_NAMES:
        shared[k] = f(inputs[k]).reshape(1, -1)
    shared.update(make_consts())
    maps = []
    for c in range(ncores):
        m = dict(shared)
        m["xp"] = f(inputs["x_prompt"][c])
        m["xs"] = f(inputs["x_sample"][16 * c:16 * c + 16]).reshape(64, D)
        m["st_pool"] = f(inputs["state_pool_l1"][16 * c:16 * c + 16]).reshape(240, D)
        m["st_gla"] = f(inputs["state_gla_l2"][16 * c:16 * c + 16])
        m["st_ssm"] = f(inputs["state_ssm_l3"][16 * c:16 * c + 16]).reshape(16, 2048, 128)
        m["st_conv"] = f(inputs["state_conv_l3"][16 * c:16 * c + 16]).reshape(48, 3072)
        m["pp"] = f(inputs["p_prompt"][:, c])
        m["ps"] = f(inputs["p_sample"][:, 16 * c:16 * c + 16]).reshape(4, 64, 256)
        maps.append(m)
    return maps


def gather(results, NPT, ncores):
    TP = NPT * 128
    cat = lambda k, shp: np.concatenate([np.asarray(r[k]).reshape(shp) for r in results], axis=0)
    return (
        cat("yp", (1, TP, D)), cat("ys", (16, 4, D)), cat("chunk_v", (16, 4, D)),
        cat("pool_p", (1, 15, D)), cat("pool_s", (16, 15, D)),
        cat("gla_p", (1, 4, 128, 256)), cat("gla_s", (16, 4, 128, 256)),
        cat("ssm_p", (1, 32, 64, 128)), cat("ssm_s", (16, 32, 64, 128)),
        cat("conv_p", (1, 3, 3072)), cat("conv_s", (16, 3, 3072)),
    )


_NC_CACHE = {}


def kernel(**inputs):
    NPT = 16
    ncores = 8
    if "nc" not in _NC_CACHE:
        _NC_CACHE["nc"] = build_nc(NPT)
    nc = _NC_CACHE["nc"]
    maps = make_in_maps(inputs, NPT, ncores)
    res = run_bass_kernel_spmd(nc, maps, core_ids=list(range(ncores)))
    outs = gather(res.results, NPT, ncores)
    return tuple(np.ascontiguousarray(o, dtype=np.float32) for o in outs)
```

```python
import numpy as np
import concourse.bass as bass
import concourse.mybir as mybir
from concourse.bass_utils import run_bass_kernel_spmd

F32 = mybir.dt.float32
BF16 = mybir.dt.bfloat16
AF = mybir.ActivationFunctionType
ALU = mybir.AluOpType

D = 1024
DFF = 2816
NCH = 8
EPS = 1e-6
NEG = -30000.0


def _esize(dt):
    return 4 if dt == F32 else 2


class Rec:
    __slots__ = ("plo", "phi", "ivals", "lo", "hi", "who", "key")


class Sched:
    ENG = ["pe", "act", "dve", "pool", "sp"]

    def __init__(self, nc, same_engine_sync=True):
        self.nc = nc
        self.sem = {e: nc.alloc_semaphore("sem_" + e) for e in self.ENG}
        self.count = {e: 0 for e in self.ENG}
        self.ops = {e: [] for e in self.ENG}
        self.seen = {e: {} for e in self.ENG}
        self.tens = {}
        self.chan = {}
        self.chan_by_sem = {}
        self.rec = None
        import os as _os
        self.vclock = _os.environ.get("VCLOCK", "1") == "1"
        self.sem_eng = {self.sem[e].num: e for e in self.ENG}
        self.snaps = {e: {} for e in self.ENG}
        self.same = same_engine_sync
        self.nwaits = 0

    def region(self, ap):
        tn = type(ap.tensor).__name__
        if not (tn.startswith("SB") or tn.startswith("PSum")):
            return None
        pat = ap.ap
        es = _esize(ap.dtype)
        pstep, pcnt = pat[0]
        off = int(ap.offset)
        if pstep > 0:
            p0 = off // pstep
            f0 = off % pstep
        else:
            p0 = 0
            f0 = off
        free = [(s, c) for (s, c) in pat[1:] if c > 1 and s != 0]
        free.sort(key=lambda x: x[0])
        run = 1
        rest = []
        for (s, c) in free:
            if s == run:
                run = run * c
            else:
                rest.append((s, c))
        starts = [0]
        nrest = 1
        for (s, c) in rest:
            nrest *= c
        if nrest <= 64:
            for (s, c) in rest:
                starts = [a + s * i for a in starts for i in range(c)]
            ivals = [((f0 + a) * es, (f0 + a + run) * es) for a in starts]
        else:
            ext = run + sum(s * (c - 1) for (s, c) in rest)
            ivals = [(f0 * es, (f0 + ext) * es)]
        r = Rec()
        r.plo = p0
        r.phi = p0 + pcnt
        if tn.startswith("PSum"):
            r.plo, r.phi = 0, 128
            ivals = [(0, 2048)]
        r.ivals = ivals
        r.lo = min(a for a, b in ivals)
        r.hi = max(b for a, b in ivals)
        r.key = (ap.tensor.name, r.plo, r.phi, tuple(ivals))
        return ap.tensor.name, r

    @staticmethod
    def _overlap(a, b):
        if a.hi <= b.lo or b.hi <= a.lo or a.phi <= b.plo or b.phi <= a.plo:
            return False
        for (x0, x1) in a.ivals:
            for (y0, y1) in b.ivals:
                if x0 < y1 and y0 < x1:
                    return True
        return False

    @staticmethod
    def _covers(w, r):
        if not (w.plo <= r.plo and r.phi <= w.phi):
            return False
        for (y0, y1) in r.ivals:
            ok = False
            for (x0, x1) in w.ivals:
                if x0 <= y0 and y1 <= x1:
                    ok = True
                    break
            if not ok:
                return False
        return True

    def _collect(self, reads, writes):
        whos = []
        rr = []
        ww = []
        for ap in reads:
            x = self.region(ap)
            if x is None:
                continue
            name, r = x
            rr.append((name, r))
            t = self.tens.setdefault(name, {"w": [], "r": []})
            for rec in t["w"]:
                if self._overlap(rec, r):
                    whos.append(rec.who)
            if type(ap.tensor).__name__.startswith("PSum"):
                for rec in t["r"]:
                    whos.append(rec.who)
        for ap in writes:
            x = self.region(ap)
            if x is None:
                continue
            name, r = x
            ww.append((name, r))
            t = self.tens.setdefault(name, {"w": [], "r": []})
            for rec in t["w"]:
                if self._overlap(rec, r):
                    whos.append(rec.who)
            for rec in t["r"]:
                if self._overlap(rec, r):
                    whos.append(rec.who)
        return whos, rr, ww

    def _record(self, rr, ww, who):
        for name, r in ww:
            t = self.tens[name]
            t["w"] = [x for x in t["w"] if not self._covers(r, x)]
            t["r"] = [x for x in t["r"] if not self._covers(r, x)]
            r.who = who
            t["w"].append(r)
        for name, r in rr:
            t = self.tens[name]
            r.who = who
            done = False
            if who[0] == "e":
                for x in t["r"]:
                    if x.key == r.key and x.who[0] == "e" and x.who[1] == who[1]:
                        x.who = who
                        done = True
                        break
            if not done:
                t["r"].append(r)

    def _waits(self, eng, whos):
        seen = self.seen[eng]
        best = {}
        for w in whos:
            if w[0] == "e":
                _, e2, seq = w
                if e2 == eng:
                    if eng == "pe" or not self.same:
                        continue
                    assert seq <= self.count[eng], "same-engine dep on pending op"
                sem = self.sem[e2]
                val = seq
            else:
                _, sem, val = w
                val = max(val, self.chan_by_sem[sem.num][1])
            k = sem.num
            if k not in best or best[k][1] < val:
                best[k] = (sem, val)
        waits = []
        items = sorted(best.items(), key=lambda kv: -kv[1][1])
        for k, (sem, val) in items:
            if seen.get(k, 0) >= val:
                continue
            seen[k] = val
            waits.append((sem, val))
            if self.vclock:
                e2 = self.sem_eng.get(k)
                if e2 is not None:
                    snap = self.snaps[e2].get(val)
                    if snap is not None:
                        for kk, vv in snap.items():
                            if seen.get(kk, 0) < vv:
                                seen[kk] = vv
        self.nwaits += len(waits)
        return waits

    def play(self, items):
        for it in items:
            if it[0] == "group":
                self.play(it[1])
            elif it[0] == "op":
                self.op(*it[1:])
            else:
                self.dma(*it[1:-1], **it[-1])

    def op(self, eng, fn, reads=(), writes=(), inc=True):
        if self.rec is not None:
            self.rec.append(("op", eng, fn, list(reads), list(writes), inc))
            return
        whos, rr, ww = self._collect(reads, writes)
        waits = self._waits(eng, whos)
        seq = self.count[eng] + 1
        if inc:
            self.count[eng] = seq
            if self.vclock:
                self.snaps[eng][seq] = dict(self.seen[eng])
        self.ops[eng].append((waits, fn, (self.sem[eng], 1) if inc else None))
        self._record(rr, ww, ("e", eng, seq))

    def dma(self, eng, out, in_, chan, **kw):
        if self.rec is not None:
            self.rec.append(("dma", eng, out, in_, chan, kw))
            return
        if not chan.startswith(eng + "_"):
            chan = eng + "_" + chan
        whos, rr, ww = self._collect([in_], [out])
        waits = self._waits(eng, whos)
        if chan not in self.chan:
            self.chan[chan] = [self.nc.alloc_semaphore("ch_" + chan), 0]
            self.chan_by_sem[self.chan[chan][0].num] = self.chan[chan]
        c = self.chan[chan]
        c[1] += 16
        sem, val = c[0], c[1]
        self.ops[eng].append((waits, lambda e: e.dma_start(out=out, in_=in_, **kw), (sem, 16)))
        self._record(rr, ww, ("d", sem, val))

    def finish(self):
        nc = self.nc
        whos = [("d", c[0], c[1]) for c in self.chan.values()]
        whos += [("e", e, self.count[e]) for e in self.ENG if e != "sp" and self.count[e] > 0]
        waits = self._waits("sp", whos)
        self.ops["sp"].append((waits, None, None))

        import os as _os
        fuse = _os.environ.get("FUSE_WAIT", "1") == "1"

        def replay(e, lst):
            for waits, fn, inc in lst:
                if fn is None:
                    for (sem, val) in waits:
                        e.wait_ge(sem, val)
                    continue
                ws = list(waits)
                last = ws.pop() if (fuse and ws) else None
                for (sem, val) in ws:
                    e.wait_ge(sem, val)
                ins = fn(e)
                if last is not None:
                    ins._wait_ge(last[0], last[1])
                if inc is not None:
                    ins.then_inc(inc[0], inc[1])

        with nc.Block() as block:
            @block.tensor
            def _(e):
                replay(e, self.ops["pe"])

            @block.scalar
            def _(e):
                replay(e, self.ops["act"])

            @block.vector
            def _(e):
                replay(e, self.ops["dve"])

            @block.gpsimd
            def _(e):
                replay(e, self.ops["pool"])

            @block.sync
            def _(e):
                replay(e, self.ops["sp"])


class Arena:
    def __init__(self, t, nbytes):
        self.t = t
        self.n = nbytes
        self.off = 0

    def reset(self, off=0):
        self.off = off

    def alloc(self, shape, dt):
        es = _esize(dt)
        n = 1
        for s in shape[1:]:
            n *= s
        nb = n * es
        self.off = (self.off + 31) // 32 * 32
        assert self.off + nb <= self.n, f"arena overflow {self.off}+{nb}>{self.n}"
        v = self.t[:, self.off // 2:(self.off + nb) // 2]
        self.off += nb
        if dt == F32:
            v = v.bitcast(F32)
        if len(shape) == 3:
            v = v.rearrange("p (a b) -> p a b", a=shape[1])
        elif len(shape) == 4:
            v = v.rearrange("p (a b c) -> p a b c", a=shape[1], b=shape[2])
        if shape[0] < 128:
            v = v[0:shape[0]]
        return v


class Stage:
    def __init__(self, loads, compute, la=2):
        self.loads = loads
        self.compute = compute
        self.la = la


INPUT_SHAPES = None


def in_specs(NPT):
    TP = NPT * 128
    return {
        "xp": [TP, D], "xs": [64, D],
        "st_pool": [240, D], "st_gla": [16, 4, 128, 256], "st_ssm": [16, 2048, 128], "st_conv": [48, 3072],
        "pp": [4, TP, 256], "ps": [4, 64, 256],
        "norm_ffn1": [4, D], "ffn1_gate": [4, D, DFF], "ffn1_up": [4, D, DFF], "ffn1_down": [4, DFF, D],
        "norm_mix": [4, D], "norm_ffn2": [4, D],
        "ffn2_gate": [4, D, DFF], "ffn2_up": [4, D, DFF], "ffn2_down": [4, DFF, D],
        "norm_ple": [4, D], "ple_gate": [4, D, D], "ple_proj": [4, 256, D], "norm_final": [1, D],
        "gm_w_in": [D, 2048], "gm_ln": [1, D], "gm_w_s": [8, 128, 128], "gm_b_s": [8, 128], "gm_w_out": [D, D],
        "pool_w": [4, 256, 256], "pool_scale": [1, D],
        "gla_w_in": [D, 3072], "gla_w_a1": [D, 16], "gla_w_a2": [16, 512], "gla_b_a": [1, 512],
        "gla_norm": [1, D], "gla_w_out": [D, D],
        "ssm_w_in": [D, 5152], "ssm_conv_w": [4, 3072], "ssm_conv_b": [1, 3072], "ssm_dt_bias": [1, 32],
        "ssm_a_log": [1, 32], "ssm_d": [1, 32], "ssm_norm": [1, 2048], "ssm_w_out": [2048, D],
        "c_sq": [128, 10, 128],
        "c_pool": [128, 12, 128], "c_pools": [64, 4, 64], "c_poolh": [120, 4, 2, 64],
        "c_selb": [64, 16], "c_selbt": [128, 16, 64], "c_expand": [8, 4, 128],
    }


def out_specs(NPT):
    TP = NPT * 128
    return {
        "yp": [TP, D], "ys": [64, D], "chunk_v": [64, D], "pool_p": [15, D], "pool_s": [16, 15, D],
        "gla_p": [4, 128, 256], "gla_s": [16, 4, 128, 256], "ssm_p": [2048, 128], "ssm_s": [16, 2048, 128],
        "conv_p": [3, 3072], "conv_s": [16, 3, 3072],
    }


def make_consts():
    c = {}
    sq = np.zeros((128, 10, 128), np.float32)
    s = np.arange(128)[:, None]
    t = np.arange(128)[None, :]
    sq[:, 0] = (s == t)
    sq[:, 1] = (s <= t)
    sq[:, 2] = (s > t)
    sq[:, 3] = 1.0
    same = (s // 4 == t // 4) & (s < 64) & (t < 64)
    sq[:, 4] = same & (s <= t)
    sq[:, 5] = same & (s > t)
    sq[:, 6] = np.where(s <= t, 0.0, NEG)
    sq[:, 7] = np.where(same & (s <= t), 0.0, NEG)
    sq[:, 8] = (s >= t)
    sq[:, 9] = (t == 4 * (s // 4) + 3) & (s < 64)
    c["c_sq"] = sq
    pc = np.zeros((128, 12, 128), np.float32)
    pss = np.zeros((64, 4, 64), np.float32)
    ph = np.zeros((120, 4, 2, 64), np.float32)
    for wi, w in enumerate((2, 4, 8, 16)):
        d = t - s
        pc[:, wi * 3 + 0] = ((d >= 0) & (d < w)) / float(w) - (s == t)
        d2 = t + 128 - s
        pc[:, wi * 3 + 1] = ((d2 >= 0) & (d2 < w)) / float(w)
        cnt = np.minimum(t + 1, w).astype(np.float32)
        pc[:, wi * 3 + 2] = ((d >= 0) & (d < w)) / cnt - (s == t)
        s6 = np.arange(64)[:, None]
        t6 = np.arange(64)[None, :]
        d6 = (t6 % 4) - (s6 % 4)
        pss[:, wi] = ((s6 // 4 == t6 // 4) & (d6 >= 0) & (d6 < w)) / float(w) - (s6 == t6)
        for half in range(2):
            r = np.arange(120)[:, None]
            bb = r // 15 + half * 8
            j = r % 15
            dd = 15 + (t6 % 4) - j
            ph[:, wi, half] = ((bb == t6 // 4) & (dd < w)) / float(w)
    c["c_pool"] = pc
    c["c_pools"] = pss
    c["c_poolh"] = ph
    s6 = np.arange(64)[:, None]
    c["c_selb"] = (s6 // 4 == np.arange(16)[None, :]).astype(np.float32)
    sel = (np.arange(16)[:, None] == (np.arange(64)[None, :] // 4)).astype(np.float32)
    c["c_selbt"] = np.broadcast_to(sel[None], (128, 16, 64)).copy()
    ex = np.zeros((8, 4, 128), np.float32)
    for h in range(8):
        ex[h, h // 2, (h % 2) * 64:(h % 2) * 64 + 64] = 1.0
    c["c_expand"] = ex
    return c


class K:
    def __init__(self, NPT=16, mixers=(0, 1, 2, 3), same_engine_sync=True):
        self.NPT = NPT
        self.NT = NPT + 1
        self.T = NPT * 128 + 64
        self.mixers = mixers
        self.tiles = [(i, i * 128, 128) for i in range(NPT)] + [(NPT, NPT * 128, 64)]
        self.blocks = [(c0, min(512, NPT * 128 - c0)) for c0 in range(0, NPT * 128, 512)] + [(NPT * 128, 64)]
        nc = bass.Bass("TRN2", target_bir_lowering=False)
        self.nc = nc
        self.d = {k: nc.dram_tensor(k, v, F32, kind="ExternalInput").ap() for k, v in in_specs(NPT).items()}
        self.o = {k: nc.dram_tensor(k, v, F32, kind="ExternalOutput").ap() for k, v in out_specs(NPT).items()}
        self.S = Sched(nc, same_engine_sync)
        self.stages = []
        self.reserved = set()
        self._bset = None
        self._tbset = None
        self._bctr = {}
        self.phase_start = 0
        self._bank = 0
        self._tbank = 0
        self._gb = 0
        self._uid = 0

    def use_banks(self, idx, tidx):
        self._bset = idx
        self._tbset = tidx

    def bank(self):
        lst = self.banks if self._bset is None else [self.banks[j] for j in self._bset]
        key = "all" if self._bset is None else tuple(self._bset)
        while True:
            n = self._bctr.get(key, 0)
            self._bctr[key] = n + 1
            b = lst[n % len(lst)]
            if b.name not in self.reserved:
                return b

    def tbank(self):
        lst = self.tbanks if self._tbset is None else [self.tbanks[j] for j in self._tbset]
        key = "tall" if self._tbset is None else ("t",) + tuple(self._tbset)
        n = self._bctr.get(key, 0)
        self._bctr[key] = n + 1
        return lst[n % len(lst)]

    def record(self, f):
        assert self.S.rec is None
        self.S.rec = []
        f()
        lst = self.S.rec
        self.S.rec = None
        return lst

    @staticmethod
    def merge(a, b):
        out = []
        na, nb = len(a), len(b)
        ia = ib = 0
        while ia < na or ib < nb:
            if ib >= nb or (ia < na and ia * nb <= ib * na):
                out.append(a[ia])
                ia += 1
            else:
                out.append(b[ib])
                ib += 1
        return out

    def mm(self, ps, pairs, last_inc=True):
        n = len(pairs)
        for i, (l, r) in enumerate(pairs):
            self.S.op("pe", (lambda e, l=l, r=r, st=(i == 0), sp=(i == n - 1): e.matmul(ps, l, r, start=st, stop=sp)),
                      reads=[l, r], writes=[ps], inc=(i == n - 1) and last_inc)

    def tr(self, ps, in_, ident, inc):
        if in_.dtype == F32:
            self.mm(ps, [(in_, ident)], last_inc=inc)
            return
        self.S.op("pe", lambda e: e.transpose(out=ps, in_=in_, identity=ident), reads=[in_, ident], writes=[ps], inc=inc)

    def act(self, out, in_, func, reads=None, **kw):
        rd = [in_] + [v for v in kw.values() if hasattr(v, "ap")]
        wr = [out]
        if "accum_out" in kw:
            wr.append(kw["accum_out"])
            rd = [in_] + [v for k, v in kw.items() if hasattr(v, "ap") and k != "accum_out"]
        self.S.op("act", lambda e: e.activation(out=out, in_=in_, func=func, **kw), reads=rd, writes=wr)

    def tt(self, out, in0, in1, op, eng="dve"):
        self.S.op(eng, lambda e: e.tensor_tensor(out=out, in0=in0, in1=in1, op=op), reads=[in0, in1], writes=[out])

    def ts(self, out, in0, s1, s2, op0, op1=None, eng="dve"):
        rd = [in0] + [x for x in (s1, s2) if hasattr(x, "ap")]
        if op1 is None:
            self.S.op(eng, lambda e: e.tensor_scalar(out=out, in0=in0, scalar1=s1, scalar2=None, op0=op0), reads=rd, writes=[out])
        else:
            self.S.op(eng, lambda e: e.tensor_scalar(out=out, in0=in0, scalar1=s1, scalar2=s2, op0=op0, op1=op1), reads=rd, writes=[out])

    def stt(self, out, in0, scalar, in1, op0, op1, eng="dve"):
        rd = [in0, in1] + ([scalar] if hasattr(scalar, "ap") else [])
        self.S.op(eng, lambda e: e.scalar_tensor_tensor(out=out, in0=in0, scalar=scalar, in1=in1, op0=op0, op1=op1), reads=rd, writes=[out])

    def cp(self, out, in_, eng="dve"):
        self.S.op(eng, lambda e: e.tensor_copy(out=out, in_=in_), reads=[in_], writes=[out])

    def memset(self, out, val, eng="dve"):
        self.S.op(eng, lambda e: e.memset(out, val), reads=[], writes=[out])

    def uid(self, p):
        self._uid += 1
        return f"{p}{self._uid}"

    def load_cast(self, out, in_, chan):
        self.S.dma("pool", out, in_, chan)

    def load(self, out, in_, chan):
        self.S.dma("sp", out, in_, chan)

    def store(self, out, in_, chan):
        self.S.dma("sp", out, in_, chan)

    def build(self):
        nc = self.nc
        NT = self.NT
        T = self.T
        ARENA = 93184
        with (
            nc.sbuf_tensor("X", [128, NT, D], F32) as X,
            nc.sbuf_tensor("hT", [128, NCH, T], BF16) as hT,
            nc.sbuf_tensor("arena", [128, ARENA // 2], BF16) as arena,
            nc.sbuf_tensor("csq", [128, 10, 128], F32) as csq,
            nc.sbuf_tensor("identb", [128, 128], BF16) as identb,
            nc.sbuf_tensor("gB", [128, 1, D], F32) as gB,
            nc.sbuf_tensor("hn", [128, 2, D], BF16) as hn,
            nc.sbuf_tensor("junk", [128, D], BF16) as junk,
            nc.sbuf_tensor("stat", [128, 64], F32) as stat,
            nc.psum_tensor("B0", [128, 512], F32) as B0,
            nc.psum_tensor("B1", [128, 512], F32) as B1,
            nc.psum_tensor("B2", [128, 512], F32) as B2,
            nc.psum_tensor("B3", [128, 512], F32) as B3,
            nc.psum_tensor("B4", [128, 512], F32) as B4,
            nc.psum_tensor("B5", [128, 512], F32) as B5,
            nc.psum_tensor("T0", [128, 1024], BF16) as T0,
            nc.psum_tensor("T1", [128, 1024], BF16) as T1,
        ):
            self.X, self.hT, self.csq, self.identb, self.gB, self.hn, self.junk, self.stat = X, hT, csq, identb, gB, hn, junk, stat
            self.banks = [B0, B1, B2, B3, B4, B5]
            self.tbanks = [T0, T1]
            self.ar = Arena(arena, ARENA)
            self.prologue()
            for l in range(4):
                self.ffn(l, 1)
                self.mixer(l)
                self.ffn(l, 2)
                self.ple(l)
            self.final()
            self.emit()
            self.S.finish()
        return nc

    def emit(self):
        st = self.stages
        n = len(st)
        at = {}
        for i, s in enumerate(st):
            at.setdefault(max(0, i - s.la), []).append(i)
        for j in range(n):
            for i in at.get(j, []):
                if st[i].loads is not None:
                    st[i].loads()
            st[j].compute()

    def stage(self, loads, compute, la=2):
        i = len(self.stages)
        la = max(0, min(la, i - self.phase_start))
        self.stages.append(Stage(loads, compute, la))

    def new_phase(self):
        self.phase_start = len(self.stages)

    def prologue(self):
        def loads():
            self.load(self.csq[:], self.d["c_sq"][:], "csq")
            self.load_cast(self.identb[:], self.d["c_sq"][:, 0, :], "identb")
            for (i, c0, r) in self.tiles:
                src = self.d["xp"][c0:c0 + r, :] if i < self.NPT else self.d["xs"][:, :]
                self.load(self.X[0:r, i, :], src, "X%d" % (i * 4 // self.NT))

        self.stage(loads, lambda: None, la=0)

    def make_hT(self, grow, extra=None):
        slot = 0
        gBs = self.gB[:, slot, :]

        def loads():
            self.load(gBs, grow.partition_broadcast(128), "gB%d" % slot)

        def compute():
            X, hT, stat = self.X, self.hT, self.stat

            def stA(i, c0, r):
                hs = self.hn[0:r, i % 2, :]
                ss = stat[0:r, (i % 4) * 2:(i % 4) * 2 + 1]
                rs = stat[0:r, (i % 4) * 2 + 1:(i % 4) * 2 + 2]
                self.act(self.junk[0:r, :], X[0:r, i, :], AF.Square, accum_out=ss)
                self.act(rs, ss, AF.Ln, scale=1.0 / D, bias=EPS)
                self.act(rs, rs, AF.Exp, scale=-0.5)
                self.stt(hs, X[0:r, i, :], rs, gBs[0:r, :], ALU.mult, ALU.mult)
                if extra is not None:
                    extra(i, c0, r, rs, gBs)

            def stB(i, c0, r):
                hs = self.hn[0:r, i % 2, :]
                tb = self.tbank()
                tbv = tb[:].rearrange("p (a b) -> p a b", a=8)
                for k in range(8):
                    self.tr(tbv[:, k, 0:r], hs[:, 128 * k:128 * k + 128], self.identb[0:r, 0:r], inc=(k == 7))
                self.cp(hT[:, :, c0:c0 + r], tbv[:, :, 0:r])

            n = len(self.tiles)
            for idx in range(n + 1):
                if idx < n:
                    stA(*self.tiles[idx])
                if idx >= 1:
                    stB(*self.tiles[idx - 1])

        return loads, compute

    def ffn(self, l, which):
        S = self.S
        Wg = self.d["ffn%d_gate" % which][l]
        Wu = self.d["ffn%d_up" % which][l]
        Wd = self.d["ffn%d_down" % which][l]
        self.new_phase()
        nl, ncmp = self.make_hT(self.d["norm_ffn%d" % which][l:l + 1, :])
        ar = self.ar
        ar.reset()
        NSLOT = 4
        gu = [ar.alloc([128, 2, 8, 128], BF16) for _ in range(NSLOT)]
        wd = ar.alloc([128, 6, D], BF16)
        aT = ar.alloc([128, 6, self.T], BF16)
        sil = [ar.alloc([128, 512], F32) for _ in range(2)]
        quarters = [list(range(0, 6)), list(range(6, 11)), list(range(11, 17)), list(range(17, 22))]
        first = [True]
        cnt = [0]

        def gu_stage(j, jj, la):
            slot = cnt[0] % NSLOT
            cnt[0] += 1
            g = gu[slot]

            def loads():
                self.load_cast(g[:, 0], Wg[:, 128 * j:128 * j + 128].rearrange("(k p) n -> p k n", p=128), "gug%d" % slot)
                self.load_cast(g[:, 1], Wu[:, 128 * j:128 * j + 128].rearrange("(k p) n -> p k n", p=128), "guu%d" % slot)

            def compute():
                for bi, (c0, bs) in enumerate(self.blocks):
                    pg = self.bank()
                    pu = self.bank()
                    self.mm(pg[:, 0:bs], [(g[:, 0, k, :], self.hT[:, k, c0:c0 + bs]) for k in range(8)])
                    self.mm(pu[:, 0:bs], [(g[:, 1, k, :], self.hT[:, k, c0:c0 + bs]) for k in range(8)])
                    st = sil[bi % 2]
                    self.act(st[:, 0:bs], pg[:, 0:bs], AF.Silu)
                    self.tt(aT[:, jj, c0:c0 + bs], st[:, 0:bs], pu[:, 0:bs], ALU.mult)

            self.stage(loads, compute, la)

        def down_stage(q):
            nq = len(q)

            def loads():
                self.load_cast(wd[:, 0:nq, :], Wd[128 * q[0]:128 * (q[-1] + 1), :].rearrange("(j p) n -> p j n", p=128), "wd")

            def compute():
                for (i, c0, r) in self.tiles:
                    for nb in range(2):
                        ps = self.bank()
                        self.mm(ps[0:r, :], [(aT[:, jj, c0:c0 + r], wd[:, jj, nb * 512:nb * 512 + 512]) for jj in range(nq)])
                        xs = self.X[0:r, i, nb * 512:nb * 512 + 512]
                        self.stt(xs, ps[0:r, :], 0.5, xs, ALU.mult, ALU.add)

            self.stage(loads, compute, 2)

        self.stage(nl, ncmp, la=0)
        for qi, q in enumerate(quarters):
            for jj, j in enumerate(q):
                gu_stage(j, jj, 2)
            down_stage(q)

    def ple(self, l):
        ar = self.ar
        self.new_phase()
        nl, ncmp = self.make_hT(self.d["norm_ple"][l:l + 1, :])
        ar.reset()
        Wpg = ar.alloc([128, 8, D], BF16)
        Wpp = ar.alloc([128, 2, D], BF16)
        ptok = ar.alloc([128, self.NT, 256], BF16)
        pT = ar.alloc([128, 2, self.T], BF16)
        sig = [ar.alloc([128, 512], F32) for _ in range(2)]
        tmp = [ar.alloc([128, 512], F32) for _ in range(2)]
        NPT = self.NPT

        def loads():
            nl()
            self.load_cast(ptok[:, 0:NPT, :], self.d["pp"][l].rearrange("(i p) c -> p i c", p=128), "ptok")
            self.load_cast(ptok[0:64, NPT, :], self.d["ps"][l], "ptoks")
            self.load_cast(Wpp[:], self.d["ple_proj"][l].rearrange("(k p) n -> p k n", p=128), "Wpp")
            self.load_cast(Wpg[:], self.d["ple_gate"][l].rearrange("(k p) n -> p k n", p=128), "Wpg")

        def compute():
            ncmp()
            for (i, c0, r) in self.tiles:
                tb = self.tbank()
                for k in range(2):
                    self.tr(tb[:, k * 128:k * 128 + r], ptok[0:r, i, 128 * k:128 * k + 128], self.identb[0:r, 0:r], inc=(k == 1))
                self.act(pT[:, :, c0:c0 + r], tb[:, 0:256].rearrange("p (a b) -> p a b", a=2)[:, :, 0:r], AF.Copy)
            for (i, c0, r) in self.tiles:
                for nb in range(2):
                    pg = self.bank()
                    pp_ = self.bank()
                    self.mm(pg[0:r, :], [(self.hT[:, k, c0:c0 + r], Wpg[:, k, nb * 512:nb * 512 + 512]) for k in range(8)])
                    self.mm(pp_[0:r, :], [(pT[:, k, c0:c0 + r], Wpp[:, k, nb * 512:nb * 512 + 512]) for k in range(2)])
                    sg = sig[nb]
                    tm = tmp[nb]
                    self.act(sg[0:r, :], pg[0:r, :], AF.Sigmoid)
                    self.tt(tm[0:r, :], sg[0:r, :], pp_[0:r, :], ALU.mult)
                    xs = self.X[0:r, i, nb * 512:nb * 512 + 512]
                    self.tt(xs, xs, tm[0:r, :], ALU.add)

        self.stage(loads, compute, la=0)

    def final(self):
        ar = self.ar
        self.new_phase()
        ar.reset()
        yb = [ar.alloc([128, D], F32) for _ in range(2)]
        slot = 0
        gBs = self.gB[:, slot, :]

        def loads():
            self.load(gBs, self.d["norm_final"][0:1, :].partition_broadcast(128), "gB%d" % slot)

        def compute():
            X, stat = self.X, self.stat
            for (i, c0, r) in self.tiles:
                ss = stat[0:r, (i % 4) * 2:(i % 4) * 2 + 1]
                rs = stat[0:r, (i % 4) * 2 + 1:(i % 4) * 2 + 2]
                self.act(self.junk[0:r, :], X[0:r, i, :], AF.Square, accum_out=ss)
                self.act(rs, ss, AF.Ln, scale=1.0 / D, bias=EPS)
                self.act(rs, rs, AF.Exp, scale=-0.5)
                y = yb[i % 2]
                self.stt(y[0:r, :], X[0:r, i, :], rs, gBs[0:r, :], ALU.mult, ALU.mult)
                dst = self.o["yp"][c0:c0 + r, :] if i < self.NPT else self.o["ys"][:, :]
                self.store(dst, y[0:r, :], "yb%d" % (i % 2))

        self.stage(loads, compute, la=0)

    def mixer(self, l):
        if l not in self.mixers:
            return
        [self.mix_gmlp, self.mix_pool, self.mix_gla, self.mix_ssd][l](l)

    def mix_gmlp(self, l):
        ar = self.ar
        self.new_phase()
        nl, ncmp = self.make_hT(self.d["norm_mix"][l:l + 1, :])
        ar.reset()
        Wu = ar.alloc([128, 8, D], BF16)
        Wv = ar.alloc([128, 8, D], BF16)
        Wo = ar.alloc([128, 8, D], BF16)
        wsbf = ar.alloc([128, 8, 128], BF16)
        WsT = ar.alloc([128, 8, 128], BF16)
        WsS = ar.alloc([128, 8, 64], BF16)
        bsB = ar.alloc([128, 8, 128], F32)
        bsS = ar.alloc([128, 8, 64], F32)
        gln = ar.alloc([128, D], F32)
        mark = ar.off
        wsnat = ar.alloc([128, 8, 128], F32)
        ar.reset(mark)
        uTs = [ar.alloc([128, 8, 128], F32) for _ in range(2)]
        vt = ar.alloc([128, D], F32)
        vtmp = ar.alloc([128, D], F32)
        vbfs = [ar.alloc([128, D], BF16) for _ in range(2)]
        gT = ar.alloc([128, 8, 128], BF16)
        svt = ar.alloc([128, 4, 128], F32)
        st = self.stat
        gw = self.d["gm_w_in"]

        def loads():
            nl()
            self.load_cast(Wu[:], gw[:, 0:D].rearrange("(k p) n -> p k n", p=128), "mwA")
            self.load_cast(Wv[:], gw[:, D:2 * D].rearrange("(k p) n -> p k n", p=128), "mwB")
            self.load_cast(Wo[:], self.d["gm_w_out"].rearrange("(k p) n -> p k n", p=128), "mwC")
            self.load(wsnat[:], self.d["gm_w_s"].rearrange("g t s -> t g s"), "mwD")
            self.load(bsB[:].rearrange("p a b -> p (a b)"), self.d["gm_b_s"].rearrange("g t -> (g t)").partition_broadcast(128), "mwE")
            self.load(gln[:], self.d["gm_ln"][0, :].partition_broadcast(128), "mwF")

        def compute():
            ncmp()
            self.tt(wsbf[:], wsnat[:], self.csq[:, 8, :].unsqueeze(1).to_broadcast([128, 8, 128]), ALU.mult)
            tb = self.tbank()
            tbv = tb[:].rearrange("p (a b) -> p a b", a=8)
            for g in range(8):
                self.tr(tbv[:, g, :], wsbf[:, g, :], self.identb[:], inc=(g == 7))
            self.act(WsT[:], tbv[:], AF.Copy)
            self.memset(WsS[:], 0.0)
            for b in range(16):
                self.S.dma("sp", WsS[4 * b:4 * b + 4, :, 4 * b:4 * b + 4], WsT[0:4, :, 0:4], "wss")
            self.cp(bsS[:].rearrange("p g (b t) -> p g b t", b=16), bsB[:, :, 0:4].unsqueeze(2).to_broadcast([128, 8, 16, 4]))
            def part1(i, c0, r, par):
                samp = (i == self.NPT)
                uT, vbf = uTs[par], vbfs[par]
                for half in range(2):
                    pu = self.bank()
                    puv = pu[:].rearrange("p (a b) -> p a b", a=4)
                    for mm_ in range(4):
                        m = half * 4 + mm_
                        self.mm(puv[:, mm_, 0:r], [(Wu[:, k, 128 * m:128 * m + 128], self.hT[:, k, c0:c0 + r]) for k in range(8)], last_inc=(mm_ == 3))
                    self.act(uT[:, half * 4:half * 4 + 4, 0:r], puv[:, :, 0:r], AF.Gelu_apprx_tanh)
                for nb in range(2):
                    pv = self.bank()
                    self.mm(pv[0:r, :], [(self.hT[:, k, c0:c0 + r], Wv[:, k, nb * 512:nb * 512 + 512]) for k in range(8)])
                    self.act(vt[0:r, nb * 512:nb * 512 + 512], pv[0:r, :], AF.Gelu_apprx_tanh, accum_out=st[0:r, 16 + nb:17 + nb])
                self.act(self.junk[0:r, :], vt[0:r, :], AF.Square, accum_out=st[0:r, 18:19])
                self.tt(st[0:r, 19:20], st[0:r, 16:17], st[0:r, 17:18], ALU.add)
                self.ts(st[0:r, 20:21], st[0:r, 19:20], 1.0 / D, None, ALU.mult)
                self.tt(st[0:r, 21:22], st[0:r, 20:21], st[0:r, 20:21], ALU.mult)
                self.stt(st[0:r, 22:23], st[0:r, 18:19], 1.0 / D, st[0:r, 21:22], ALU.mult, ALU.subtract)
                self.act(st[0:r, 23:24], st[0:r, 22:23], AF.Ln, bias=EPS)
                self.act(st[0:r, 23:24], st[0:r, 23:24], AF.Exp, scale=-0.5)
                self.ts(vtmp[0:r, :], vt[0:r, :], st[0:r, 20:21], st[0:r, 23:24], ALU.subtract, ALU.mult)
                if samp:
                    self.tt(vt[0:r, :], vtmp[0:r, :], gln[0:r, :], ALU.mult)
                    self.store(self.o["chunk_v"][:, :], vt[0:r, :], "cv")
                    self.cp(vbf[0:r, :], vt[0:r, :])
                else:
                    self.tt(vbf[0:r, :], vtmp[0:r, :], gln[0:r, :], ALU.mult)

            def part2(i, c0, r, par):
                samp = (i == self.NPT)
                uT, vbf = uTs[par], vbfs[par]
                Wmix = WsS if samp else WsT
                bias = bsS if samp else bsB
                for half in range(2):
                    psv = self.bank()
                    pv4 = psv[:].rearrange("p (a b) -> p a b", a=4)
                    for gg in range(4):
                        g = half * 4 + gg
                        self.mm(pv4[:, gg, 0:r], [(vbf[0:r, 128 * g:128 * g + 128], Wmix[0:r, g, 0:r])], last_inc=(gg == 3))
                    self.tt(svt[:, :, 0:r], pv4[:, :, 0:r], bias[:, half * 4:half * 4 + 4, 0:r], ALU.add)
                    self.tt(gT[:, half * 4:half * 4 + 4, 0:r], svt[:, :, 0:r], uT[:, half * 4:half * 4 + 4, 0:r], ALU.mult)
                for nb in range(2):
                    po = self.bank()
                    self.mm(po[0:r, :], [(gT[:, m, 0:r], Wo[:, m, nb * 512:nb * 512 + 512]) for m in range(8)])
                    xs = self.X[0:r, i, nb * 512:nb * 512 + 512]
                    self.tt(xs, xs, po[0:r, :], ALU.add)

            prev = None
            for idx, (i, c0, r) in enumerate(self.tiles):
                par = idx % 2
                self.use_banks([0, 1, 2], [0])
                L1 = self.record(lambda: part1(i, c0, r, par))
                if prev is None:
                    self.S.play(L1)
                else:
                    self.use_banks([3, 4, 5], [1])
                    L2 = self.record(lambda: part2(*prev))
                    self.S.play(self.merge(L1, L2))
                prev = (i, c0, r, par)
            self.use_banks([3, 4, 5], [1])
            L2 = self.record(lambda: part2(*prev))
            self.S.play(L2)
            self.use_banks(None, None)

        self.stage(loads, compute, la=0)

    def mix_pool(self, l):
        ar = self.ar
        NPT = self.NPT
        self.new_phase()
        ar.reset()
        HN = ar.alloc([128, self.NT, D], BF16)
        Wp = ar.alloc([128, 4, 2, 256], BF16)
        PC = ar.alloc([128, 12, 128], BF16)
        PS = ar.alloc([128, 4, 64], BF16)
        PH = ar.alloc([128, 4, 2, 64], BF16)
        HB = ar.alloc([128, 2, D], BF16)
        scB = ar.alloc([128, D], F32)
        hf = ar.alloc([128, 2, D], F32)
        diffT = [ar.alloc([128, 8, 128], BF16) for _ in range(2)]
        tmp = [ar.alloc([128, 512], F32) for _ in range(2)]

        def extra(i, c0, r, rs, gBs):
            self.cp(HN[0:r, i, :], self.hn[0:r, i % 2, :], eng="pool")
            if i >= NPT - 1:
                self.stt(hf[0:r, i - (NPT - 1), :], self.X[0:r, i, :], rs, gBs[0:r, :], ALU.mult, ALU.mult)

        nl, ncmp = self.make_hT(self.d["norm_mix"][l:l + 1, :], extra=extra)

        def loads():
            nl()
            self.load_cast(Wp[:], self.d["pool_w"].rearrange("g (cc p) d -> p g cc d", p=128), "mwA")
            self.load_cast(PC[:], self.d["c_pool"], "mwB")
            self.load_cast(PS[0:64], self.d["c_pools"], "mwC")
            self.load_cast(PH[0:120], self.d["c_poolh"], "mwD")
            self.load_cast(HB[0:120], self.d["st_pool"].rearrange("(h r) d -> r h d", h=2), "mwE")
            self.load(scB[:], self.d["pool_scale"][0, :].partition_broadcast(128), "mwF")
            self.S.dma("sp", self.o["pool_s"][:, 0:11, :], self.d["st_pool"].rearrange("(b j) d -> b j d", j=15)[:, 4:15, :], "poolhist")

        def compute():
            ncmp()
            self.store(self.o["pool_p"][:, :], hf[113:128, 0, :], "poolp")
            for b in range(16):
                self.store(self.o["pool_s"][b, 11:15, :], hf[4 * b:4 * b + 4, 1, :], "pools")
            for (i, c0, r) in self.tiles:
                samp = (i == NPT)
                dT = diffT[i % 2]
                for half in range(2):
                    pb = self.bank()
                    pbv = pb[:].rearrange("p (a b) -> p a b", a=4)
                    for mm_ in range(4):
                        m = half * 4 + mm_
                        w = m // 2
                        fs = slice(128 * m, 128 * m + 128)
                        if samp:
                            pairs = [(HN[0:64, i, fs], PS[0:64, w, :]), (HB[0:120, 0, fs], PH[0:120, w, 0, :]), (HB[0:120, 1, fs], PH[0:120, w, 1, :])]
                        else:
                            pairs = [(HN[0:r, i, fs], PC[0:r, w * 3 + (2 if i == 0 else 0), 0:r])]
                            if i > 0:
                                pairs.append((HN[64:128, i - 1, fs], PC[64:128, w * 3 + 1, 0:r]))
                        self.mm(pbv[:, mm_, 0:r], pairs, last_inc=(mm_ == 3))
                    self.act(dT[:, half * 4:half * 4 + 4, 0:r], pbv[:, :, 0:r], AF.Copy)
                for nb in range(2):
                    po = self.bank()
                    for gg in range(2):
                        g = nb * 2 + gg
                        self.mm(po[0:r, gg * 256:gg * 256 + 256], [(dT[:, 2 * g + cc, 0:r], Wp[:, g, cc, :]) for cc in range(2)], last_inc=(gg == 1))
                    tm = tmp[nb]
                    self.tt(tm[0:r, :], po[0:r, :], scB[0:r, nb * 512:nb * 512 + 512], ALU.mult)
                    xs = self.X[0:r, i, nb * 512:nb * 512 + 512]
                    self.tt(xs, xs, tm[0:r, :], ALU.add)

        self.stage(loads, compute, la=0)

    def mix_gla(self, l):
        ar = self.ar
        NPT = self.NPT
        T = self.T
        self.new_phase()
        ar.reset()
        nl, ncmp = self.make_hT(self.d["norm_mix"][l:l + 1, :])
        Wa1 = ar.alloc([128, 8, 16], BF16)
        Wa2 = ar.alloc([128, 512], BF16)
        baB = ar.alloc([128, 512], F32)
        gnT = ar.alloc([128, 8], F32)
        gnB = ar.alloc([128, 8, 128], F32)
        selb = ar.alloc([128, 16], F32)
        t1 = ar.alloc([128, T], BF16)
        Ws = [dict(q=ar.alloc([128, 8, 128], BF16), k=ar.alloc([128, 8, 128], BF16), v=ar.alloc([128, 8, 256], BF16),
                   r=ar.alloc([128, 8, 256], BF16), o=ar.alloc([128, 2, D], BF16)) for _ in range(2)]
        qds = [ar.alloc([128, 128], BF16) for _ in range(2)]
        ki = ar.alloc([128, 128], BF16)
        rss = [ar.alloc([128, 2, 128], F32) for _ in range(2)]
        vbfs = [ar.alloc([128, 256], BF16) for _ in range(2)]
        zb = ar.alloc([128, 128], F32)
        lp = ar.alloc([128, 128], F32)
        ebs = [ar.alloc([128, 128], F32) for _ in range(2)]
        einv = ar.alloc([128, 128], F32)
        ee = ar.alloc([128, 128], F32)
        kends = [ar.alloc([128, 128], BF16) for _ in range(2)]
        scs = [ar.alloc([128, 128], BF16) for _ in range(2)]
        oT = ar.alloc([128, 2, 128], F32)
        sq = ar.alloc([128, 2, 128], F32)
        rstdB = ar.alloc([128, 128], F32)
        gT = ar.alloc([128, 2, 128], BF16)
        Sst = ar.alloc([128, 256], F32)
        Sbf = ar.alloc([128, 256], BF16)
        S0 = ar.alloc([128, 16, 256], F32)
        S0bf = [ar.alloc([128, 256], BF16) for _ in range(4)]
        Snew = [ar.alloc([128, 256], F32) for _ in range(4)]
        Vexp = ar.alloc([128, 16, 256], BF16)
        csq = self.csq
        ones = csq[:, 3, :]
        win = self.d["gla_w_in"]

        def loads0():
            nl()
            self.load_cast(Wa1[:], self.d["gla_w_a1"].rearrange("(k p) n -> p k n", p=128), "mwA")
            self.load_cast(Wa2[0:16, :], self.d["gla_w_a2"], "mwB")
            self.load(baB[:], self.d["gla_b_a"][0, :].partition_broadcast(128), "mwC")
            self.S.dma("sp", gnT[:], self.d["gla_norm"][0, :].rearrange("(m p) -> p m", p=128), "mwD", allow_slow_non_contiguous=True)
            self.load(selb[0:64, :], self.d["c_selb"], "mwE")

        def compute0():
            ncmp()
            for (c0, bs) in self.blocks:
                pt = self.bank()
                self.mm(pt[0:16, 0:bs], [(Wa1[:, k, :], self.hT[:, k, c0:c0 + bs]) for k in range(8)])
                self.act(t1[0:16, c0:c0 + bs], pt[0:16, 0:bs], AF.Copy)
            self.cp(gnB[:], gnT[:].unsqueeze(2).to_broadcast([128, 8, 128]))

        self.stage(loads0, compute0, la=0)

        def head_stage(hd):
            W = Ws[hd % 2]

            def loads():
                self.load_cast(W["q"][:], win[:, hd * 128:hd * 128 + 128].rearrange("(k p) n -> p k n", p=128), "gq%d" % (hd % 2))
                self.load_cast(W["k"][:], win[:, 512 + hd * 128:512 + hd * 128 + 128].rearrange("(k p) n -> p k n", p=128), "gk%d" % (hd % 2))
                self.load_cast(W["v"][:], win[:, 1024 + hd * 256:1024 + hd * 256 + 256].rearrange("(k p) n -> p k n", p=128), "gv%d" % (hd % 2))
                self.load_cast(W["r"][:], win[:, 2048 + hd * 256:2048 + hd * 256 + 256].rearrange("(k p) n -> p k n", p=128), "gr%d" % (hd % 2))
                self.load_cast(W["o"][:], self.d["gla_w_out"][hd * 256:hd * 256 + 256, :].rearrange("(k p) n -> p k n", p=128), "go%d" % (hd % 2))

            def part1(i, c0, r, par):
                samp = (i == NPT)
                TriU = csq[:, 4 if samp else 1, :]
                TriSL = csq[:, 5 if samp else 2, :]
                qd, sc, kend, vbf, rs, eb = qds[par], scs[par], kends[par], vbfs[par], rss[par], ebs[par]
                hTt = lambda k: self.hT[:, k, c0:c0 + r]
                st8 = {}

                def pa():
                    pq = self.bank()
                    self.mm(pq[:, 0:r], [(W["q"][:, k, :], hTt(k)) for k in range(8)])
                    self.mm(pq[:, 128:128 + r], [(W["k"][:, k, :], hTt(k)) for k in range(8)])
                    pv = self.bank()
                    self.mm(pv[0:r, 0:256], [(hTt(k), W["v"][:, k, :]) for k in range(8)])
                    self.mm(pv[0:r, 256:384], [(hTt(k), W["k"][:, k, :]) for k in range(8)])
                    self.act(vbf[0:r, :], pv[0:r, 0:256], AF.Copy)
                    st8["pq"], st8["pv"] = pq, pv

                def pb():
                    pz = self.bank()
                    self.mm(pz[0:r, 0:128], [(t1[0:16, c0:c0 + r], Wa2[0:16, hd * 128:hd * 128 + 128])])
                    self.tt(zb[0:r, :], pz[0:r, 0:128], baB[0:r, hd * 128:hd * 128 + 128], ALU.add)
                    self.act(zb[0:r, :], zb[0:r, :], AF.Exp, scale=-1.0)
                    self.act(lp[0:r, :], zb[0:r, :], AF.Ln, bias=1.0)
                    pc = self.bank()
                    self.mm(pc[:, 0:r], [(lp[0:r, :], TriU[0:r, 0:r])])
                    self.mm(pc[0:r, 128:256], [(TriSL[0:r, 0:r], lp[0:r, :])])
                    self.act(eb[:, 0:r], pc[:, 0:r], AF.Exp, scale=-1.0 / 16)
                    self.act(einv[:, 0:r], pc[:, 0:r], AF.Exp, scale=1.0 / 16)
                    self.act(ee[0:r, :], pc[0:r, 128:256], AF.Exp, scale=-1.0 / 16)

                def pc_():
                    pq, pv = st8["pq"], st8["pv"]
                    self.stt(qd[:, 0:r], pq[:, 0:r], 128.0 ** -0.5, eb[:, 0:r], ALU.mult, ALU.mult)
                    self.tt(ki[:, 0:r], pq[:, 128:128 + r], einv[:, 0:r], ALU.mult)
                    self.tt(kend[0:r, :], pv[0:r, 256:384], ee[0:r, :], ALU.mult)
                    pr = self.bank()
                    prv = pr[:, 0:256].rearrange("p (a b) -> p a b", a=2)
                    for vh in range(2):
                        self.mm(prv[:, vh, 0:r], [(W["r"][:, k, vh * 128:vh * 128 + 128], hTt(k)) for k in range(8)], last_inc=(vh == 1))
                    self.act(rs[:, :, 0:r], prv[:, :, 0:r], AF.Silu)
                    psc = self.bank()
                    self.mm(psc[0:r, 0:r], [(ki[:, 0:r], qd[:, 0:r])])
                    self.tt(sc[0:r, 0:r], psc[0:r, 0:r], TriU[0:r, 0:r], ALU.mult)

                self.S.rec = None
                self.use_banks([0, 1], [0])
                LA = self.record(pa)
                self.use_banks([2], [0])
                LB = self.record(pb)
                self.use_banks([0, 1], [0])
                LC = self.record(pc_)
                self.S.rec = self.merge(LA, LB) + LC

            def part2(i, c0, r, par):
                samp = (i == NPT)
                qd, sc, kend, vbf, rs, eb = qds[par], scs[par], kends[par], vbfs[par], rss[par], ebs[par]
                if not samp:
                    po = self.bank()
                    pov = po[:, 0:256].rearrange("p (a b) -> p a b", a=2)
                    for vh in range(2):
                        pairs = [(vbf[0:r, vh * 128:vh * 128 + 128], sc[0:r, 0:r])]
                        if i > 0:
                            pairs.append((Sbf[:, vh * 128:vh * 128 + 128], qd[:, 0:r]))
                        self.mm(pov[:, vh, 0:r], pairs, last_inc=(vh == 1))
                    self.act(oT[:, :, 0:r], pov[:, :, 0:r], AF.Copy)
                else:
                    pos = [self.bank(), self.bank()]
                    for vh in range(2):
                        l0, r0 = vbf[0:r, vh * 128:vh * 128 + 128], sc[0:r, 0:r]
                        self.S.op("pe", (lambda e, o_=pos[vh][:, 0:r], l0=l0, r0=r0: e.matmul(o_, l0, r0, start=True, stop=False)),
                                  reads=[l0, r0], writes=[pos[vh][:, 0:r]], inc=False)
                    for b in range(16):
                        sb = S0bf[b % 4]
                        self.act(sb[:], S0[:, b, :], AF.Copy)
                        for vh in range(2):
                            l1, r1 = sb[:, vh * 128:vh * 128 + 128], qd[:, 4 * b:4 * b + 4]
                            last = (b == 15)
                            self.S.op("pe", (lambda e, o_=pos[vh][:, 4 * b:4 * b + 4], l1=l1, r1=r1, last=last: e.matmul(o_, l1, r1, start=False, stop=last)),
                                      reads=[l1, r1], writes=[pos[vh][:, 4 * b:4 * b + 4]], inc=(vh == 1))
                    for vh in range(2):
                        self.act(oT[:, vh, 0:r], pos[vh][:, 0:r], AF.Copy)
                self.tt(sq[:, :, 0:r], oT[:, :, 0:r], oT[:, :, 0:r], ALU.mult)
                pss = self.bank()
                self.mm(pss[:, 0:r], [(ones, sq[:, 0, 0:r]), (ones, sq[:, 1, 0:r])])
                self.act(rstdB[:, 0:r], pss[:, 0:r], AF.Ln, scale=1.0 / 256, bias=EPS)
                self.act(rstdB[:, 0:r], rstdB[:, 0:r], AF.Exp, scale=-0.5)
                self.tt(oT[:, :, 0:r], oT[:, :, 0:r], rstdB[:, 0:r].unsqueeze(1).to_broadcast([128, 2, r]), ALU.mult)
                self.tt(oT[:, :, 0:r], oT[:, :, 0:r], rs[:, :, 0:r], ALU.mult)
                self.tt(gT[:, :, 0:r], oT[:, :, 0:r], gnB[:, 2 * hd:2 * hd + 2, 0:r], ALU.mult)
                for nb in range(2):
                    pout = self.bank()
                    self.mm(pout[0:r, :], [(gT[:, vh, 0:r], W["o"][:, vh, nb * 512:nb * 512 + 512]) for vh in range(2)])
                    xs = self.X[0:r, i, nb * 512:nb * 512 + 512]
                    self.tt(xs, xs, pout[0:r, :], ALU.add)
                if not samp:
                    psu = self.bank()
                    self.mm(psu[:, 0:256], [(kend[0:r, :], vbf[0:r, :])])
                    if i == 0:
                        self.cp(Sst[:], psu[:, 0:256])
                    else:
                        self.stt(Sst[:], Sst[:], eb[:, r - 1:r], psu[:, 0:256], ALU.mult, ALU.add)
                    if i == NPT - 1:
                        self.store(self.o["gla_p"][hd], Sst[:], "glap")
                    else:
                        self.act(Sbf[:], Sst[:], AF.Copy)
                else:
                    self.tt(Vexp[0:64], vbf[0:64, :].unsqueeze(1).to_broadcast([64, 16, 256]),
                            selb[0:64, :].unsqueeze(2).to_broadcast([64, 16, 256]), ALU.mult)
                    for pb in range(8):
                        psu = self.bank()
                        self.mm(psu[:, 0:512], [(kend[0:64, :], Vexp[0:64, 2 * pb:2 * pb + 2, :].rearrange("p a b -> p (a b)"))])
                        for b in (2 * pb, 2 * pb + 1):
                            sn = Snew[b % 4]
                            self.stt(sn[:], S0[:, b, :], eb[:, 4 * b + 3:4 * b + 4], psu[:, (b % 2) * 256:(b % 2) * 256 + 256], ALU.mult, ALU.add)
                            self.store(self.o["gla_s"][b, hd], sn[:], "glas%d" % (b % 2))

            def compute():
                for b in range(16):
                    self.load(S0[:, b, :], self.d["st_gla"][b, hd], "gS%d" % (b % 2))
                prev = None
                for idx, (i, c0, r) in enumerate(self.tiles):
                    par = idx % 2
                    self.use_banks([0, 1, 2], [0])
                    L1 = self.record(lambda: part1(i, c0, r, par))
                    if prev is None:
                        self.S.play(L1)
                    else:
                        self.use_banks([3, 4, 5], [1])
                        L2 = self.record(lambda: part2(*prev))
                        self.S.play(self.merge(L1, L2))
                    prev = (i, c0, r, par)
                self.use_banks([3, 4, 5], [1])
                L2 = self.record(lambda: part2(*prev))
                self.S.play(L2)
                self.use_banks(None, None)

            self.stage(loads, compute, la=1)

        for hd in range(4):
            head_stage(hd)

    def mix_ssd(self, l):
        ar = self.ar
        NPT = self.NPT
        S = self.S
        AX = mybir.AxisListType.X
        self.new_phase()
        ar.reset()
        nl, ncmp = self.make_hT(self.d["norm_mix"][l:l + 1, :])
        Wz = ar.alloc([128, 8, 512], BF16)
        Wx = ar.alloc([128, 8, 512], BF16)
        WBC = ar.alloc([128, 8, 256], BF16)
        Wdt = ar.alloc([128, 8, 8], BF16)
        Wo = ar.alloc([128, 4, D], BF16)
        gnB2 = ar.alloc([128, 512], F32)
        cw = ar.alloc([128, 6, 4], F32)
        cb = ar.alloc([128, 6], F32)
        dtbB = ar.alloc([128, 32], F32)
        aB = ar.alloc([128, 32], F32)
        DB = ar.alloc([128, 32], F32)
        selb = ar.alloc([128, 16], F32)
        expand = ar.alloc([128, 4, 128], F32)
        xpre = ar.alloc([128, 6, 131], F32)
        acc = ar.alloc([128, 6, 128], F32)
        seg = ar.alloc([128, 8, 128], F32)
        Mh = ar.alloc([128, 8, 128], BF16)
        extT = ar.alloc([128, 6, 16, 7], F32)
        cst = xpre[:].rearrange("p a b -> p (a b)")[:, 0:768]
        cvs = acc[:].rearrange("p a b -> p (a b)")
        xcbs = [ar.alloc([128, 6, 128], BF16) for _ in range(2)]
        xtmBs = [ar.alloc([128, 640], BF16) for _ in range(2)]
        zss = [ar.alloc([128, 512], F32) for _ in range(2)]
        sms = [ar.alloc([128, 8, 8], F32) for _ in range(2)]
        ybs = [ar.alloc([128, 512], F32) for _ in range(2)]
        tmp = ar.alloc([128, 512], F32)
        yn = ar.alloc([128, 512], BF16)
        ynT = ar.alloc([128, 4, 128], BF16)
        xw = ar.alloc([128, 512], BF16)
        ST = ar.alloc([128, 512], F32)
        STbf = ar.alloc([128, 512], BF16)
        S0nat = [ar.alloc([128, 4, 128], F32) for _ in range(4)]
        ST0bf = [ar.alloc([128, 512], BF16) for _ in range(2)]
        Snew = [ar.alloc([128, 4, 128], F32) for _ in range(2)]
        CTm = ar.alloc([128, 16, 64], BF16)
        Btmb = ar.alloc([128, 16, 128], BF16)
        edT = ar.alloc([128, 16], F32)
        edn = ar.alloc([128, 4, 16], F32)
        csq = self.csq
        ones = csq[:, 3, :]
        identF = csq[:, 0, :]
        win = self.d["ssm_w_in"]
        import os as _os
        CONV_ENG = _os.environ.get("SSD_CONV_ENG", "dve")
        POOL_ENG = _os.environ.get("SSD_POOL_ENG", "dve")

        def loads0():
            nl()
            self.load(dtbB[:], self.d["ssm_dt_bias"][0, :].partition_broadcast(128), "mwA")
            self.load(aB[:], self.d["ssm_a_log"][0, :].partition_broadcast(128), "mwB")
            self.load(DB[:], self.d["ssm_d"][0, :].partition_broadcast(128), "mwC")
            self.load(selb[0:64, :], self.d["c_selb"], "mwD")
            self.load(expand[0:8], self.d["c_expand"], "mwE")

        def compute0():
            ncmp()
            self.act(aB[:], aB[:], AF.Exp)
            self.ts(aB[:], aB[:], -1.0, None, ALU.mult)

        self.stage(loads0, compute0, la=0)

        def group_stage(g):
            xcols = [(512 * g + 128 * c) for c in range(4)] + [2048 + 128 * g, 2560 + 128 * g]
            segs = [(0, 512, 512 * g), (512, 128, 2048 + 128 * g), (640, 128, 2560 + 128 * g)]

            def loads():
                self.load_cast(Wx[:], win[:, 2048 + 512 * g:2048 + 512 * g + 512].rearrange("(k p) n -> p k n", p=128), "sx")
                self.load_cast(WBC[:, :, 0:128], win[:, 4096 + 128 * g:4096 + 128 * g + 128].rearrange("(k p) n -> p k n", p=128), "sb")
                self.load_cast(WBC[:, :, 128:256], win[:, 4608 + 128 * g:4608 + 128 * g + 128].rearrange("(k p) n -> p k n", p=128), "sc")
                self.load_cast(Wdt[:], win[:, 5120 + 8 * g:5120 + 8 * g + 8].rearrange("(k p) n -> p k n", p=128), "sd")
                self.load_cast(Wz[:], win[:, 512 * g:512 * g + 512].rearrange("(k p) n -> p k n", p=128), "sz")
                self.load_cast(Wo[:], self.d["ssm_w_out"][512 * g:512 * g + 512, :].rearrange("(k p) n -> p k n", p=128), "so")
                self.load(gnB2[:], self.d["ssm_norm"][0, 512 * g:512 * g + 512].partition_broadcast(128), "sg")
                for c in range(6):
                    S.dma("sp", cw[:, c, :], self.d["ssm_conv_w"][:, xcols[c]:xcols[c] + 128].rearrange("j p -> p j"), "scw", allow_slow_non_contiguous=True)
                    S.dma("sp", cb[:, c:c + 1], self.d["ssm_conv_b"][0, xcols[c]:xcols[c] + 128].rearrange("(p o) -> p o", o=1), "scb")

            def part1a(i, c0, r, par):
                samp = (i == NPT)
                xcb, xtmB, zs = xcbs[par], xtmBs[par], zss[par]
                hTt = lambda k: self.hT[:, k, c0:c0 + r]
                if samp:
                    for si, (a0, w_, d0) in enumerate(segs):
                        self.load(cst[0:48, a0:a0 + w_], self.d["st_conv"][:, d0:d0 + w_], "scs%d" % si)
                    pcs = self.bank()
                    for c in range(6):
                        self.tr(pcs[:, 48 * c:48 * c + 48], cst[0:48, 128 * c:128 * c + 128], identF[0:48, 0:48], inc=(c == 5))
                    self.act(extT[:, :, :, 0:3], pcs[:, 0:288].rearrange("p (c b j) -> p c b j", c=6, b=16), AF.Copy)
                px1 = self.bank()
                px1v = px1[:].rearrange("p (a b) -> p a b", a=4)
                for c in range(4):
                    self.mm(px1v[:, c, 0:r], [(Wx[:, k, 128 * c:128 * c + 128], hTt(k)) for k in range(8)], last_inc=(c == 3))
                if not samp:
                    self.act(xpre[:, 0:4, 3:3 + r], px1v[:, :, 0:r], AF.Copy)
                else:
                    self.act(extT[:, 0:4, :, 3:7], px1v[:, :, 0:64].rearrange("p c (b t) -> p c b t", t=4), AF.Copy)
                px2 = self.bank()
                px2v = px2[:, 0:256].rearrange("p (a b) -> p a b", a=2)
                for c in range(2):
                    self.mm(px2v[:, c, 0:r], [(WBC[:, k, 128 * c:128 * c + 128], hTt(k)) for k in range(8)], last_inc=(c == 1))
                if not samp:
                    self.act(xpre[:, 4:6, 3:3 + r], px2v[:, :, 0:r], AF.Copy)
                    srcv = lambda c, j: xpre[:, c, j:j + r]
                    accv = lambda c: acc[:, c, 0:r]
                else:
                    self.act(extT[:, 4:6, :, 3:7], px2v[:, :, 0:64].rearrange("p c (b t) -> p c b t", t=4), AF.Copy)
                    srcv = lambda c, j: extT[:, c, :, j:j + 4]
                    accv = lambda c: acc[:, c, 0:64].rearrange("p (b t) -> p b t", t=4)
                pz = self.bank()
                self.mm(pz[0:r, :], [(hTt(k), Wz[:, k, :]) for k in range(8)])
                for c in range(6):
                    self.ts(accv(c), srcv(c, 0), cw[:, c, 0:1], cb[:, c:c + 1], ALU.mult, ALU.add, eng=POOL_ENG)
                for j in range(1, 4):
                    for c in range(6):
                        self.stt(accv(c), srcv(c, j), cw[:, c, j:j + 1], accv(c), ALU.mult, ALU.add)
                if not samp and i < NPT - 1:
                    self.cp(xpre[:, :, 0:3], xpre[:, :, r:r + 3])
                outer = S.rec
                S.rec = []
                self.act(xcb[:, :, 0:r], acc[:, :, 0:r], AF.Silu)
                self.act(zs[0:r, :], pz[0:r, :], AF.Silu)
                grp = S.rec
                S.rec = outer
                S.rec.append(("group", grp))
                tbx = self.tbank()
                for c in range(5):
                    self.tr(tbx[0:r, 128 * c:128 * c + 128], xcb[:, c, 0:r], self.identb[:, :], inc=(c == 4))
                self.act(xtmB[0:r, :], tbx[0:r, 0:640], AF.Copy)

            def part1b(i, c0, r, par):
                samp = (i == NPT)
                TriU = csq[:, 4 if samp else 1, :]
                Neg = csq[:, 7 if samp else 6, :]
                sm = sms[par]
                dtp, dt_, lnd, dA, cl, ecum, wendc, edecB = (sm[:, j, :] for j in range(8))
                hTt = lambda k: self.hT[:, k, c0:c0 + r]
                pd = self.bank()
                self.mm(pd[0:r, 0:8], [(hTt(k), Wdt[:, k, :]) for k in range(8)])
                self.tt(dtp[0:r], pd[0:r, 0:8], dtbB[0:r, 8 * g:8 * g + 8], ALU.add)
                self.act(dtp[0:r], dtp[0:r], AF.Exp)
                self.act(dt_[0:r], dtp[0:r], AF.Ln, bias=1.0)
                self.act(lnd[0:r], dt_[0:r], AF.Ln)
                self.tt(dA[0:r], dt_[0:r], aB[0:r, 8 * g:8 * g + 8], ALU.mult)
                pcm = self.bank()
                self.mm(pcm[0:r, 0:8], [(TriU[0:r, 0:r], dA[0:r])])
                self.tt(cl[0:r], pcm[0:r, 0:8], lnd[0:r], ALU.subtract)
                self.act(ecum[0:r], pcm[0:r, 0:8], AF.Exp)
                self.tt(seg[0:r, :, 0:r], dA[0:r].unsqueeze(2).to_broadcast([r, 8, r]), TriU[0:r, 0:r].unsqueeze(1).to_broadcast([r, 8, r]), ALU.mult, eng=POOL_ENG)
                for half in range(2):
                    pcb = self.bank()
                    pcbv = pcb[:].rearrange("p (a b) -> p a b", a=4)
                    if r == 128:
                        self.mm(pcbv[:, :, 0:r], [(ones[0:r, :], seg[0:r, 4 * half:4 * half + 4, 0:r])])
                    else:
                        for hh in range(4):
                            self.mm(pcbv[:, hh, 0:r], [(ones[0:r, :], seg[0:r, 4 * half + hh, 0:r])], last_inc=(hh == 3))
                    if not samp:
                        self.act(edecB[:, 4 * half:4 * half + 4], pcbv[:, :, r - 1], AF.Exp)
                    self.tt(seg[0:r, 4 * half:4 * half + 4, 0:r], pcbv[0:r, :, 0:r],
                            cl[0:r, 4 * half:4 * half + 4].unsqueeze(2).to_broadcast([r, 4, r]), ALU.subtract)
                self.tt(seg[0:r, :, 0:r], seg[0:r, :, 0:r], Neg[0:r, 0:r].unsqueeze(1).to_broadcast([r, 8, r]), ALU.add)
                self.act(seg[0:r, :, 0:r], seg[0:r, :, 0:r], AF.Exp)
                if not samp:
                    self.cp(wendc[0:r], seg[0:r, :, r - 1])
                else:
                    tmpE = Mh[0:64].rearrange("p a b -> p (a b)").bitcast(F32).rearrange("p (h t) -> p h t", h=8)
                    self.tt(tmpE, seg[0:64, :, 0:64], csq[0:64, 9, 0:64].unsqueeze(1).to_broadcast([64, 8, 64]), ALU.mult)
                    S.op("dve", lambda e: e.tensor_reduce(out=wendc[0:64, :], in_=tmpE, axis=AX, op=ALU.add),
                         reads=[tmpE], writes=[wendc[0:64, :]])

            def part1c(i, c0, r, par):
                samp = (i == NPT)
                xcb, xtmB, yb = xcbs[par], xtmBs[par], ybs[par]
                xtm, BCT = xtmB[:, 0:512], xcb[:, 4:6, :]
                hTt = lambda k: self.hT[:, k, c0:c0 + r]
                pg = self.bank()
                self.mm(pg[0:r, 0:r], [(BCT[:, 0, 0:r], BCT[:, 1, 0:r])])
                self.tt(Mh[0:r, :, 0:r], seg[0:r, :, 0:r], pg[0:r, 0:r].unsqueeze(1).to_broadcast([r, 8, r]), ALU.mult)
                py = self.bank()
                for h in range(8):
                    self.mm(py[0:r, 64 * h:64 * h + 64], [(Mh[0:r, h, 0:r], xtm[0:r, 64 * h:64 * h + 64])], last_inc=(h == 7))
                self.act(yb[0:r, :], py[0:r, :], AF.Copy)
                if i >= NPT - 1:
                    pc1 = self.bank()
                    self.mm(pc1[0:r, :], [(hTt(k), Wx[:, k, :]) for k in range(8)])
                    pc2 = self.bank()
                    self.mm(pc2[0:r, 0:256], [(hTt(k), WBC[:, k, :]) for k in range(8)])
                    self.act(cvs[0:r, 0:512], pc1[0:r, :], AF.Copy)
                    self.act(cvs[0:r, 512:768], pc2[0:r, 0:256], AF.Copy)
                    if not samp:
                        for (a0, w_, d0) in segs:
                            self.store(self.o["conv_p"][:, d0:d0 + w_], cvs[125:128, a0:a0 + w_], "cvp")
                    else:
                        for b in range(16):
                            for (a0, w_, d0) in segs:
                                self.store(self.o["conv_s"][b, :, d0:d0 + w_], cvs[4 * b + 1:4 * b + 4, a0:a0 + w_], "cvs")

            def part2(i, c0, r, par):
                samp = (i == NPT)
                TriU = csq[:, 4 if samp else 1, :]
                xcb, xtmB, zs, sm, yb = xcbs[par], xtmBs[par], zss[par], sms[par], ybs[par]
                xtf, Btm, BCT = xtmB[:, 0:512], xtmB[:, 512:640], xcb[:, 4:6, :]
                dtp, dt_, lnd, dA, cl, ecum, wendc, edecB = (sm[:, j, :] for j in range(8))
                v8 = lambda ap: ap.rearrange("p (h q) -> p h q", h=8)
                self.tt(v8(xw[0:r, :]), v8(xtf[0:r, :]), wendc[0:r].unsqueeze(2).to_broadcast([r, 8, 64]), ALU.mult, eng=POOL_ENG)
                if not samp:
                    if i > 0:
                        pi = self.bank()
                        self.mm(pi[0:r, :], [(BCT[:, 1, 0:r], STbf[:, :])])
                        self.tt(v8(tmp[0:r, :]), v8(pi[0:r, :]), ecum[0:r].unsqueeze(2).to_broadcast([r, 8, 64]), ALU.mult)
                        self.tt(yb[0:r, :], yb[0:r, :], tmp[0:r, :], ALU.add)
                    psu = self.bank()
                    self.mm(psu[:, :], [(Btm[0:r, :], xw[0:r, :])])
                    if i == 0:
                        self.cp(ST[:], psu[:, :])
                    else:
                        self.tt(v8(ST[:]), v8(ST[:]), edecB[:].unsqueeze(2).to_broadcast([128, 8, 64]), ALU.mult)
                        self.tt(ST[:], ST[:], psu[:, :], ALU.add)
                    if i < NPT - 1:
                        self.act(STbf[:], ST[:], AF.Copy)
                    else:
                        pso = self.bank()
                        for c in range(4):
                            self.tr(pso[:, 128 * c:128 * c + 128], ST[:, 128 * c:128 * c + 128], identF, inc=(c == 3))
                        so = Snew[0]
                        self.act(so[:], pso[:].rearrange("p (c n) -> p c n", c=4), AF.Copy)
                        self.store(self.o["ssm_p"][512 * g:512 * g + 512, :].rearrange("(c p) n -> p c n", p=128), so[:], "ssmo0")
                else:
                    self.memset(CTm[:], 0.0)
                    for b in range(16):
                        self.cp(CTm[:, b, 4 * b:4 * b + 4], BCT[:, 1, 4 * b:4 * b + 4])
                    self.tt(Btmb[0:64], Btm[0:64, :].unsqueeze(1).to_broadcast([64, 16, 128]), selb[0:64, :].unsqueeze(2).to_broadcast([64, 16, 128]), ALU.mult)
                    pct = self.bank()
                    self.mm(pct[0:8, 0:64], [(dA[0:64], TriU[0:64, 0:64])])
                    self.act(edT[0:8, :], pct[0:8, 0:64].rearrange("h (b t) -> h b t", t=4)[:, :, 3], AF.Exp)
                    pen = self.bank()
                    for c in range(4):
                        self.mm(pen[:, 16 * c:16 * c + 16], [(expand[0:8, c, :], edT[0:8, :])], last_inc=(c == 3))
                    self.act(edn[:], pen[:, 0:64].rearrange("p (c b) -> p c b", c=4), AF.Copy)
                    pi = self.bank()
                    self.reserved.add(pi.name)
                    def ld_state(b_):
                        self.load(S0nat[b_ % 4][:], self.d["st_ssm"][b_, 512 * g:512 * g + 512, :].rearrange("(c p) n -> p c n", p=128), "ssn%d" % (b_ % 4))
                    for b_ in range(3):
                        ld_state(b_)
                    for b in range(16):
                        sn = S0nat[b % 4]
                        if b + 3 < 16:
                            ld_state(b + 3)
                        pst = self.bank()
                        for c in range(4):
                            self.tr(pst[:, 128 * c:128 * c + 128], sn[:, c, :], identF, inc=(c == 3))
                        sb = ST0bf[b % 2]
                        self.act(sb[:], pst[:, :], AF.Copy)
                        l1, r1 = CTm[:, b, :], sb[:, :]
                        S.op("pe", (lambda e, l1=l1, r1=r1, st_=(b == 0), sp_=(b == 15): e.matmul(pi[0:64, :], l1, r1, start=st_, stop=sp_)),
                             reads=[l1, r1], writes=[pi[0:64, :]], inc=True)
                        psn = self.bank()
                        psnv = psn[:].rearrange("p (c n) -> p c n", c=4)
                        for c in range(4):
                            self.mm(psnv[:, c, :], [(xw[0:64, 128 * c:128 * c + 128], Btmb[0:64, b, :])], last_inc=(c == 3))
                        so = Snew[b % 2]
                        for c in range(4):
                            self.stt(so[:, c, :], sn[:, c, :], edn[:, c, b:b + 1], psnv[:, c, :], ALU.mult, ALU.add)
                        self.S.dma("pool", self.o["ssm_s"][b, 512 * g:512 * g + 512, :].rearrange("(c p) n -> p c n", p=128), so[:], "ssms%d" % (b % 2))
                    self.reserved.discard(pi.name)
                    self.tt(v8(tmp[0:64, :]), v8(pi[0:64, :]), ecum[0:64].unsqueeze(2).to_broadcast([64, 8, 64]), ALU.mult)
                    self.tt(yb[0:64, :], yb[0:64, :], tmp[0:64, :], ALU.add)
                self.tt(v8(tmp[0:r, :]), v8(xtf[0:r, :]), DB[0:r, 8 * g:8 * g + 8].unsqueeze(2).to_broadcast([r, 8, 64]), ALU.mult, eng=POOL_ENG)
                self.tt(yb[0:r, :], yb[0:r, :], tmp[0:r, :], ALU.add, eng=POOL_ENG)
                self.tt(yb[0:r, :], yb[0:r, :], zs[0:r, :], ALU.mult, eng=POOL_ENG)
                ssq = self.stat[0:r, 32:33]
                rsd = self.stat[0:r, 33:34]
                self.act(self.junk[0:r, 0:512], yb[0:r, :], AF.Square, accum_out=ssq)
                self.act(rsd, ssq, AF.Ln, scale=1.0 / 512, bias=EPS)
                self.act(rsd, rsd, AF.Exp, scale=-0.5)
                self.stt(yn[0:r, :], yb[0:r, :], rsd, gnB2[0:r, :], ALU.mult, ALU.mult)
                tb = self.tbank()
                tbv = tb[:, 0:512].rearrange("p (a b) -> p a b", a=4)
                for c in range(4):
                    self.tr(tbv[:, c, 0:r], yn[0:r, 128 * c:128 * c + 128], self.identb[0:r, 0:r], inc=(c == 3))
                self.act(ynT[:, :, 0:r], tbv[:, :, 0:r], AF.Copy)
                for nb in range(2):
                    pout = self.bank()
                    self.mm(pout[0:r, :], [(ynT[:, c, 0:r], Wo[:, c, nb * 512:nb * 512 + 512]) for c in range(4)])
                    xs = self.X[0:r, i, nb * 512:nb * 512 + 512]
                    self.tt(xs, xs, pout[0:r, :], ALU.add)

            def compute():
                self.memset(xpre[:, :, 0:3], 0.0)
                prev = None
                for idx, (i, c0, r) in enumerate(self.tiles):
                    par = idx % 2
                    self.use_banks([0, 1], [0])
                    LA = self.record(lambda: part1a(i, c0, r, par))
                    self.use_banks([2], [0])
                    LB = self.record(lambda: part1b(i, c0, r, par))
                    self.use_banks([0, 1], [0])
                    LC = self.record(lambda: part1c(i, c0, r, par))
                    L1 = self.merge(LA, LB) + LC
                    if prev is None:
                        S.play(L1)
                    else:
                        self.use_banks([3, 4, 5], [1])
                        L2 = self.record(lambda: part2(*prev))
                        S.play(self.merge(L1, L2))
                    prev = (i, c0, r, par)
                self.use_banks([3, 4, 5], [1])
                L2 = self.record(lambda: part2(*prev))
                S.play(L2)
                self.use_banks(None, None)

            self.stage(loads, compute, la=0)

        for g in range(4):
            group_stage(g)


def build_nc(NPT=16, mixers=(0, 1, 2, 3), same_engine_sync=True):
    k = K(NPT, mixers, same_engine_sync)
    return k.build()


WEIGHT_NAMES = ["norm_ffn1", "ffn1_gate", "ffn1_up", "ffn1_down", "norm_mix", "norm_ffn2", "ffn2_gate", "ffn2_up",
                "ffn2_down", "norm_ple", "ple_gate", "ple_proj", "gm_w_in", "gm_w_s", "gm_b_s", "gm_w_out",
                "pool_w", "gla_w_in", "gla_w_a1", "gla_w_a2", "gla_w_out", "ssm_w_in", "ssm_conv_w", "ssm_w_out"]
ROW_NAMES = ["norm_final", "gm_ln", "pool_scale", "gla_b_a", "gla_norm", "ssm_conv_b", "ssm_dt_bias", "ssm_a_log",
             "ssm_d", "ssm_norm"]


def make_in_maps(inputs, NPT, ncores):
    f = lambda a: np.ascontiguousarray(np.asarray(a, dtype=np.float32))
    shared = {k: f(inputs[k]) for k in WEIGHT_NAMES}
    for k in ROW_NAMES:
        shared[k] = f(inputs[k]).reshape(1, -1)
    shared.update(make_consts())
    maps = []
    for c in range(ncores):
        m = dict(shared)
        m["xp"] = f(inputs["x_prompt"][c])
        m["xs"] = f(inputs["x_sample"][16 * c:16 * c + 16]).reshape(64, D)
        m["st_pool"] = f(inputs["state_pool_l1"][16 * c:16 * c + 16]).reshape(240, D)
        m["st_gla"] = f(inputs["state_gla_l2"][16 * c:16 * c + 16])
        m["st_ssm"] = f(inputs["state_ssm_l3"][16 * c:16 * c + 16]).reshape(16, 2048, 128)
        m["st_conv"] = f(inputs["state_conv_l3"][16 * c:16 * c + 16]).reshape(48, 3072)
        m["pp"] = f(inputs["p_prompt"][:, c])
        m["ps"] = f(inputs["p_sample"][:, 16 * c:16 * c + 16]).reshape(4, 64, 256)
        maps.append(m)
    return maps


def gather(results, NPT, ncores):
    TP = NPT * 128
    cat = lambda k, shp: np.concatenate([np.asarray(r[k]).reshape(shp) for r in results], axis=0)
    return (
        cat("yp", (1, TP, D)), cat("ys", (16, 4, D)), cat("chunk_v", (16, 4, D)),
        cat("pool_p", (1, 15, D)), cat("pool_s", (16, 15, D)),
        cat("gla_p", (1, 4, 128, 256)), cat("gla_s", (16, 4, 128, 256)),
        cat("ssm_p", (1, 32, 64, 128)), cat("ssm_s", (16, 32, 64, 128)),
        cat("conv_p", (1, 3, 3072)), cat("conv_s", (16, 3, 3072)),
    )


_NC_CACHE = {}


def kernel(**inputs):
    NPT = 16
    ncores = 8
    if "nc" not in _NC_CACHE:
        _NC_CACHE["nc"] = build_nc(NPT)
    nc = _NC_CACHE["nc"]
    maps = make_in_maps(inputs, NPT, ncores)
    res = run_bass_kernel_spmd(nc, maps, core_ids=list(range(ncores)))
    outs = gather(res.results, NPT, ncores)
    return tuple(np.ascontiguousarray(o, dtype=np.float32) for o in outs)
```
